# Optimizing a Trainium2 kernel written in Bass

```python
import math
import jax
import jax.numpy as jnp
from jax import lax
import numpy as np

D_MODEL = 1024
BATCH = 16
SEQ = 4096
DEPTH = 2

EPS = 1e-6
N_BRANCH = 3
GDN_HEADS = 4
GDN_DK = 128
GDN_DV = 128
GDN_CONV = 4
GDN_CHUNK = 64
GDN_KEY_W = GDN_HEADS * GDN_DK
GDN_VAL_W = GDN_HEADS * GDN_DV
HGRN_HEADS = 4
HGRN_DK = 128
HGRN_DV = 128
HGRN_CHUNK = 16
HGRN_KEY_W = HGRN_HEADS * HGRN_DK
HGRN_VAL_W = HGRN_HEADS * HGRN_DV
SSD_HEADS = 8
SSD_HEAD_DIM = 64
SSD_GROUPS = 2
SSD_HEADS_PER_GROUP = SSD_HEADS // SSD_GROUPS
SSD_STATE = 128
SSD_CONV = 4
SSD_CHUNK = 64
SSD_INNER = SSD_HEADS * SSD_HEAD_DIM
SSD_XBC_W = SSD_INNER + 2 * SSD_GROUPS * SSD_STATE
FFN_HIDDEN = 2816
FFN_CONV = 3

SPLIT_SIZES = (
    2 * GDN_KEY_W + GDN_VAL_W,
    GDN_HEADS,
    GDN_HEADS,
    GDN_VAL_W,
    HGRN_KEY_W,
    HGRN_KEY_W,
    HGRN_VAL_W,
    HGRN_VAL_W,
    SSD_INNER,
    SSD_XBC_W,
    SSD_HEADS,
    N_BRANCH * D_MODEL,
)
N_IN = sum(SPLIT_SIZES)

kernel_name = "hybrid_gdn_hgrn2_ssd_adaln_block"


def _f32(t):
    return t.astype(jnp.float32)


def rms_norm(x, w):
    xf = x.astype(jnp.float32)
    y = xf * lax.rsqrt(jnp.mean(xf * xf, axis=-1, keepdims=True) + EPS)
    return (y * w.astype(jnp.float32)).astype(x.dtype)


def l2_normalize(x):
    return x * lax.rsqrt(jnp.sum(x * x, axis=-1, keepdims=True) + EPS)


def modulate(h, shift, scale):
    return h * (1.0 + scale[:, None, :]) + shift[:, None, :]


def causal_dwconv(x, w, b=None):
    k_width = w.shape[0]
    s = x.shape[1]
    xp = jnp.pad(x, ((0, 0), (k_width - 1, 0), (0, 0)))
    y = w[k_width - 1] * x
    for k in range(k_width - 1):
        y = y + w[k] * xp[:, k:k + s]
    if b is not None:
        y = y + b
    return y


def to_chunks(t, c):
    b, s = t.shape[:2]
    t = t.reshape((b, s // c, c) + t.shape[2:])
    t = jnp.swapaxes(t, 0, 1)
    return jnp.swapaxes(t, 2, 3)


def from_chunks(t):
    t = jnp.swapaxes(jnp.swapaxes(t, 2, 3), 0, 1)
    return t.reshape((t.shape[0], t.shape[1] * t.shape[2]) + t.shape[3:])


def gated_delta_chunked(q, k, v, g, beta):
    b, s, h, dk = q.shape
    dv = v.shape[-1]
    c = GDN_CHUNK
    qc = to_chunks(q * (dk ** -0.5), c)
    kc = to_chunks(k, c)
    vc = to_chunks(v, c)
    bc = to_chunks(beta, c)
    big_g = jnp.cumsum(to_chunks(g, c), axis=-1)
    incl = jnp.tril(jnp.ones((c, c), bool))
    strict = jnp.tril(jnp.ones((c, c), bool), -1)
    diff = big_g[..., :, None] - big_g[..., None, :]
    decay = jnp.where(incl, jnp.exp(jnp.where(incl, diff, 0.0)), 0.0)
    kb = kc * bc[..., None]
    m = jnp.where(strict, jnp.einsum('nbhlk,nbhsk->nbhls', kb, kc) * decay, 0.0)
    a = m + jnp.eye(c, dtype=m.dtype)
    rhs = jnp.concatenate([vc * bc[..., None], kb * jnp.exp(big_g)[..., None]], axis=-1)
    sol = lax.linalg.triangular_solve(a, rhs, left_side=True, lower=True, unit_diagonal=True)
    u, w = sol[..., :dv], sol[..., dv:]
    attn = jnp.einsum('nbhlk,nbhsk->nbhls', qc, kc) * decay
    qg = qc * jnp.exp(big_g)[..., None]
    k_end = kc * jnp.exp(big_g[..., -1:] - big_g)[..., None]
    g_end = jnp.exp(big_g[..., -1])

    def step(state, inp):
        qg_i, k_end_i, u_i, w_i, attn_i, g_end_i = inp
        v_new = u_i - jnp.einsum('bhlk,bhkv->bhlv', w_i, state)
        o = (jnp.einsum('bhlk,bhkv->bhlv', qg_i, state)
             + jnp.einsum('bhls,bhsv->bhlv', attn_i, v_new))
        state = state * g_end_i[..., None, None] + jnp.einsum('bhsk,bhsv->bhkv', k_end_i, v_new)
        return state, o

    state0 = jnp.zeros((b, h, dk, dv), q.dtype)
    _, o = lax.scan(step, state0, (qg, k_end, u, w, attn, g_end))
    return from_chunks(o)


def hgrn2_chunked(q, k, v, logf):
    b, s, h, dk = q.shape
    dv = v.shape[-1]
    c = HGRN_CHUNK
    qc, kc, vc = to_chunks(q, c), to_chunks(k, c), to_chunks(v, c)
    big_g = jnp.cumsum(to_chunks(logf, c), axis=-2)
    g_ref = big_g[..., c // 2 - 1:c // 2, :]
    incl = jnp.tril(jnp.ones((c, c), bool))
    scores = jnp.einsum('nbhlk,nbhsk->nbhls', qc * jnp.exp(big_g - g_ref), kc * jnp.exp(g_ref - big_g))
    attn = jnp.where(incl, scores, 0.0)
    o_intra = jnp.einsum('nbhls,nbhsv->nbhlv', attn, vc)
    qg = qc * jnp.exp(big_g)
    k_end = kc * jnp.exp(big_g[..., -1:, :] - big_g)
    g_end = jnp.exp(big_g[..., -1, :])

    def step(state, inp):
        qg_i, k_end_i, v_i, g_end_i = inp
        o = jnp.einsum('bhlk,bhkv->bhlv', qg_i, state)
        state = state * g_end_i[..., None] + jnp.einsum('bhsk,bhsv->bhkv', k_end_i, v_i)
        return state, o

    state0 = jnp.zeros((b, h, dk, dv), q.dtype)
    _, o_inter = lax.scan(step, state0, (qg, k_end, vc, g_end))
    return from_chunks(o_intra + o_inter)


def ssd_chunked(xs, da, bm, cm):
    b, s, g, hg, p = xs.shape
    n_state = bm.shape[-1]
    c = SSD_CHUNK
    n = s // c
    xc = jnp.swapaxes(xs.reshape(b, n, c, g, hg, p), 0, 1)
    bc = jnp.swapaxes(bm.reshape(b, n, c, g, n_state), 0, 1)
    cc = jnp.swapaxes(cm.reshape(b, n, c, g, n_state), 0, 1)
    acs = jnp.cumsum(jnp.swapaxes(da.reshape(b, n, c, g, hg), 0, 1), axis=2)
    incl = jnp.tril(jnp.ones((c, c), bool))[:, :, None, None]
    diff = acs[:, :, :, None] - acs[:, :, None, :]
    seg = jnp.where(incl, jnp.exp(jnp.where(incl, diff, 0.0)), 0.0)
    cb = jnp.einsum('nblgd,nbsgd->nblsg', cc, bc)
    y_diag = jnp.einsum('nblsg,nblsgh,nbsghp->nblghp', cb, seg, xc)

    def step(state, inp):
        x_i, b_i, c_i, acs_i = inp
        y_off = jnp.einsum('blgd,bghpd,blgh->blghp', c_i, state, jnp.exp(acs_i))
        last = acs_i[:, -1]
        state = (state * jnp.exp(last)[..., None, None]
                 + jnp.einsum('bsgd,bsgh,bsghp->bghpd', b_i, jnp.exp(last[:, None] - acs_i), x_i))
        return state, y_off

    state0 = jnp.zeros((b, g, hg, p, n_state), xs.dtype)
    _, y_off = lax.scan(step, state0, (xc, bc, cc, acs))
    return jnp.swapaxes(y_diag + y_off, 0, 1).reshape(b, s, g, hg, p)


def gdn_branch(qkv_raw, a_raw, b_raw, z_raw, conv_w, a_log, dt_bias, norm_w):
    bsz, s, _ = qkv_raw.shape
    qkv = jax.nn.silu(causal_dwconv(qkv_raw, conv_w))
    q, k, v = jnp.split(qkv, [GDN_KEY_W, 2 * GDN_KEY_W], axis=-1)
    q = l2_normalize(q.reshape(bsz, s, GDN_HEADS, GDN_DK))
    k = l2_normalize(k.reshape(bsz, s, GDN_HEADS, GDN_DK))
    v = v.reshape(bsz, s, GDN_HEADS, GDN_DV)
    beta = jax.nn.sigmoid(b_raw)
    g = -jnp.exp(a_log) * jax.nn.softplus(a_raw + dt_bias)
    o = gated_delta_chunked(q, k, v, g, beta)
    o = rms_norm(o, norm_w) * jax.nn.silu(z_raw.reshape(bsz, s, GDN_HEADS, GDN_DV))
    return o.reshape(bsz, s, GDN_VAL_W)


def hgrn2_branch(q_raw, f_raw, i_raw, g_raw, lb, norm_w):
    bsz, s, _ = q_raw.shape
    shp = (bsz, s, HGRN_HEADS, HGRN_DK)
    q = jax.nn.silu(q_raw).reshape(shp)
    logf = jnp.log(lb + (1.0 - lb) * jax.nn.sigmoid(f_raw)).reshape(shp)
    k = ((1.0 - lb) * jax.nn.sigmoid(-f_raw)).reshape(shp)
    v = i_raw.reshape(bsz, s, HGRN_HEADS, HGRN_DV)
    o = hgrn2_chunked(q, k, v, logf)
    o = rms_norm(o, norm_w) * jax.nn.silu(g_raw.reshape(bsz, s, HGRN_HEADS, HGRN_DV))
    return o.reshape(bsz, s, HGRN_VAL_W)


def ssd_branch(z_raw, xbc_raw, dt_raw, conv_w, conv_b, a_log, dt_bias, d_skip, norm_w):
    bsz, s, _ = xbc_raw.shape
    xbc = jax.nn.silu(causal_dwconv(xbc_raw, conv_w, conv_b))
    xs, bm, cm = jnp.split(xbc, [SSD_INNER, SSD_INNER + SSD_GROUPS * SSD_STATE], axis=-1)
    xs = xs.reshape(bsz, s, SSD_GROUPS, SSD_HEADS_PER_GROUP, SSD_HEAD_DIM)
    bm = bm.reshape(bsz, s, SSD_GROUPS, SSD_STATE)
    cm = cm.reshape(bsz, s, SSD_GROUPS, SSD_STATE)
    dt = jax.nn.softplus(dt_raw + dt_bias).reshape(bsz, s, SSD_GROUPS, SSD_HEADS_PER_GROUP)
    a = -jnp.exp(a_log).reshape(SSD_GROUPS, SSD_HEADS_PER_GROUP)
    y = ssd_chunked(xs * dt[..., None], dt * a, bm, cm)
    y = y + d_skip.reshape(SSD_GROUPS, SSD_HEADS_PER_GROUP)[..., None] * xs
    group_w = SSD_HEADS_PER_GROUP * SSD_HEAD_DIM
    y = y.reshape(bsz, s, SSD_GROUPS, group_w)
    z = z_raw.reshape(bsz, s, SSD_GROUPS, group_w)
    y = rms_norm(y * jax.nn.silu(z), norm_w.reshape(SSD_GROUPS, group_w))
    return y.reshape(bsz, s, SSD_INNER)


def token_mixing(h, lb, w_in, gdn_conv_w, gdn_a_log, gdn_dt_bias, gdn_norm_w, hgrn_norm_w,
                 ssd_conv_w, ssd_conv_b, ssd_a_log, ssd_dt_bias, ssd_d, ssd_norm_w,
                 w_br_a, w_br_b, w_br_c, w_out):
    bsz, s, _ = h.shape
    dtype = h.dtype
    split_at = [int(i) for i in np.cumsum(SPLIT_SIZES)[:-1]]
    parts = jnp.split(h @ w_in, split_at, axis=-1)
    (gdn_qkv, gdn_a, gdn_b, gdn_z, hg_q, hg_f, hg_i, hg_g,
     ssd_z, ssd_xbc, ssd_dt, gate_raw) = parts
    o_a = gdn_branch(_f32(gdn_qkv), _f32(gdn_a), _f32(gdn_b), _f32(gdn_z),
                     _f32(gdn_conv_w), _f32(gdn_a_log), _f32(gdn_dt_bias), _f32(gdn_norm_w))
    o_b = hgrn2_branch(_f32(hg_q), _f32(hg_f), _f32(hg_i), _f32(hg_g), lb, _f32(hgrn_norm_w))
    o_c = ssd_branch(_f32(ssd_z), _f32(ssd_xbc), _f32(ssd_dt), _f32(ssd_conv_w), _f32(ssd_conv_b),
                     _f32(ssd_a_log), _f32(ssd_dt_bias), _f32(ssd_d), _f32(ssd_norm_w))
    gates = jax.nn.sigmoid(gate_raw).reshape(bsz, s, N_BRANCH, D_MODEL)
    merged = (gates[:, :, 0] * (o_a.astype(dtype) @ w_br_a)
              + gates[:, :, 1] * (o_b.astype(dtype) @ w_br_b)
              + gates[:, :, 2] * (o_c.astype(dtype) @ w_br_c))
    return merged @ w_out


def conv_glu_ffn(h, w_up, conv_w, conv_b, w_down):
    u = causal_dwconv(h @ w_up, conv_w, conv_b)
    gate, val = jnp.split(u, 2, axis=-1)
    return (jax.nn.silu(gate) * val) @ w_down


def _log_uniform_dt_bias(key, shape):
    lo, hi = math.log(1e-3), math.log(1e-1)
    dt = jnp.exp(jax.random.uniform(key, shape) * (hi - lo) + lo)
    return dt + jnp.log(-jnp.expm1(-dt))


def setup_inputs(seed: int = 0) -> dict:
    key = jax.random.key(seed)
    ks = jax.random.split(key, 32)
    nrm = jax.random.normal
    d = D_MODEL
    f2 = 2 * FFN_HIDDEN
    gdn_qkv_w = 2 * GDN_KEY_W + GDN_VAL_W
    return {
        "x": nrm(ks[0], (BATCH, SEQ, d), jnp.float32),
        "c": nrm(ks[1], (BATCH, d), jnp.float32),
        "w_ada": nrm(ks[2], (DEPTH, d, 6 * d)) * (0.5 * d ** -0.5),
        "b_ada": 0.02 * nrm(ks[3], (DEPTH, 6 * d)),
        "norm1_w": 1.0 + 0.05 * nrm(ks[4], (DEPTH, d)),
        "w_in": nrm(ks[5], (DEPTH, d, N_IN)) * d ** -0.5,
        "gdn_conv_w": nrm(ks[6], (DEPTH, GDN_CONV, gdn_qkv_w)) * GDN_CONV ** -0.5,
        "gdn_a_log": jnp.log(jax.random.uniform(ks[7], (DEPTH, GDN_HEADS), minval=1.0, maxval=16.0)),
        "gdn_dt_bias": _log_uniform_dt_bias(ks[8], (DEPTH, GDN_HEADS)),
        "gdn_norm_w": 1.0 + 0.05 * nrm(ks[9], (DEPTH, GDN_DV)),
        "hgrn_lb_param": nrm(ks[10], (DEPTH, HGRN_KEY_W)),
        "hgrn_norm_w": 1.0 + 0.05 * nrm(ks[11], (DEPTH, HGRN_DV)),
        "ssd_conv_w": nrm(ks[12], (DEPTH, SSD_CONV, SSD_XBC_W)) * SSD_CONV ** -0.5,
        "ssd_conv_b": 0.02 * nrm(ks[13], (DEPTH, SSD_XBC_W)),
        "ssd_a_log": jnp.log(jax.random.uniform(ks[14], (DEPTH, SSD_HEADS), minval=1.0, maxval=16.0)),
        "ssd_dt_bias": _log_uniform_dt_bias(ks[15], (DEPTH, SSD_HEADS)),
        "ssd_d": 1.0 + 0.05 * nrm(ks[16], (DEPTH, SSD_HEADS)),
        "ssd_norm_w": 1.0 + 0.05 * nrm(ks[17], (DEPTH, SSD_INNER)),
        "w_br_a": nrm(ks[18], (DEPTH, GDN_VAL_W, d)) * GDN_VAL_W ** -0.5,
        "w_br_b": nrm(ks[19], (DEPTH, HGRN_VAL_W, d)) * HGRN_VAL_W ** -0.5,
        "w_br_c": nrm(ks[20], (DEPTH, SSD_INNER, d)) * SSD_INNER ** -0.5,
        "w_out": nrm(ks[21], (DEPTH, d, d)) * d ** -0.5,
        "norm2_w": 1.0 + 0.05 * nrm(ks[22], (DEPTH, d)),
        "ffn_w_up": nrm(ks[23], (DEPTH, d, f2)) * d ** -0.5,
        "ffn_conv_w": nrm(ks[24], (DEPTH, FFN_CONV, f2)) * FFN_CONV ** -0.5,
        "ffn_conv_b": 0.02 * nrm(ks[25], (DEPTH, f2)),
        "ffn_w_down": nrm(ks[26], (DEPTH, FFN_HIDDEN, d)) * FFN_HIDDEN ** -0.5,
        "final_norm_w": 1.0 + 0.05 * nrm(ks[27], (d,)),
    }


def reference(x, c, w_ada, b_ada, norm1_w, w_in, gdn_conv_w, gdn_a_log, gdn_dt_bias, gdn_norm_w,
              hgrn_lb_param, hgrn_norm_w, ssd_conv_w, ssd_conv_b, ssd_a_log, ssd_dt_bias, ssd_d,
              ssd_norm_w, w_br_a, w_br_b, w_br_c, w_out, norm2_w, ffn_w_up, ffn_conv_w, ffn_conv_b,
              ffn_w_down, final_norm_w):
    c_act = jax.nn.silu(c)
    lb_soft = jax.nn.softmax(hgrn_lb_param.astype(jnp.float32), axis=0)
    lower_bounds = jnp.cumsum(lb_soft, axis=0) - lb_soft[0]
    for l in range(DEPTH):
        mod = c_act @ w_ada[l] + b_ada[l]
        shift1, scale1, gate1, shift2, scale2, gate2 = jnp.split(mod, 6, axis=-1)
        h = modulate(rms_norm(x, norm1_w[l]), shift1, scale1)
        mix = token_mixing(h, lower_bounds[l], w_in[l], gdn_conv_w[l], gdn_a_log[l], gdn_dt_bias[l],
                           gdn_norm_w[l], hgrn_norm_w[l], ssd_conv_w[l], ssd_conv_b[l], ssd_a_log[l],
                           ssd_dt_bias[l], ssd_d[l], ssd_norm_w[l], w_br_a[l], w_br_b[l], w_br_c[l],
                           w_out[l])
        x = x + gate1[:, None, :] * mix
        h = modulate(rms_norm(x, norm2_w[l]), shift2, scale2)
        x = x + gate2[:, None, :] * conv_glu_ffn(h, ffn_w_up[l], ffn_conv_w[l], ffn_conv_b[l], ffn_w_down[l])
    return rms_norm(x, final_norm_w)
```

```python
import numpy as np
from contextlib import ExitStack
import concourse.bass as bass
import concourse.mybir as mybir
from concourse.bass_utils import run_bass_kernel_spmd

F32 = mybir.dt.float32
BF16 = mybir.dt.bfloat16
AF = mybir.ActivationFunctionType
ALU = mybir.AluOpType

D = 1024
NIN = 8720
FH = 2816
F2 = 5632
EPS = 1e-6
O_Q, O_K, O_V = 0, 512, 1024
O_A, O_B, O_Z = 1536, 1540, 1544
O_HQ, O_HF, O_HI, O_HG = 2056, 2568, 3080, 3592
O_SZ, O_SX, O_SB, O_SC, O_DT, O_GATE = 4104, 4616, 5128, 5384, 5640, 5648


class V:
    __slots__ = ("bufs", "ap")

    def __init__(self, bufs, ap):
        self.bufs = bufs
        self.ap = ap

    def __getitem__(self, idx):
        return V(self.bufs, self.ap[idx])

    def re(self, s, **kw):
        return V(self.bufs, self.ap.rearrange(s, **kw))

    def bc(self, shape):
        return V(self.bufs, self.ap.to_broadcast(list(shape)))

    def un(self, axis):
        return V(self.bufs, self.ap.unsqueeze(axis))


class Buf:
    def __init__(self, name, t, ap):
        self.name = name
        self.t = t
        self.ap = ap
        self.lw = None
        self.rd = []
        self.psum = False

    def __getitem__(self, idx):
        return V((self,), self.ap[idx])

    @property
    def v(self):
        return V((self,), self.ap)


class KB:
    ENG = ("pe", "act", "dve", "pool", "sp")
    EP = 30000
    NEP = 10

    def __init__(self, nc, n_dma_sems=24, same_engine_sync=True):
        self.nc = nc
        self.st = ExitStack()
        self.q = {e: [] for e in self.ENG}
        self.sems = {}
        self.cnt = {}
        for e in ("pe", "act", "dve", "pool"):
            for ep in range(self.NEP):
                self.sems["%s#%d" % (e, ep)] = self.st.enter_context(nc.semaphore("c_%s_%d" % (e, ep)))
            self.cnt[e] = 0
        self.dsem = []
        for i in range(n_dma_sems):
            k = "d%d" % i
            self.sems[k] = self.st.enter_context(nc.semaphore(k))
            self.cnt[k] = 0
            self.dsem.append(k)
        self.dnext = 0
        self.dnext2 = 0
        self.waited = {e: {} for e in self.ENG}
        self.same = same_engine_sync
        self.nbuf = 0
        self.ninst = 0

    def sb(self, shape, dtype=F32, name=None):
        self.nbuf += 1
        name = name or ("b%d" % self.nbuf)
        t = self.st.enter_context(self.nc.sbuf_tensor(name, list(shape), dtype))
        return Buf(name, t, t[:])

    def ps(self, shape, dtype=F32, name=None):
        self.nbuf += 1
        name = name or ("p%d" % self.nbuf)
        t = self.st.enter_context(self.nc.psum_tensor(name, list(shape), dtype))
        b = Buf(name, t, t[:])
        b.psum = True
        return b

    def dram(self, name, shape, dtype, kind):
        t = self.nc.dram_tensor(name, list(shape), dtype, kind=kind)
        return Buf(name, t, t.ap())

    def _wait(self, eng, key, val):
        w = self.waited[eng]
        if w.get(key, 0) >= val:
            return
        w[key] = val
        sem = self.sems[key]
        self.q[eng].append(lambda e, sem=sem, val=val: e.wait_ge(sem, val))

    def _deps(self, eng, reads, writes):
        need = {}
        for v in reads:
            for b in v.bufs:
                if b.lw is not None:
                    k, val = b.lw
                    need[k] = max(need.get(k, 0), val)
        for v in writes:
            for b in v.bufs:
                if b.lw is not None:
                    k, val = b.lw
                    need[k] = max(need.get(k, 0), val)
                for (k, val) in b.rd:
                    need[k] = max(need.get(k, 0), val)
        for k, val in need.items():
            if k.split("#")[0] == eng and (eng == "pe" or not self.same):
                continue
            self._wait(eng, k, val)

    def _mark(self, ticket, reads, writes):
        for v in writes:
            for b in v.bufs:
                b.lw = ticket
                b.rd = []
        for v in reads:
            for b in v.bufs:
                if b.lw != ticket:
                    b.rd.append(ticket)
                    if len(b.rd) > 32:
                        m = {}
                        for (k, val) in b.rd:
                            m[k] = max(m.get(k, 0), val)
                        b.rd = list(m.items())

    def op(self, eng, fn, reads, writes):
        pr = [v for v in reads if any(b.psum for b in v.bufs)]
        if pr:
            writes = list(writes) + pr
        self._deps(eng, reads, writes)
        ep, within = divmod(self.cnt[eng], self.EP)
        self.cnt[eng] += 1
        self.ninst += 1
        key = "%s#%d" % (eng, ep)
        sem = self.sems[key]
        self.q[eng].append(lambda e, fn=fn, sem=sem: fn(e).then_inc(sem, 1))
        self._mark((key, within + 1), reads, writes)

    def dma(self, out, in_, eng="sp", **kw):
        if eng == "sp":
            k = self.dsem[self.dnext]
            self.dnext = (self.dnext + 1) % (len(self.dsem) - 8)
        else:
            k = self.dsem[len(self.dsem) - 8 + self.dnext2]
            self.dnext2 = (self.dnext2 + 1) % 8
        if self.cnt[k] > 0:
            self._wait(eng, k, self.cnt[k])
        self._deps(eng, [in_], [out])
        self.cnt[k] += 16
        self.ninst += 1
        sem = self.sems[k]
        self.q[eng].append(lambda e, o=out.ap, i=in_.ap, sem=sem, kw=kw: e.dma_start(out=o, in_=i, **kw).then_inc(sem, 16))
        t = (k, self.cnt[k])
        self._mark(t, [in_], [out])
        return t

    def mm(self, out, lhsT, rhs, start=True, stop=True):
        rd = [lhsT, rhs] + ([] if start else [out])
        self.op("pe", lambda e, o=out.ap, l=lhsT.ap, r=rhs.ap: e.matmul(o, l, r, start=start, stop=stop), rd, [out])

    def tr(self, out, in_, ident):
        self.op("pe", lambda e, o=out.ap, i=in_.ap, d=ident.ap: e.transpose(o, i, d), [in_, ident], [out])

    def act(self, out, in_, func, scale=None, bias=None, accum=None):
        rd = [in_]
        kw = {}
        if scale is not None:
            if isinstance(scale, V):
                rd.append(scale); kw["scale"] = scale.ap
            else:
                kw["scale"] = scale
        if bias is not None:
            if isinstance(bias, V):
                rd.append(bias); kw["bias"] = bias.ap
            else:
                kw["bias"] = bias
        wr = [out]
        if accum is not None:
            wr.append(accum); kw["accum_out"] = accum.ap
        self.op("act", lambda e, o=out.ap, i=in_.ap, kw=kw: e.activation(o, i, func, **kw), rd, wr)

    def tt(self, out, a, b, op, eng="dve"):
        self.op(eng, lambda e, o=out.ap, a_=a.ap, b_=b.ap: e.tensor_tensor(o, a_, b_, op), [a, b], [out])

    def ts(self, out, a, s1, op0, s2=None, op1=None, eng="dve"):
        rd = [a]
        s1a = s1.ap if isinstance(s1, V) else s1
        s2a = s2.ap if isinstance(s2, V) else s2
        if isinstance(s1, V): rd.append(s1)
        if isinstance(s2, V): rd.append(s2)
        if op1 is None:
            self.op(eng, lambda e, o=out.ap, a_=a.ap: e.tensor_scalar(o, a_, s1a, None, op0), rd, [out])
        else:
            self.op(eng, lambda e, o=out.ap, a_=a.ap: e.tensor_scalar(o, a_, s1a, s2a, op0, op1), rd, [out])

    def stt(self, out, a, s, b, op0, op1):
        rd = [a, b]
        sa = s.ap if isinstance(s, V) else s
        if isinstance(s, V): rd.append(s)
        self.op("dve", lambda e, o=out.ap, a_=a.ap, b_=b.ap: e.scalar_tensor_tensor(o, a_, sa, b_, op0, op1), rd, [out])

    def scan(self, out, d0, d1, init, op0, op1):
        self.op("dve", lambda e, o=out.ap, a_=d0.ap, b_=d1.ap: e.tensor_tensor_scan(o, a_, b_, init, op0, op1), [d0, d1], [out])

    def copy(self, out, in_, eng="dve"):
        if eng == "act":
            self.act(out, in_, AF.Copy)
        else:
            self.op(eng, lambda e, o=out.ap, i=in_.ap: e.tensor_copy(o, i), [in_], [out])

    def memset(self, out, val, eng="pool"):
        self.op(eng, lambda e, o=out.ap: e.memset(o, val), [], [out])

    def recip(self, out, in_):
        self.op("dve", lambda e, o=out.ap, i=in_.ap: e.reciprocal(o, i), [in_], [out])

    def finish(self, out_tickets):
        for (k, val) in out_tickets:
            self._wait("sp", k, val)
        nc = self.nc
        with nc.Block() as block:
            @block.tensor
            def _(e):
                for f in self.q["pe"]: f(e)

            @block.scalar
            def _(e):
                for f in self.q["act"]: f(e)

            @block.vector
            def _(e):
                for f in self.q["dve"]: f(e)

            @block.gpsimd
            def _(e):
                for f in self.q["pool"]: f(e)

            @block.sync
            def _(e):
                for f in self.q["sp"]: f(e)
        self.st.close()


def make_masks():
    t = np.arange(128)
    r = t[:, None]
    c = t[None, :]
    same64 = (r // 64) == (c // 64)
    same32 = (r // 32) == (c // 32)
    m = np.zeros((128, 11, 128), np.float32)
    m[:, 0] = (r == c)
    m[:, 1] = 1.0
    m[:, 2] = (c >= r) & same64
    m[:, 3] = (r > c)
    m[:, 4] = (r <= c)
    m[:, 5] = (r <= c) & same64
    m[:, 6] = (r > c) & same64
    m[:, 7] = (r < 64) & (c >= 0)
    m[:, 8] = (r >= 64) & (c >= 0)
    m[:, 9] = (c >= r) & same32
    ind = np.zeros((128, 128), np.float32)
    for j in range(4):
        ind[:, j] = (t // 32 == j)
    ind[:, 4] = (t < 64)
    ind[:, 5] = (t >= 64)
    m[:, 10] = ind
    return m


PNAMES = ["w_ada", "b_ada", "norm1_w", "w_in", "gdn_conv_w", "gdn_a_log", "gdn_dt_bias", "gdn_norm_w",
          "hgrn_lb_param", "hgrn_norm_w", "ssd_conv_w", "ssd_conv_b", "ssd_a_log", "ssd_dt_bias", "ssd_d",
          "ssd_norm_w", "w_br_a", "w_br_b", "w_br_c", "w_out", "norm2_w", "ffn_w_up", "ffn_conv_w",
          "ffn_conv_b", "ffn_w_down", "final_norm_w"]
PSHAPES = {
    "w_ada": (D, 6 * D), "b_ada": (6 * D,), "norm1_w": (D,), "w_in": (D, NIN), "gdn_conv_w": (4, 1536),
    "gdn_a_log": (4,), "gdn_dt_bias": (4,), "gdn_norm_w": (128,), "hgrn_lb_param": (512,),
    "hgrn_norm_w": (128,), "ssd_conv_w": (4, 1024), "ssd_conv_b": (1024,), "ssd_a_log": (8,),
    "ssd_dt_bias": (8,), "ssd_d": (8,), "ssd_norm_w": (512,), "w_br_a": (512, D), "w_br_b": (512, D),
    "w_br_c": (512, D), "w_out": (D, D), "norm2_w": (D,), "ffn_w_up": (D, F2), "ffn_conv_w": (3, F2),
    "ffn_conv_b": (F2,), "ffn_w_down": (FH, D),
}


def build(S, NSEQ, DEPTH=2, T=256, dbg=False, stop=99):
    NS = T // 128
    NT = S // T
    assert S % T == 0
    nc = bass.Bass("TRN2", target_bir_lowering=False)
    kb = KB(nc)
    x_d = kb.dram("x", [NSEQ * S, D], F32, "ExternalInput")
    c_d = kb.dram("c", [NSEQ, D], F32, "ExternalInput")
    cm_d = kb.dram("cmask", [128, 11, 128], F32, "ExternalInput")
    P = {}
    for n in PNAMES:
        if n == "final_norm_w":
            P[n] = kb.dram(n, [D], F32, "ExternalInput")
        else:
            P[n] = kb.dram(n, [DEPTH] + list(PSHAPES[n]), F32, "ExternalInput")
    out_d = kb.dram("out", [NSEQ * S, D], F32, "ExternalOutput")
    dbg_d = {}

    def dbgout(name, view, shape):
        if not dbg:
            return None
        dbg_d[name] = kb.dram("dbg_" + name, list(shape), F32, "ExternalOutput")
        return kb.dma(dbg_d[name].v, view)

    WIN_BLK = [O_Q, O_K, O_V, O_Z, O_HQ, O_HF, O_HI, O_HG, O_SZ, O_SX, O_SB] + [O_GATE + 512 * j for j in range(6)]
    WIN_IDX = {c0: i for i, c0 in enumerate(WIN_BLK)}
    win_s = [kb.dram("win_s%d" % l, [128, 17, 4096], BF16, "Internal") for l in range(DEPTH)]
    wbr_s = [[kb.dram("wbr_s%d_%d" % (l, b), [128, 4, D], BF16, "Internal") for b in range(3)] for l in range(DEPTH)]
    wout_s = [kb.dram("wout_s%d" % l, [128, 2, 4096], BF16, "Internal") for l in range(DEPTH)]
    wup_s = [kb.dram("wup_s%d" % l, [128, 11, 4096], BF16, "Internal") for l in range(DEPTH)]
    wdn_s = [kb.dram("wdn_s%d" % l, [128, 8, 22 * 128], BF16, "Internal") for l in range(DEPTH)]

    tickets = []
    cm = kb.sb([128, 11, 128], F32, "cm")
    kb.dma(cm.v, cm_d.v)
    IDENT = cm[:, 0, :]
    ONES = cm[:, 1, :]
    M_INCL64 = cm[:, 2, :]
    M_UPALL = cm[:, 3, :]
    M_TRIALL = cm[:, 4, :]
    M_BTRI64 = cm[:, 5, :]
    M_BUP64 = cm[:, 6, :]
    M_IND64 = [cm[:, 7, :], cm[:, 8, :]]
    M_INCL32 = cm[:, 9, :]
    INDC = cm[:, 10, :]
    ones_bf = kb.sb([128, 128], BF16, "ones_bf")
    kb.memset(ones_bf.v, 1.0)
    resetm = kb.sb([128, T], F32, "resetm")
    kb.memset(resetm.v, 1.0)
    kb.memset(resetm.v.re("p (c r) -> p c r", r=32)[:, :, 0:1], 0.0)

    banks = [kb.ps([128, 512], F32, "bank%d" % i) for i in range(8)]
    bstate = {"i": 0}

    def bank():
        b = banks[bstate["i"]]
        bstate["i"] = (bstate["i"] + 1) % 6
        return b

    def pload(name, l, pattern, shape, **kw):
        b = kb.sb(shape, F32, "%s_%d" % (name, l))
        src = P[name].ap[l] if name != "final_norm_w" else P[name].ap
        C = shape[1]
        for c in range(C):
            if len(shape) == 3:
                kb.dma(b[:, c, :], V((P[name],), src[:, c * 128:(c + 1) * 128].rearrange("k p -> p k")),
                       allow_slow_non_contiguous=True)
            else:
                kb.dma(b[:, c:c + 1], V((P[name],), src[c * 128:(c + 1) * 128].rearrange("(p o) -> p o", o=1)),
                       allow_slow_non_contiguous=True)
        return b

    def pbc(name, l, n):
        b = kb.sb([128, n], F32, "%s_bc%d" % (name, l))
        kb.dma(b.v, V((P[name],), P[name].ap[l].partition_broadcast(128)))
        return b

    cT = kb.sb([128, 8, NSEQ], F32, "cT")
    for s_ in range(NSEQ):
        kb.dma(cT[:, :, s_], V((c_d,), c_d.ap[s_].rearrange("(k p) -> p k", p=128)), allow_slow_non_contiguous=True)
    cact = kb.sb([128, 8, NSEQ], F32, "cact")
    kb.act(cact.v, cT.v, AF.Silu)
    fnw = pload("final_norm_w", 0, "(c p) -> p c", [128, 8], p=128)

    LP = []
    stage_f = [kb.sb([128, 1024], F32, "stf%d" % i) for i in range(2)]
    stage_b = [kb.sb([128, 1024], BF16, "stb%d" % i) for i in range(2)]
    stg = {"i": 0}
    for l in range(DEPTH):
        L = {}
        L["n1"] = pload("norm1_w", l, "(c p) -> p c", [128, 8], p=128)
        L["n2"] = pload("norm2_w", l, "(c p) -> p c", [128, 8], p=128)
        L["bada"] = pload("b_ada", l, "(c p) -> p c", [128, 48], p=128)
        L["gcw"] = pload("gdn_conv_w", l, "k (c p) -> p c k", [128, 12, 4], p=128)
        L["scw"] = pload("ssd_conv_w", l, "k (c p) -> p c k", [128, 8, 4], p=128)
        L["scb"] = pload("ssd_conv_b", l, "(c p) -> p c", [128, 8], p=128)
        L["fcw"] = pload("ffn_conv_w", l, "k (c p) -> p c k", [128, 44, 3], p=128)
        L["fcb"] = pload("ffn_conv_b", l, "(c p) -> p c", [128, 44], p=128)
        L["gnw"] = pload("gdn_norm_w", l, "(c p) -> p c", [128, 1], p=128)
        L["hnw"] = pload("hgrn_norm_w", l, "(c p) -> p c", [128, 1], p=128)
        L["snw"] = pload("ssd_norm_w", l, "(c p) -> p c", [128, 4], p=128)
        alog = kb.sb([128, 12], F32, "alog%d" % l)
        kb.dma(alog[:, 0:4], V((P["gdn_a_log"],), P["gdn_a_log"].ap[l].partition_broadcast(128)))
        kb.dma(alog[:, 4:12], V((P["ssd_a_log"],), P["ssd_a_log"].ap[l].partition_broadcast(128)))
        dtb = kb.sb([128, 12], F32, "dtb%d" % l)
        kb.dma(dtb[:, 0:4], V((P["gdn_dt_bias"],), P["gdn_dt_bias"].ap[l].partition_broadcast(128)))
        kb.dma(dtb[:, 4:12], V((P["ssd_dt_bias"],), P["ssd_dt_bias"].ap[l].partition_broadcast(128)))
        nea = kb.sb([128, 12], F32, "nea%d" % l)
        kb.act(nea.v, alog.v, AF.Exp)
        kb.ts(nea.v, nea.v, -1.0, ALU.mult)
        L["nea"] = nea
        L["dtb"] = dtb
        dsk = pbc("ssd_d", l, 8)
        dfull = kb.sb([128, 8, 64], F32, "dfull%d" % l)
        kb.copy(dfull.v, dsk.v.un(2).bc([128, 8, 64]))
        L["dfull"] = dfull
        LP.append(L)

    lbp = [pload("hgrn_lb_param", l, "(h p) -> p h", [128, 4], p=128) for l in range(DEPTH)]
    persist = [{k_: kb.sb([128, 32], F32, "ps_%s_%d" % (k_, s_)) for k_ in ("g", "spl", "bt", "e1", "e2m", "e3")} for s_ in range(2)]
    gepool = [kb.sb([128, 32], F32, "gepool%d" % i) for i in range(2)]
    lbe = [kb.sb([128, 4], F32, "lbe%d" % l) for l in range(DEPTH)]
    for l in range(DEPTH):
        kb.act(lbe[l].v, lbp[l].v, AF.Exp)
    lsum = kb.sb([128, 4], F32, "lsum")
    kb.copy(lsum.v, lbe[0].v)
    for l in range(1, DEPTH):
        kb.tt(lsum.v, lsum.v, lbe[l].v, ALU.add)
    lrs = kb.sb([128, 4], F32, "lrs")
    kb.recip(lrs.v, lsum.v)
    cum = kb.sb([128, 4], F32, "lcum")
    kb.memset(cum.v, 0.0)
    for l in range(DEPTH):
        lb = kb.sb([128, 4], F32, "lb%d" % l)
        oml = kb.sb([128, 4], F32, "oml%d" % l)
        if l > 0:
            sm = kb.sb([128, 4], F32, "lsm%d" % l)
            kb.tt(sm.v, lbe[l].v, lrs.v, ALU.mult)
            kb.tt(cum.v, cum.v, sm.v, ALU.add)
        kb.copy(lb.v, cum.v)
        kb.ts(oml.v, lb.v, -1.0, ALU.mult, 1.0, ALU.add)
        LP[l]["lb"] = lb
        LP[l]["oml"] = oml

    for l in range(DEPTH):
        L = LP[l]
        mod = kb.sb([128, 48, NSEQ], F32, "mod%d" % l)
        for fc in range(48):
            sf = stage_f[stg["i"] % 2]; stg["i"] += 1
            sfv = sf.v.re("p (k n) -> p k n", n=128)
            kb.dma(sfv, V((P["w_ada"],), P["w_ada"].ap[l][:, fc * 128:(fc + 1) * 128].rearrange("(k p) n -> p k n", p=128)))
            pb = bank()
            for k in range(8):
                kb.mm(pb[:, 0:NSEQ], sfv[:, k, :], cact[:, k, :], start=(k == 0), stop=(k == 7))
            kb.ts(mod[:, fc, :], pb[:, 0:NSEQ], L["bada"][:, fc:fc + 1], ALU.add)
        L["mod"] = mod
        a1 = kb.sb([128, 8, NSEQ], F32, "a1_%d" % l)
        a2 = kb.sb([128, 8, NSEQ], F32, "a2_%d" % l)
        for s_ in range(NSEQ):
            kb.stt(a1[:, :, s_], mod[:, 8:16, s_], 1.0, L["n1"].v, ALU.add, ALU.mult)
            kb.stt(a2[:, :, s_], mod[:, 32:40, s_], 1.0, L["n2"].v, ALU.add, ALU.mult)
        L["a1"], L["a2"] = a1, a2

    DEPTH_C = DEPTH if stop >= 1 else 0
    cast_eng = ["act", "dve", "pool"]
    cst = {"i": 0}

    def cast_piece(src_buf, src_ap_piece, w, dst_v, src_to_dst=None, scale_v=None):
        i = stg["i"] % 2; stg["i"] += 1
        sf, sbb = stage_f[i], stage_b[i]
        kb.dma(sf[:, 0:w], V((src_buf,), src_ap_piece), eng=("sp" if i == 0 else "pool"))
        if scale_v is not None:
            kb.ts(sbb[:, 0:w], sf[:, 0:w], scale_v, ALU.mult)
        else:
            e = cast_eng[cst["i"] % 3]; cst["i"] += 1
            kb.copy(sbb[:, 0:w], sf[:, 0:w], eng=e)
        sv = sbb[:, 0:w] if src_to_dst is None else src_to_dst(sbb[:, 0:w])
        kb.dma(dst_v, sv, eng=("sp" if i == 1 else "pool"))

    def cast_weight(src_buf, src_ap, K, N, dst, scale=None):
        KC = K // 128
        for k in range(KC):
            for n0 in range(0, N, 1024):
                w = min(1024, N - n0)
                cast_piece(src_buf, src_ap[k * 128:(k + 1) * 128, n0:n0 + w], w, dst[:, k, n0:n0 + w],
                           scale_v=(scale[:, k:k + 1] if scale is not None else None))

    for l in range(DEPTH_C):
        L = LP[l]
        wi = P["w_in"].ap[l]
        for k in range(8):
            for bi, c0 in enumerate(WIN_BLK):
                cast_piece(P["w_in"], wi[k * 128:(k + 1) * 128, c0:c0 + 512], 512, win_s[l][:, bi, k * 512:(k + 1) * 512])
        cast_weight(P["w_br_a"], P["w_br_a"].ap[l], 512, D, wbr_s[l][0], scale=L["gnw"][:, 0:1].bc([128, 4]))
        cast_weight(P["w_br_b"], P["w_br_b"].ap[l], 512, D, wbr_s[l][1], scale=L["hnw"][:, 0:1].bc([128, 4]))
        cast_weight(P["w_br_c"], P["w_br_c"].ap[l], 512, D, wbr_s[l][2], scale=L["snw"].v)
        wo_ = P["w_out"].ap[l]
        for k in range(8):
            cast_piece(P["w_out"], wo_[k * 128:(k + 1) * 128, :], 1024,
                       wout_s[l].v.re("p b (k c) -> p b k c", c=512)[:, :, k, :],
                       src_to_dst=lambda v: v.re("p (b c) -> p b c", c=512))
        wu_ = P["ffn_w_up"].ap[l]
        for k in range(8):
            for half in range(2):
                for n0 in range(0, FH, 1024):
                    w = min(1024, FH - n0)
                    b0, nb = n0 // 256, w // 256
                    cast_piece(P["ffn_w_up"], wu_[k * 128:(k + 1) * 128, half * FH + n0:half * FH + n0 + w], w,
                               wup_s[l].v.re("p b (k c) -> p b k c", c=512)[:, b0:b0 + nb, k, half * 256:(half + 1) * 256],
                               src_to_dst=lambda v: v.re("p (b c) -> p b c", c=256))
        wd_ = P["ffn_w_down"].ap[l]
        for k in range(22):
            cast_piece(P["ffn_w_down"], wd_[k * 128:(k + 1) * 128, :], 1024,
                       wdn_s[l].v.re("p b (k c) -> p b k c", c=128)[:, :, k, :],
                       src_to_dst=lambda v: v.re("p (b c) -> p b c", c=128))
        wsm = kb.sb([128, 8, 16], BF16, "wsm%d" % l)
        sf = stage_f[stg["i"] % 2]; stg["i"] += 1
        sfv = sf[:, 0:128].re("p (k n) -> p k n", n=16)
        for (d0, c0, n) in ((0, O_A, 4), (4, O_DT, 8), (12, O_B, 4)):
            kb.dma(sfv[:, :, d0:d0 + n], V((P["w_in"],), wi[:, c0:c0 + n].rearrange("(k p) n -> p k n", p=128)),
                   allow_slow_non_contiguous=True)
        kb.copy(wsm.v, sfv)
        L["wsm"] = wsm

    NW = 4
    wbufs = [kb.sb([128, 4096], BF16, "wbuf%d" % i) for i in range(NW)]
    wst = {"i": 0}

    def wload(src_view, kc, ncols):
        b = wbufs[wst["i"] % NW]; wst["i"] += 1
        kb.dma(b.v[:, 0:kc * ncols], src_view)
        return b.v[:, 0:kc * ncols].re("p (k n) -> p k n", n=ncols)

    xT = kb.sb([128, 8, T], F32, "xT")
    xio = kb.sb([128, NS, D], F32, "xio")
    hT = kb.sb([128, 8, T], BF16, "hT")
    macc = kb.sb([128, 8, T], F32, "macc")
    mergedT = kb.sb([128, 8, T], BF16, "mergedT")
    obr = [kb.sb([128, 4, T], BF16, "obr%d" % i) for i in range(3)]
    hidT = kb.sb([128, 22, T], BF16, "hidT")
    ftmp = [kb.sb([128, T + 4], F32, "ftmp%d" % i) for i in range(14)]
    fst = {"i": 0}

    def ft():
        b = ftmp[fst["i"] % len(ftmp)]; fst["i"] += 1
        return b
    ft_global = ft

    abf = [kb.sb([128, 4, T], BF16, "abf%d" % i) for i in range(3)]
    tokbf = [[kb.sb([128, 512], BF16, "tokbf%d_%d" % (i, s_)) for s_ in range(NS)] for i in range(4)]
    tokf = [[kb.sb([128, 512], F32, "tokf%d_%d" % (i, s_)) for s_ in range(NS)] for i in range(2)]
    sq128 = [kb.sb([128, 128], F32, "sq%d" % i) for i in range(2)]
    sqst = {"i": 0}

    def sq():
        b = sq128[sqst["i"] % len(sq128)]; sqst["i"] += 1
        return b

    def alias(parent, ap, name):
        b = Buf(name, parent.t, ap)
        b.lw = parent.lw
        b.rd = list(parent.rd)
        return b

    sq_extra = [alias(stage_f[i // 8], stage_f[i // 8].ap[:, (i % 8) * 128:(i % 8 + 1) * 128], "sqx%d" % i) for i in range(16)]
    bf_extra = [alias(stage_b[i // 8], stage_b[i // 8].ap[:, (i % 8) * 128:(i % 8 + 1) * 128], "bfx%d" % i) for i in range(16)]
    gdn_tp = [{"sq": sq_extra[h * 4:h * 4 + 4] + [kb.sb([128, 128], F32, "gsq%d_%d" % (h, i)) for i in range(2)],
               "bf": bf_extra[h * 4:h * 4 + 4]} for h in range(4)]

    for h_ in range(4):
        kb.memset(gdn_tp[h_]["bf"][3].v, 0.0)

    def run_threads(gens):
        gens = list(gens)
        while gens:
            for g in list(gens):
                try:
                    next(g)
                except StopIteration:
                    gens.remove(g)

    sqb = [kb.sb([128, 128], BF16, "sqb%d" % i) for i in range(4)]
    sqbst = {"i": 0}

    def sqbf():
        b = sqb[sqbst["i"] % len(sqb)]; sqbst["i"] += 1
        return b

    big4 = [kb.sb([128, 4, 128], F32, "big4_%d" % i) for i in range(4)]
    big4b = [kb.sb([128, 4, 128], BF16, "big4b_%d" % i) for i in range(2)]
    small = [kb.sb([128, 32], F32, "small%d" % i) for i in range(16)]
    smst = {"i": 0}

    def sm_():
        b = small[smst["i"] % len(small)]; smst["i"] += 1
        return b

    qgz = [[kb.sb([128, 640], BF16, "qgz%d_%d" % (i, s_)) for s_ in range(NS)] for i in range(2)]
    for i in range(2):
        for s_ in range(NS):
            kb.memset(qgz[i][s_].v, 0.0)
    kendz = [[kb.sb([128, 4, 128], BF16, "kendz%d_%d" % (i, s_)) for s_ in range(NS)] for i in range(2)]
    vnew = kb.sb([128, 128], BF16, "vnew")
    kb.memset(vnew.v, 0.0)
    otok = [kb.sb([128, 256], F32, "otok%d" % i) for i in range(6)]
    otst = {"i": 0}

    def ot():
        b = otok[otst["i"] % 6]; otst["i"] += 1
        return b

    ST = []
    for l in range(DEPTH):
        s_ = {}
        s_["gS"] = kb.sb([128, 4, 128], F32, "gS%d" % l)
        s_["gSb"] = kb.sb([128, 4, 128], BF16, "gSb%d" % l)
        s_["hS"] = kb.sb([128, 4, 128], F32, "hS%d" % l)
        s_["hSb"] = kb.sb([128, 4, 128], BF16, "hSb%d" % l)
        s_["sS"] = kb.sb([128, 2, 256], F32, "sS%d" % l)
        s_["sSb"] = kb.sb([128, 2, 256], BF16, "sSb%d" % l)
        s_["gh"] = kb.sb([128, 12, 3], F32, "gh%d" % l)
        s_["sh"] = kb.sb([128, 8, 3], F32, "sh%d" % l)
        s_["fh"] = kb.sb([128, 44, 2], F32, "fh%d" % l)
        ST.append(s_)

    def rms_mod(a_v, b_v, out_bf=None, out_f32=None):
        pb = bank()
        for c in range(8):
            s2 = ft()
            kb.act(s2[:, 0:T], xT[:, c, :], AF.Square)
            s2b = abf[2]
            kb.copy(s2b[:, c % 4, :], s2[:, 0:T], eng="pool")
            kb.mm(pb[:, 0:T], ones_bf.v, s2b[:, c % 4, :], start=(c == 0), stop=(c == 7))
        rs = ft()
        kb.act(rs[:, 0:T], pb[:, 0:T], AF.Sqrt, scale=1.0 / D, bias=EPS)
        kb.recip(rs[:, 0:T], rs[:, 0:T])
        for c in range(8):
            t_ = ft()
            kb.tt(t_[:, 0:T], xT[:, c, :], rs[:, 0:T], ALU.mult)
            if out_bf is not None:
                kb.act(out_bf[:, c, :], t_[:, 0:T], AF.Identity, scale=a_v[:, c:c + 1], bias=b_v[:, c:c + 1])
            else:
                kb.act(out_f32[:, c, :], t_[:, 0:T], AF.Identity, scale=a_v[:, c:c + 1])

    def fm_proj(wv, col0, rhs_buf, KC, evac):
        pb = bank()
        for k in range(KC):
            kb.mm(pb[:, 0:T], wv[:, k, col0:col0 + 128], rhs_buf[:, k, :], start=(k == 0), stop=(k == KC - 1))
        evac(pb[:, 0:T])

    def tm_proj(wv, col0, ncols, sub, evac):
        pb = bank()
        for k in range(8):
            kb.mm(pb[:, 0:ncols], hT[:, k, sub * 128:(sub + 1) * 128], wv[:, k, col0:col0 + ncols],
                  start=(k == 0), stop=(k == 7))
        evac(pb[:, 0:ncols])

    def slot_alloc(slot, n=7):
        lst = ftmp[slot * n:(slot + 1) * n]
        stt_ = {"i": 0}

        def al():
            b = lst[stt_["i"] % n]; stt_["i"] += 1
            return b
        return al

    def run_slots(factories, nslots=2, offset=1):
        pending = list(factories)[::-1]
        slots = [None] * nslots
        steps = [0] * nslots
        while pending or any(g is not None for g in slots):
            for i in range(nslots):
                if slots[i] is None:
                    if pending and all(slots[j] is None or steps[j] >= offset for j in range(nslots) if j != i):
                        slots[i] = pending.pop()(i)
                        steps[i] = 0
                    else:
                        continue
                try:
                    next(slots[i])
                    steps[i] += 1
                except StopIteration:
                    slots[i] = None

    def conv(ps_v, hist_v, wv, ntap, bias_v=None, ft=None):
        if ft is None:
            ft = ft_global
        H = ntap - 1
        buf = ft()
        kb.copy(buf[:, H:H + T], ps_v, eng="act")
        kb.copy(buf[:, 0:H], hist_v, eng="pool")
        acc = ft()
        if bias_v is not None:
            kb.ts(acc[:, 0:T], buf[:, 0:T], wv[:, 0:1], ALU.mult, bias_v, ALU.add)
        else:
            kb.ts(acc[:, 0:T], buf[:, 0:T], wv[:, 0:1], ALU.mult)
        for k in range(1, ntap):
            kb.stt(acc[:, 0:T], buf[:, k:k + T], wv[:, k:k + 1], acc[:, 0:T], ALU.mult, ALU.add)
        kb.copy(hist_v, buf[:, T:T + H], eng="pool")
        return acc

    def norm_gate_T(o_v, width, gate_v, dst_list, junk=None, og=None):
        junk = junk or ot()
        ss = sm_()
        kb.act(junk[:, 0:width], o_v, AF.Square, accum=ss[:, 0:1])
        kb.act(ss[:, 1:2], ss[:, 0:1], AF.Sqrt, scale=1.0 / width, bias=EPS)
        kb.recip(ss[:, 2:3], ss[:, 1:2])
        og = og or ot()
        if gate_v is not None:
            kb.stt(og[:, 0:width], o_v, ss[:, 2:3], gate_v, ALU.mult, ALU.mult)
        else:
            kb.ts(og[:, 0:width], o_v, ss[:, 2:3], ALU.mult)
        for i, dst in enumerate(dst_list):
            pb = bank()
            kb.tr(pb[:, 0:128], og[:, i * 128:(i + 1) * 128], IDENT)
            kb.copy(dst, pb[:, 0:128], eng="act")

    def layer_body(l, seq, first_tile):
        L = LP[l]
        st = ST[l]
        mod = L["mod"]
        W = win_s[l]
        if first_tile:
            for k_ in ("gS", "gSb", "hS", "hSb", "sS", "sSb", "gh", "sh", "fh"):
                kb.memset(st[k_].v, 0.0)
        rms_mod(L["a1"][:, :, seq], mod[:, 0:8, seq], out_bf=hT)

        if stop < 3:
            return
        gda = [None] * NS
        dtv = [None] * NS
        beta = [None] * NS
        nbeta = [None] * NS
        for sub in range(NS):
            pb = bank()
            for k in range(8):
                kb.mm(pb[:, 0:16], hT[:, k, sub * 128:(sub + 1) * 128], L["wsm"][:, k, :], start=(k == 0), stop=(k == 7))
            t1 = sm_()
            kb.tt(t1[:, 0:12], pb[:, 0:12], L["dtb"].v, ALU.add)
            kb.act(t1[:, 0:12], t1[:, 0:12], AF.Exp)
            spl = persist[sub]["spl"]
            kb.act(spl[:, 0:12], t1[:, 0:12], AF.Ln, bias=1.0)
            g_ = persist[sub]["g"]
            kb.tt(g_[:, 0:12], spl[:, 0:12], L["nea"].v, ALU.mult)
            bt = persist[sub]["bt"]
            kb.act(bt[:, 0:4], pb[:, 12:16], AF.Sigmoid)
            kb.ts(bt[:, 4:8], bt[:, 0:4], -1.0, ALU.mult)
            gda[sub], dtv[sub], beta[sub], nbeta[sub] = g_, spl, bt[:, 0:4], bt[:, 4:8]

        expGA = [None] * NS
        eend = [None] * NS
        gend = [None] * NS
        for sub in range(NS):
            g_ = gda[sub]
            pb = bank()
            kb.mm(pb[:, 0:12], M_BTRI64, g_[:, 0:12])
            kb.mm(pb[:, 16:28], M_BUP64, g_[:, 0:12])
            kb.mm(pb[:, 32:44], M_IND64[0], g_[:, 0:12])
            kb.mm(pb[:, 48:60], M_IND64[1], g_[:, 0:12])
            e1 = persist[sub]["e1"]
            kb.act(e1[:, 0:12], pb[:, 0:12], AF.Exp)
            e2 = sm_()
            kb.act(e2[:, 0:12], pb[:, 16:28], AF.Exp)
            e2m = persist[sub]["e2m"]
            kb.ts(e2m[:, 0:12], e2[:, 0:12], INDC[:, 4:5], ALU.mult)
            kb.ts(e2m[:, 12:24], e2[:, 0:12], INDC[:, 5:6], ALU.mult)
            e3 = persist[sub]["e3"]
            kb.act(e3[:, 0:12], pb[:, 32:44], AF.Exp)
            kb.act(e3[:, 12:24], pb[:, 48:60], AF.Exp)
            expGA[sub], eend[sub], gend[sub] = e1, e2m, e3

        def decay_T(sub, h0):
            rh = big4[0]
            kb.tt(rh.v, M_TRIALL.un(1).bc([128, 4, 128]), gda[sub][:, h0:h0 + 4].un(2).bc([128, 4, 128]), ALU.mult)
            pb = bank()
            kb.mm(pb[:, 0:512], M_UPALL, rh.v.re("p h l -> p (h l)"))
            dec = big4[1]
            kb.act(dec.v.re("p h l -> p (h l)"), pb[:, 0:512], AF.Exp)
            return dec

        if stop < 4:
            return
        wq = wload(W[:, WIN_IDX[O_Q], :], 8, 512)
        wk = wload(W[:, WIN_IDX[O_K], :], 8, 512)
        wv_ = wload(W[:, WIN_IDX[O_V], :], 8, 512)
        qT, kT = abf[0], abf[1]
        vtok, kg, ke0, ke1 = tokbf[0], tokbf[1], tokbf[2], tokbf[3]

        def gdn_chunk(fc):
            def gen(slot):
                al = slot_alloc(slot)
                which, h = fc // 4, fc % 4
                wv = (wq, wk, wv_)[which]
                pb = bank()
                for k in range(8):
                    kb.mm(pb[:, 0:T], wv[:, k, h * 128:(h + 1) * 128], hT[:, k, :], start=(k == 0), stop=(k == 7))
                yield
                acc = conv(pb[:, 0:T], st["gh"][:, fc, :], L["gcw"][:, fc, :], 4, ft=al)
                s_ = al()
                kb.act(s_[:, 0:T], acc[:, 0:T], AF.Silu)
                if which < 2:
                    s2 = al()
                    kb.act(s2[:, 0:T], s_[:, 0:T], AF.Square)
                    s2b = abf[2][:, slot, :]
                    kb.copy(s2b, s2[:, 0:T], eng="pool")
                    pb = bank()
                    kb.mm(pb[:, 0:T], ones_bf.v, s2b)
                    yield
                    rs = al()
                    if which == 0:
                        kb.act(rs[:, 0:T], pb[:, 0:T], AF.Sqrt, scale=128.0, bias=128.0 * EPS)
                    else:
                        kb.act(rs[:, 0:T], pb[:, 0:T], AF.Sqrt, bias=EPS)
                    kb.recip(rs[:, 0:T], rs[:, 0:T])
                    if which == 0:
                        kb.tt(qT[:, h, :], s_[:, 0:T], rs[:, 0:T], ALU.mult)
                        return
                    knf = al()
                    kb.tt(knf[:, 0:T], s_[:, 0:T], rs[:, 0:T], ALU.mult)
                    kb.copy(kT[:, h, :], knf[:, 0:T], eng="act")
                    pbs = []
                    for sub in range(NS):
                        pb2 = bank()
                        kb.tr(pb2[:, 0:128], knf[:, sub * 128:(sub + 1) * 128], IDENT)
                        pbs.append(pb2)
                    yield
                    for sub in range(NS):
                        pb2 = pbs[sub]
                        kb.act(kg[sub][:, h * 128:(h + 1) * 128], pb2[:, 0:128], AF.Identity, scale=expGA[sub][:, h:h + 1])
                        kb.act(ke0[sub][:, h * 128:(h + 1) * 128], pb2[:, 0:128], AF.Identity, scale=eend[sub][:, h:h + 1])
                        kb.act(ke1[sub][:, h * 128:(h + 1) * 128], pb2[:, 0:128], AF.Identity, scale=eend[sub][:, 12 + h:13 + h])
                else:
                    pbs = []
                    for sub in range(NS):
                        pb2 = bank()
                        kb.tr(pb2[:, 0:128], s_[:, sub * 128:(sub + 1) * 128], IDENT)
                        pbs.append(pb2)
                    yield
                    for sub in range(NS):
                        kb.copy(vtok[sub][:, h * 128:(h + 1) * 128], pbs[sub][:, 0:128], eng="act")
            return gen

        order = [0, 4, 1, 5, 2, 6, 3, 7, 8, 9, 10, 11]
        run_slots([gdn_chunk(fc) for fc in order], nslots=2, offset=1)
        if stop < 4.1:
            return
        wz = wload(W[:, WIN_IDX[O_Z], :], 8, 512)
        siluz = tokf[0]
        for sub in range(NS):
            tm_proj(wz, 0, 512, sub, lambda ps_v, sub=sub: kb.act(siluz[sub].v, ps_v, AF.Silu))

        if stop < 4.2:
            return

        def gdn_head(sub, h, decm, decs, TP):
            sl = slice(sub * 128, (sub + 1) * 128)
            hs = slice(h * 128, (h + 1) * 128)
            PA, PTA, PB, PTB, X, XT = TP["sq"]
            Xb, wT, attnT, vn = TP["bf"]
            pk = bank()
            kb.mm(pk[:, 0:128], kT[:, h, sl], kT[:, h, sl])
            yield
            kb.stt(PA.v, pk[:, 0:128], nbeta[sub][:, h:h + 1], decs[:, h, :], ALU.mult, ALU.mult)
            pt = bank()
            kb.tr(pt[:, 0:128], PA.v, IDENT)
            yield
            kb.copy(PTA.v, pt[:, 0:128], eng="act")
            kb.tt(X.v, PA.v, IDENT, ALU.add)
            kb.tt(XT.v, PTA.v, IDENT, ALU.add, eng="pool")
            Pm, PT, Pn, PTn = PA, PTA, PB, PTB
            for lev in range(5):
                last = (lev == 4)
                p2 = bank()
                kb.mm(p2[:, 0:128], PT.v, Pm.v)
                if not last:
                    kb.mm(p2[:, 128:256], Pm.v, PT.v)
                yield
                kb.copy(Pn.v, p2[:, 0:128], eng="act")
                if not last:
                    kb.copy(PTn.v, p2[:, 128:256], eng="dve")
                px = bank()
                kb.mm(px[:, 0:128], XT.v, Pn.v)
                if not last:
                    kb.mm(px[:, 128:256], Pn.v, XT.v)
                yield
                if last:
                    kb.tt(Xb.v, px[:, 0:128], X.v, ALU.add)
                else:
                    kb.tt(X.v, px[:, 0:128], X.v, ALU.add)
                    kb.tt(XT.v, px[:, 128:256], XT.v, ALU.add)
                    Pm, PT, Pn, PTn = Pn, PTn, Pm, PT
            if stop < 4.4:
                return
            bu, oa, o_ = PA, PTA, PB
            pu = bank()
            kb.mm(pu[:, 0:128], Xb.v, vtok[sub][:, hs])
            kb.mm(pu[:, 128:256], kg[sub][:, hs], Xb.v)
            kb.mm(pu[:, 256:384], kT[:, h, sl], qT[:, h, sl])
            yield
            kb.act(bu.v, pu[:, 0:128], AF.Identity, scale=beta[sub][:, h:h + 1])
            kb.copy(wT.v, pu[:, 128:256], eng="act")
            kb.tt(attnT.v, pu[:, 256:384], decm[:, h, :], ALU.mult)
            for c in range(2):
                rows = slice(c * 64, (c + 1) * 64)
                p1 = bank()
                kb.mm(p1[:, 0:128], wT.v, st["gSb"][:, h, :])
                kb.mm(p1[:, 128:256], qT[:, h, sl], st["gSb"][:, h, :])
                yield
                kb.stt(vn[rows, :], p1[rows, 0:128], nbeta[sub][rows, h:h + 1], bu[rows, :], ALU.mult, ALU.add)
                kb.act(oa[rows, :], p1[rows, 128:256], AF.Identity, scale=expGA[sub][rows, h:h + 1])
                pbq = bank()
                kb.mm(pbq[:, 0:128], attnT.v, vn.v)
                ke = (ke0, ke1)[c]
                kb.mm(pbq[:, 128:256], ke[sub][:, hs], vn.v)
                yield
                kb.tt(o_[rows, :], oa[rows, :], pbq[rows, 0:128], ALU.add)
                kb.stt(st["gS"][:, h, :], st["gS"][:, h, :], gend[sub][:, c * 12 + h:c * 12 + h + 1], pbq[:, 128:256],
                       ALU.mult, ALU.add)
                kb.copy(st["gSb"][:, h, :], st["gS"][:, h, :], eng="act")
            norm_gate_T(o_.v, 128, siluz[sub][:, hs], [obr[0][:, h, sl]])

        for sub in range(NS):
            dec = decay_T(sub, 0)
            decm = big4[2]
            kb.tt(decm.v, dec.v, M_INCL64.un(1).bc([128, 4, 128]), ALU.mult)
            decs = big4[3]
            kb.tt(decs.v, decm.v, IDENT.un(1).bc([128, 4, 128]), ALU.subtract)
            if stop < 4.3:
                continue
            run_threads([gdn_head(sub, h, decm, decs, gdn_tp[h]) for h in range(4)])

        if stop < 5:
            return
        whi = wload(W[:, WIN_IDX[O_HI], :], 8, 512)
        vh = tokbf[0]
        for sub in range(NS):
            tm_proj(whi, 0, 512, sub, lambda ps_v, sub=sub: kb.copy(vh[sub].v, ps_v, eng="act"))
        whg = wload(W[:, WIN_IDX[O_HG], :], 8, 512)
        silug = tokf[0]
        for sub in range(NS):
            tm_proj(whg, 0, 512, sub, lambda ps_v, sub=sub: kb.act(silug[sub].v, ps_v, AF.Silu))
        whq = wload(W[:, WIN_IDX[O_HQ], :], 8, 512)
        whf = wload(W[:, WIN_IDX[O_HF], :], 8, 512)
        NCH = T // 32

        def hgrn_head(h):
            def gen(slot):
                al = slot_alloc(slot)
                hs = slice(h * 128, (h + 1) * 128)
                q_, f_, t3, G, t5 = al(), al(), al(), al(), al()
                pq = bank()
                for k in range(8):
                    kb.mm(pq[:, 0:T], whq[:, k, hs], hT[:, k, :], start=(k == 0), stop=(k == 7))
                pf = bank()
                for k in range(8):
                    kb.mm(pf[:, 0:T], whf[:, k, hs], hT[:, k, :], start=(k == 0), stop=(k == 7))
                yield
                kb.act(q_[:, 0:T], pq[:, 0:T], AF.Silu)
                kb.act(f_[:, 0:T], pf[:, 0:T], AF.Sigmoid)
                kb.ts(f_[:, 0:T], f_[:, 0:T], L["oml"][:, h:h + 1], ALU.mult, L["lb"][:, h:h + 1], ALU.add)
                kb.act(t3[:, 0:T], f_[:, 0:T], AF.Ln)
                kk = f_
                kb.ts(kk[:, 0:T], f_[:, 0:T], -1.0, ALU.mult, 1.0, ALU.add)
                kb.scan(G[:, 0:T], resetm.v, t3[:, 0:T], 0.0, ALU.mult, ALU.add)
                yield
                G3 = G[:, 0:T].re("p (c r) -> p c r", r=32)
                Dm = t3
                kb.tt(Dm[:, 0:T].re("p (c r) -> p c r", r=32), G3, G3[:, :, 15:16].bc([128, NCH, 32]), ALU.subtract)
                kb.act(t5[:, 0:T], Dm[:, 0:T], AF.Exp)
                qt, kt = abf[0], abf[1]
                kb.tt(qt[:, slot, :], q_[:, 0:T], t5[:, 0:T], ALU.mult)
                yield
                kb.act(t5[:, 0:T], Dm[:, 0:T], AF.Exp, scale=-1.0)
                kb.tt(kt[:, slot, :], kk[:, 0:T], t5[:, 0:T], ALU.mult)
                yield
                kb.act(t5[:, 0:T], G[:, 0:T], AF.Exp)
                qz = qgz[slot]
                for sub in range(NS):
                    kb.tt(qz[sub].v.re("p (c r) -> p c r", r=160)[:, :, 0:32],
                          q_[:, sub * 128:(sub + 1) * 128].re("p (c r) -> p c r", r=32),
                          t5[:, sub * 128:(sub + 1) * 128].re("p (c r) -> p c r", r=32), ALU.mult)
                yield
                DL = t3
                kb.tt(DL[:, 0:T].re("p (c r) -> p c r", r=32), G3[:, :, 31:32].bc([128, NCH, 32]), G3, ALU.subtract)
                kb.act(t5[:, 0:T], DL[:, 0:T], AF.Exp)
                kend = t3
                kb.tt(kend[:, 0:T], kk[:, 0:T], t5[:, 0:T], ALU.mult)
                ge = gepool[slot]
                kb.act(ge[:, 0:NCH], G3[:, :, 31], AF.Exp)
                kz = kendz[slot]
                for sub in range(NS):
                    sl = slice(sub * 128, (sub + 1) * 128)
                    pk = bank()
                    kb.tr(pk[:, 0:128], kend[:, sl], IDENT)
                    psc = bank()
                    kb.mm(psc[:, 0:128], kt[:, slot, sl], qt[:, slot, sl])
                    yield
                    kb.tt(kz[sub].v, pk[:, 0:128].un(1).bc([128, 4, 128]), INDC[:, 0:4].un(2).bc([128, 4, 128]), ALU.mult)
                    attnT = sqb[slot]
                    kb.tt(attnT.v, psc[:, 0:128], M_INCL32, ALU.mult)
                    po = banks[6 + slot]
                    kb.mm(po[:, 0:128], attnT.v, vh[sub][:, hs], start=True, stop=False)
                    for c in range(4):
                        kb.mm(po[:, 0:128], qz[sub][:, c * 128:(c + 1) * 128], st["hSb"][:, h, :], start=False, stop=(c == 3))
                        pss = bank()
                        kb.mm(pss[:, 0:128], kz[sub][:, c, :], vh[sub][:, hs])
                        yield
                        kb.stt(st["hS"][:, h, :], st["hS"][:, h, :], ge[:, sub * 4 + c:sub * 4 + c + 1], pss[:, 0:128],
                               ALU.mult, ALU.add)
                        kb.copy(st["hSb"][:, h, :], st["hS"][:, h, :], eng="act")
                    norm_gate_T(po[:, 0:128], 128, silug[sub][:, hs], [obr[1][:, h, sl]])
            return gen

        run_slots([hgrn_head(h) for h in range(4)], nslots=2, offset=5)

        if stop < 6:
            return
        wsz = wload(W[:, WIN_IDX[O_SZ], :], 8, 512)
        siluzs = tokf[0]
        for sub in range(NS):
            tm_proj(wsz, 0, 512, sub, lambda ps_v, sub=sub: kb.act(siluzs[sub].v, ps_v, AF.Silu))
        wsx = wload(W[:, WIN_IDX[O_SX], :], 8, 512)
        wsbc = wload(W[:, WIN_IDX[O_SB], :], 8, 512)
        xtok, xdt, btok = tokf[1], tokbf[0], tokbf[1]
        BT, CT = abf[0], abf[1]

        def ssd_chunk(fc):
            def gen(slot):
                al = slot_alloc(slot)
                wv = wsx if fc < 4 else wsbc
                pb = bank()
                for k in range(8):
                    kb.mm(pb[:, 0:T], wv[:, k, (fc % 4) * 128:(fc % 4 + 1) * 128], hT[:, k, :], start=(k == 0), stop=(k == 7))
                yield
                acc = conv(pb[:, 0:T], st["sh"][:, fc, :], L["scw"][:, fc, :], 4, bias_v=L["scb"][:, fc:fc + 1], ft=al)
                s_ = al()
                kb.act(s_[:, 0:T], acc[:, 0:T], AF.Silu)
                if fc >= 6:
                    kb.copy(CT[:, fc - 6, :], s_[:, 0:T], eng="pool")
                    return
                if fc >= 4:
                    kb.copy(BT[:, fc - 4, :], s_[:, 0:T], eng="pool")
                pbs = []
                for sub in range(NS):
                    pb2 = bank()
                    kb.tr(pb2[:, 0:128], s_[:, sub * 128:(sub + 1) * 128], IDENT)
                    pbs.append(pb2)
                yield
                for sub in range(NS):
                    if fc < 4:
                        kb.copy(xtok[sub][:, fc * 128:(fc + 1) * 128], pbs[sub][:, 0:128], eng="act")
                    else:
                        g = fc - 4
                        kb.copy(btok[sub][:, g * 128:(g + 1) * 128], pbs[sub][:, 0:128], eng="act")
            return gen

        run_slots([ssd_chunk(fc) for fc in range(8)], nslots=2, offset=1)
        for sub in range(NS):
            kb.tt(xdt[sub].v.re("p (h d) -> p h d", d=64), xtok[sub].v.re("p (h d) -> p h d", d=64),
                  dtv[sub][:, 4:12].un(2).bc([128, 8, 64]), ALU.mult)

        def ssd_group(sub, g):
            def gen(slot):
                sl = slice(sub * 128, (sub + 1) * 128)
                gs = slice(g * 256, (g + 1) * 256)
                h0 = 4 + 4 * g
                rh, dec = big4[2 * slot], big4[2 * slot + 1]
                t1, y, t3 = otok[3 * slot], otok[3 * slot + 1], otok[3 * slot + 2]
                kb.tt(rh.v, M_TRIALL.un(1).bc([128, 4, 128]), gda[sub][:, h0:h0 + 4].un(2).bc([128, 4, 128]), ALU.mult)
                pd = bank()
                kb.mm(pd[:, 0:512], M_UPALL, rh.v.re("p h l -> p (h l)"))
                pcb = bank()
                kb.mm(pcb[:, 0:128], BT[:, g, sl], CT[:, g, sl])
                yield
                kb.act(dec.v.re("p h l -> p (h l)"), pd[:, 0:512], AF.Exp)
                cbm = sq128[slot]
                kb.tt(cbm.v, pcb[:, 0:128], M_INCL64, ALU.mult)
                at = big4b[slot]
                kb.tt(at.v, dec.v, cbm.v.un(1).bc([128, 4, 128]), ALU.mult)
                py = banks[6 + slot]
                for hh in range(4):
                    kb.mm(py[:, hh * 64:(hh + 1) * 64], at[:, hh, :], xdt[sub][:, (4 * g + hh) * 64:(4 * g + hh + 1) * 64])
                for c in range(2):
                    rows = slice(c * 64, (c + 1) * 64)
                    poff = bank()
                    kb.mm(poff[:, 0:256], CT[:, g, sl], st["sSb"][:, g, :])
                    xw = tokbf[2 + slot][sub]
                    kb.tt(xw[:, 0:256].re("p (h d) -> p h d", d=64), xdt[sub][:, gs].re("p (h d) -> p h d", d=64),
                          eend[sub][:, c * 12 + h0:c * 12 + h0 + 4].un(2).bc([128, 4, 64]), ALU.mult)
                    pst = bank()
                    kb.mm(pst[:, 0:256], btok[sub][:, g * 128:(g + 1) * 128], xw[:, 0:256])
                    yield
                    kb.tt(t1[rows, :].re("p (h d) -> p h d", d=64), poff[rows, 0:256].re("p (h d) -> p h d", d=64),
                          expGA[sub][rows, h0:h0 + 4].un(2).bc([64, 4, 64]), ALU.mult)
                    kb.tt(st["sS"][:, g, :].re("p (h d) -> p h d", d=64), st["sS"][:, g, :].re("p (h d) -> p h d", d=64),
                          gend[sub][:, c * 12 + h0:c * 12 + h0 + 4].un(2).bc([128, 4, 64]), ALU.mult, eng="pool")
                    kb.tt(st["sS"][:, g, :], st["sS"][:, g, :], pst[:, 0:256], ALU.add)
                    kb.copy(st["sSb"][:, g, :], st["sS"][:, g, :], eng="act")
                kb.tt(y.v, py[:, 0:256], t1.v, ALU.add)
                kb.tt(t3.v, xtok[sub][:, gs], L["dfull"].v.re("p h d -> p (h d)")[:, gs], ALU.mult, eng="pool")
                kb.tt(y.v, y.v, t3.v, ALU.add)
                kb.tt(y.v, y.v, siluzs[sub][:, gs], ALU.mult)
                norm_gate_T(y.v, 256, None, [obr[2][:, 2 * g, sl], obr[2][:, 2 * g + 1, sl]], junk=t3, og=t1)
            return gen

        run_slots([ssd_group(sub, g) for sub in range(NS) for g in range(2)], nslots=2, offset=1)

        if stop < 7:
            return
        if dbg and l == 0 and first_tile and seq == 0:
            for i in range(3):
                kb.copy(macc[:, 0:4, :], obr[i].v)
                tickets.append(dbgout("obr%d" % i, macc[:, 0:4, :], [128, 4, T]))

        for br in range(3):
            wb = wload(wbr_s[l][br].v.re("p k n -> p (k n)"), 4, 1024)
            for jb in range(2):
                wg = wload(W[:, WIN_IDX[O_GATE + br * 1024 + jb * 512], :], 8, 512)
                for jj in range(4):
                    j = jb * 4 + jj
                    gt = ft()
                    fm_proj(wg, jj * 128, hT, 8, lambda ps_v, gt=gt: kb.act(gt[:, 0:T], ps_v, AF.Sigmoid))
                    if br == 0:
                        fm_proj(wb, j * 128, obr[br], 4,
                                lambda ps_v, gt=gt, j=j: kb.tt(macc[:, j, :], ps_v, gt[:, 0:T], ALU.mult))
                    else:
                        tm = ft()
                        fm_proj(wb, j * 128, obr[br], 4,
                                lambda ps_v, gt=gt, tm=tm: kb.tt(tm[:, 0:T], ps_v, gt[:, 0:T], ALU.mult))
                        kb.tt(macc[:, j, :], macc[:, j, :], tm[:, 0:T], ALU.add, eng="pool")
        kb.copy(mergedT.v, macc.v, eng="act")
        for jb in range(2):
            wo = wload(wout_s[l][:, jb, :], 8, 512)
            for jj in range(4):
                j = jb * 4 + jj
                fm_proj(wo, jj * 128, mergedT, 8,
                        lambda ps_v, j=j: kb.stt(xT[:, j, :], ps_v, mod[:, 16 + j, seq:seq + 1], xT[:, j, :], ALU.mult, ALU.add))
        if dbg and l == 0 and first_tile and seq == 0:
            tickets.append(dbgout("xmix", xT.v, [128, 8, T]))

        if stop < 8:
            return
        rms_mod(L["a2"][:, :, seq], mod[:, 24:32, seq], out_bf=hT)
        for i in range(11):
            b = wbufs[wst["i"] % NW]; wst["i"] += 1
            wv = b.v.re("p (k n) -> p k n", n=512)
            kb.dma(b.v, wup_s[l][:, i, :])
            for jj in range(2):
                fcg = i * 2 + jj
                fcv = 22 + fcg
                res = {}

                def evg(ps_v, res=res, fcg=fcg):
                    res["g"] = conv(ps_v, st["fh"][:, fcg, :], L["fcw"][:, fcg, :], 3, bias_v=L["fcb"][:, fcg:fcg + 1])

                def evv(ps_v, res=res, fcv=fcv):
                    res["v"] = conv(ps_v, st["fh"][:, fcv, :], L["fcw"][:, fcv, :], 3, bias_v=L["fcb"][:, fcv:fcv + 1])
                fm_proj(wv, jj * 128, hT, 8, evg)
                fm_proj(wv, 256 + jj * 128, hT, 8, evv)
                sg = ft()
                kb.act(sg[:, 0:T], res["g"][:, 0:T], AF.Silu)
                kb.tt(hidT[:, fcg, :], sg[:, 0:T], res["v"][:, 0:T], ALU.mult)
        for j in range(8):
            b = wbufs[wst["i"] % NW]; wst["i"] += 1
            wv = b.v[:, 0:22 * 128].re("p (k n) -> p k n", n=128)
            kb.dma(b.v[:, 0:22 * 128], wdn_s[l][:, j, :])
            fm_proj(wv, 0, hidT, 22,
                    lambda ps_v, j=j: kb.stt(xT[:, j, :], ps_v, mod[:, 40 + j, seq:seq + 1], xT[:, j, :], ALU.mult, ALU.add))
        if dbg and l == 0 and first_tile and seq == 0:
            tickets.append(dbgout("xffn", xT.v, [128, 8, T]))

    for seq in range(NSEQ):
        for ti in range(NT):
            r0 = seq * S + ti * T
            kb.dma(xio.v, V((x_d,), x_d.ap[r0:r0 + T, :].rearrange("(s p) d -> p s d", p=128)), eng="pool")
            for sub in range(NS):
                for c in range(8):
                    pb = bank()
                    kb.tr(pb[:, 0:128], xio[:, sub, c * 128:(c + 1) * 128], IDENT)
                    kb.copy(xT[:, c, sub * 128:(sub + 1) * 128], pb[:, 0:128], eng=("act" if c % 2 else "dve"))
            for l in range(DEPTH if stop >= 2 else 0):
                layer_body(l, seq, ti == 0)
            rms_mod(fnw.v, None, out_f32=macc)
            for sub in range(NS):
                for c in range(8):
                    pb = bank()
                    kb.tr(pb[:, 0:128], macc[:, c, sub * 128:(sub + 1) * 128], IDENT)
                    kb.copy(xio[:, sub, c * 128:(c + 1) * 128], pb[:, 0:128], eng=("act" if c % 2 else "dve"))
            tickets.append(kb.dma(V((out_d,), out_d.ap[r0:r0 + T, :].rearrange("(s p) d -> p s d", p=128)), xio.v, eng="pool"))

    tickets = [t for t in tickets if t is not None]
    kb.finish(tickets)
    return nc, kb, dbg_d


_CACHE = {}


def kernel(**inputs):
    NC = 8
    x = np.ascontiguousarray(inputs["x"], dtype=np.float32)
    B, S, _ = x.shape
    NSEQ = B // NC
    key = (S, NSEQ)
    if key not in _CACHE:
        _CACHE[key] = build(S, NSEQ)[0]
    nc = _CACHE[key]
    cmask = make_masks()
    in_maps = []
    for i in range(NC):
        m = {"x": x[i * NSEQ:(i + 1) * NSEQ].reshape(NSEQ * S, D),
             "c": np.ascontiguousarray(inputs["c"][i * NSEQ:(i + 1) * NSEQ], dtype=np.float32),
             "cmask": cmask}
        for n in PNAMES:
            m[n] = np.ascontiguousarray(inputs[n], dtype=np.float32)
        in_maps.append(m)
    res = run_bass_kernel_spmd(nc, in_maps, core_ids=list(range(NC)))
    out = np.concatenate([r["out"].reshape(NSEQ, S, D) for r in res.results], axis=0)
    return out.astype(np.float32)
```

```python
import numpy as np
from contextlib import ExitStack
import concourse.bass as bass
import concourse.mybir as mybir
from concourse.bass_utils import run_bass_kernel_spmd

F32 = mybir.dt.float32
BF16 = mybir.dt.bfloat16
AF = mybir.ActivationFunctionType
ALU = mybir.AluOpType

D = 1024
NIN = 8720
FH = 2816
F2 = 5632
EPS = 1e-6
O_Q, O_K, O_V = 0, 512, 1024
O_A, O_B, O_Z = 1536, 1540, 1544
O_HQ, O_HF, O_HI, O_HG = 2056, 2568, 3080, 3592
O_SZ, O_SX, O_SB, O_SC, O_DT, O_GATE = 4104, 4616, 5128, 5384, 5640, 5648


class V:
    __slots__ = ("bufs", "ap")

    def __init__(self, bufs, ap):
        self.bufs = bufs
        self.ap = ap

    def __getitem__(self, idx):
        return V(self.bufs, self.ap[idx])

    def re(self, s, **kw):
        return V(self.bufs, self.ap.rearrange(s, **kw))

    def bc(self, shape):
        return V(self.bufs, self.ap.to_broadcast(list(shape)))

    def un(self, axis):
        return V(self.bufs, self.ap.unsqueeze(axis))


class Buf:
    def __init__(self, name, t, ap):
        self.name = name
        self.t = t
        self.ap = ap
        self.lw = None
        self.rd = []
        self.psum = False

    def __getitem__(self, idx):
        return V((self,), self.ap[idx])

    @property
    def v(self):
        return V((self,), self.ap)


class KB:
    ENG = ("pe", "act", "dve", "pool", "sp")
    EP = 30000
    NEP = 10

    def __init__(self, nc, n_dma_sems=24, same_engine_sync=True):
        self.nc = nc
        self.st = ExitStack()
        self.q = {e: [] for e in self.ENG}
        self.sems = {}
        self.cnt = {}
        for e in ("pe", "act", "dve", "pool"):
            for ep in range(self.NEP):
                self.sems["%s#%d" % (e, ep)] = self.st.enter_context(nc.semaphore("c_%s_%d" % (e, ep)))
            self.cnt[e] = 0
        self.dsem = []
        for i in range(n_dma_sems):
            k = "d%d" % i
            self.sems[k] = self.st.enter_context(nc.semaphore(k))
            self.cnt[k] = 0
            self.dsem.append(k)
        self.dnext = 0
        self.dnext2 = 0
        self.waited = {e: {} for e in self.ENG}
        self.same = same_engine_sync
        self.nbuf = 0
        self.ninst = 0

    def sb(self, shape, dtype=F32, name=None):
        self.nbuf += 1
        name = name or ("b%d" % self.nbuf)
        t = self.st.enter_context(self.nc.sbuf_tensor(name, list(shape), dtype))
        return Buf(name, t, t[:])

    def ps(self, shape, dtype=F32, name=None):
        self.nbuf += 1
        name = name or ("p%d" % self.nbuf)
        t = self.st.enter_context(self.nc.psum_tensor(name, list(shape), dtype))
        b = Buf(name, t, t[:])
        b.psum = True
        return b

    def dram(self, name, shape, dtype, kind):
        t = self.nc.dram_tensor(name, list(shape), dtype, kind=kind)
        return Buf(name, t, t.ap())

    def _wait(self, eng, key, val):
        w = self.waited[eng]
        if w.get(key, 0) >= val:
            return
        w[key] = val
        sem = self.sems[key]
        self.q[eng].append(lambda e, sem=sem, val=val: e.wait_ge(sem, val))

    def _deps(self, eng, reads, writes):
        need = {}
        for v in reads:
            for b in v.bufs:
                if b.lw is not None:
                    k, val = b.lw
                    need[k] = max(need.get(k, 0), val)
        for v in writes:
            for b in v.bufs:
                if b.lw is not None:
                    k, val = b.lw
                    need[k] = max(need.get(k, 0), val)
                for (k, val) in b.rd:
                    need[k] = max(need.get(k, 0), val)
        for k, val in need.items():
            if k.split("#")[0] == eng and (eng == "pe" or not self.same):
                continue
            self._wait(eng, k, val)

    def _mark(self, ticket, reads, writes):
        for v in writes:
            for b in v.bufs:
                b.lw = ticket
                b.rd = []
        for v in reads:
            for b in v.bufs:
                if b.lw != ticket:
                    b.rd.append(ticket)
                    if len(b.rd) > 32:
                        m = {}
                        for (k, val) in b.rd:
                            m[k] = max(m.get(k, 0), val)
                        b.rd = list(m.items())

    def op(self, eng, fn, reads, writes):
        pr = [v for v in reads if any(b.psum for b in v.bufs)]
        if pr:
            writes = list(writes) + pr
        self._deps(eng, reads, writes)
        ep, within = divmod(self.cnt[eng], self.EP)
        self.cnt[eng] += 1
        self.ninst += 1
        key = "%s#%d" % (eng, ep)
        sem = self.sems[key]
        self.q[eng].append(lambda e, fn=fn, sem=sem: fn(e).then_inc(sem, 1))
        self._mark((key, within + 1), reads, writes)

    def dma(self, out, in_, eng="sp", **kw):
        if eng == "sp":
            k = self.dsem[self.dnext]
            self.dnext = (self.dnext + 1) % (len(self.dsem) - 8)
        else:
            k = self.dsem[len(self.dsem) - 8 + self.dnext2]
            self.dnext2 = (self.dnext2 + 1) % 8
        if self.cnt[k] > 0:
            self._wait(eng, k, self.cnt[k])
        self._deps(eng, [in_], [out])
        self.cnt[k] += 16
        self.ninst += 1
        sem = self.sems[k]
        self.q[eng].append(lambda e, o=out.ap, i=in_.ap, sem=sem, kw=kw: e.dma_start(out=o, in_=i, **kw).then_inc(sem, 16))
        t = (k, self.cnt[k])
        self._mark(t, [in_], [out])
        return t

    def mm(self, out, lhsT, rhs, start=True, stop=True):
        rd = [lhsT, rhs] + ([] if start else [out])
        self.op("pe", lambda e, o=out.ap, l=lhsT.ap, r=rhs.ap: e.matmul(o, l, r, start=start, stop=stop), rd, [out])

    def tr(self, out, in_, ident):
        self.op("pe", lambda e, o=out.ap, i=in_.ap, d=ident.ap: e.transpose(o, i, d), [in_, ident], [out])

    def act(self, out, in_, func, scale=None, bias=None, accum=None):
        rd = [in_]
        kw = {}
        if scale is not None:
            if isinstance(scale, V):
                rd.append(scale); kw["scale"] = scale.ap
            else:
                kw["scale"] = scale
        if bias is not None:
            if isinstance(bias, V):
                rd.append(bias); kw["bias"] = bias.ap
            else:
                kw["bias"] = bias
        wr = [out]
        if accum is not None:
            wr.append(accum); kw["accum_out"] = accum.ap
        self.op("act", lambda e, o=out.ap, i=in_.ap, kw=kw: e.activation(o, i, func, **kw), rd, wr)

    def tt(self, out, a, b, op, eng="dve"):
        self.op(eng, lambda e, o=out.ap, a_=a.ap, b_=b.ap: e.tensor_tensor(o, a_, b_, op), [a, b], [out])

    def ts(self, out, a, s1, op0, s2=None, op1=None, eng="dve"):
        rd = [a]
        s1a = s1.ap if isinstance(s1, V) else s1
        s2a = s2.ap if isinstance(s2, V) else s2
        if isinstance(s1, V): rd.append(s1)
        if isinstance(s2, V): rd.append(s2)
        if op1 is None:
            self.op(eng, lambda e, o=out.ap, a_=a.ap: e.tensor_scalar(o, a_, s1a, None, op0), rd, [out])
        else:
            self.op(eng, lambda e, o=out.ap, a_=a.ap: e.tensor_scalar(o, a_, s1a, s2a, op0, op1), rd, [out])

    def stt(self, out, a, s, b, op0, op1):
        rd = [a, b]
        sa = s.ap if isinstance(s, V) else s
        if isinstance(s, V): rd.append(s)
        self.op("dve", lambda e, o=out.ap, a_=a.ap, b_=b.ap: e.scalar_tensor_tensor(o, a_, sa, b_, op0, op1), rd, [out])

    def scan(self, out, d0, d1, init, op0, op1):
        self.op("dve", lambda e, o=out.ap, a_=d0.ap, b_=d1.ap: e.tensor_tensor_scan(o, a_, b_, init, op0, op1), [d0, d1], [out])

    def copy(self, out, in_, eng="dve"):
        if eng == "act":
            self.act(out, in_, AF.Copy)
        else:
            self.op(eng, lambda e, o=out.ap, i=in_.ap: e.tensor_copy(o, i), [in_], [out])

    def memset(self, out, val, eng="pool"):
        self.op(eng, lambda e, o=out.ap: e.memset(o, val), [], [out])

    def recip(self, out, in_):
        self.op("dve", lambda e, o=out.ap, i=in_.ap: e.reciprocal(o, i), [in_], [out])

    def finish(self, out_tickets):
        for (k, val) in out_tickets:
            self._wait("sp", k, val)
        nc = self.nc
        with nc.Block() as block:
            @block.tensor
            def _(e):
                for f in self.q["pe"]: f(e)

            @block.scalar
            def _(e):
                for f in self.q["act"]: f(e)

            @block.vector
            def _(e):
                for f in self.q["dve"]: f(e)

            @block.gpsimd
            def _(e):
                for f in self.q["pool"]: f(e)

            @block.sync
            def _(e):
                for f in self.q["sp"]: f(e)
        self.st.close()


def make_masks():
    t = np.arange(128)
    r = t[:, None]
    c = t[None, :]
    same64 = (r // 64) == (c // 64)
    same32 = (r // 32) == (c // 32)
    m = np.zeros((128, 11, 128), np.float32)
    m[:, 0] = (r == c)
    m[:, 1] = 1.0
    m[:, 2] = (c >= r) & same64
    m[:, 3] = (r > c)
    m[:, 4] = (r <= c)
    m[:, 5] = (r <= c) & same64
    m[:, 6] = (r > c) & same64
    m[:, 7] = (r < 64) & (c >= 0)
    m[:, 8] = (r >= 64) & (c >= 0)
    m[:, 9] = (c >= r) & same32
    ind = np.zeros((128, 128), np.float32)
    for j in range(4):
        ind[:, j] = (t // 32 == j)
    ind[:, 4] = (t < 64)
    ind[:, 5] = (t >= 64)
    m[:, 10] = ind
    return m


PNAMES = ["w_ada", "b_ada", "norm1_w", "w_in", "gdn_conv_w", "gdn_a_log", "gdn_dt_bias", "gdn_norm_w",
          "hgrn_lb_param", "hgrn_norm_w", "ssd_conv_w", "ssd_conv_b", "ssd_a_log", "ssd_dt_bias", "ssd_d",
          "ssd_norm_w", "w_br_a", "w_br_b", "w_br_c", "w_out", "norm2_w", "ffn_w_up", "ffn_conv_w",
          "ffn_conv_b", "ffn_w_down", "final_norm_w"]
PSHAPES = {
    "w_ada": (D, 6 * D), "b_ada": (6 * D,), "norm1_w": (D,), "w_in": (D, NIN), "gdn_conv_w": (4, 1536),
    "gdn_a_log": (4,), "gdn_dt_bias": (4,), "gdn_norm_w": (128,), "hgrn_lb_param": (512,),
    "hgrn_norm_w": (128,), "ssd_conv_w": (4, 1024), "ssd_conv_b": (1024,), "ssd_a_log": (8,),
    "ssd_dt_bias": (8,), "ssd_d": (8,), "ssd_norm_w": (512,), "w_br_a": (512, D), "w_br_b": (512, D),
    "w_br_c": (512, D), "w_out": (D, D), "norm2_w": (D,), "ffn_w_up": (D, F2), "ffn_conv_w": (3, F2),
    "ffn_conv_b": (F2,), "ffn_w_down": (FH, D),
}


def build(S, NSEQ, DEPTH=2, T=256, dbg=False, stop=99):
    NS = T // 128
    NT = S // T
    assert S % T == 0
    nc = bass.Bass("TRN2", target_bir_lowering=False)
    kb = KB(nc)
    x_d = kb.dram("x", [NSEQ * S, D], F32, "ExternalInput")
    c_d = kb.dram("c", [NSEQ, D], F32, "ExternalInput")
    cm_d = kb.dram("cmask", [128, 11, 128], F32, "ExternalInput")
    P = {}
    for n in PNAMES:
        if n == "final_norm_w":
            P[n] = kb.dram(n, [D], F32, "ExternalInput")
        else:
            P[n] = kb.dram(n, [DEPTH] + list(PSHAPES[n]), F32, "ExternalInput")
    out_d = kb.dram("out", [NSEQ * S, D], F32, "ExternalOutput")
    dbg_d = {}

    def dbgout(name, view, shape):
        if not dbg:
            return None
        dbg_d[name] = kb.dram("dbg_" + name, list(shape), F32, "ExternalOutput")
        return kb.dma(dbg_d[name].v, view)

    WIN_BLK = [O_Q, O_K, O_V, O_Z, O_HQ, O_HF, O_HI, O_HG, O_SZ, O_SX, O_SB] + [O_GATE + 512 * j for j in range(6)]
    WIN_IDX = {c0: i for i, c0 in enumerate(WIN_BLK)}
    win_s = [kb.dram("win_s%d" % l, [128, 17, 4096], BF16, "Internal") for l in range(DEPTH)]
    wbr_s = [[kb.dram("wbr_s%d_%d" % (l, b), [128, 4, D], BF16, "Internal") for b in range(3)] for l in range(DEPTH)]
    wout_s = [kb.dram("wout_s%d" % l, [128, 2, 4096], BF16, "Internal") for l in range(DEPTH)]
    wup_s = [kb.dram("wup_s%d" % l, [128, 11, 4096], BF16, "Internal") for l in range(DEPTH)]
    wdn_s = [kb.dram("wdn_s%d" % l, [128, 8, 22 * 128], BF16, "Internal") for l in range(DEPTH)]

    tickets = []
    cm = kb.sb([128, 11, 128], F32, "cm")
    kb.dma(cm.v, cm_d.v)
    IDENT = cm[:, 0, :]
    ONES = cm[:, 1, :]
    M_INCL64 = cm[:, 2, :]
    M_UPALL = cm[:, 3, :]
    M_TRIALL = cm[:, 4, :]
    M_BTRI64 = cm[:, 5, :]
    M_BUP64 = cm[:, 6, :]
    M_IND64 = [cm[:, 7, :], cm[:, 8, :]]
    M_INCL32 = cm[:, 9, :]
    INDC = cm[:, 10, :]
    ones_bf = kb.sb([128, 128], BF16, "ones_bf")
    kb.memset(ones_bf.v, 1.0)
    resetm = kb.sb([128, T], F32, "resetm")
    kb.memset(resetm.v, 1.0)
    kb.memset(resetm.v.re("p (c r) -> p c r", r=32)[:, :, 0:1], 0.0)

    banks = [kb.ps([128, 512], F32, "bank%d" % i) for i in range(8)]
    bstate = {"i": 0}

    def bank():
        b = banks[bstate["i"]]
        bstate["i"] = (bstate["i"] + 1) % 6
        return b

    def pload(name, l, pattern, shape, **kw):
        b = kb.sb(shape, F32, "%s_%d" % (name, l))
        src = P[name].ap[l] if name != "final_norm_w" else P[name].ap
        C = shape[1]
        for c in range(C):
            if len(shape) == 3:
                kb.dma(b[:, c, :], V((P[name],), src[:, c * 128:(c + 1) * 128].rearrange("k p -> p k")),
                       allow_slow_non_contiguous=True)
            else:
                kb.dma(b[:, c:c + 1], V((P[name],), src[c * 128:(c + 1) * 128].rearrange("(p o) -> p o", o=1)),
                       allow_slow_non_contiguous=True)
        return b

    def pbc(name, l, n):
        b = kb.sb([128, n], F32, "%s_bc%d" % (name, l))
        kb.dma(b.v, V((P[name],), P[name].ap[l].partition_broadcast(128)))
        return b

    cT = kb.sb([128, 8, NSEQ], F32, "cT")
    for s_ in range(NSEQ):
        kb.dma(cT[:, :, s_], V((c_d,), c_d.ap[s_].rearrange("(k p) -> p k", p=128)), allow_slow_non_contiguous=True)
    cact = kb.sb([128, 8, NSEQ], F32, "cact")
    kb.act(cact.v, cT.v, AF.Silu)
    fnw = pload("final_norm_w", 0, "(c p) -> p c", [128, 8], p=128)

    LP = []
    stage_f = [kb.sb([128, 1024], F32, "stf%d" % i) for i in range(2)]
    stage_b = [kb.sb([128, 1024], BF16, "stb%d" % i) for i in range(2)]
    stg = {"i": 0}
    for l in range(DEPTH):
        L = {}
        L["n1"] = pload("norm1_w", l, "(c p) -> p c", [128, 8], p=128)
        L["n2"] = pload("norm2_w", l, "(c p) -> p c", [128, 8], p=128)
        L["bada"] = pload("b_ada", l, "(c p) -> p c", [128, 48], p=128)
        L["gcw"] = pload("gdn_conv_w", l, "k (c p) -> p c k", [128, 12, 4], p=128)
        L["scw"] = pload("ssd_conv_w", l, "k (c p) -> p c k", [128, 8, 4], p=128)
        L["scb"] = pload("ssd_conv_b", l, "(c p) -> p c", [128, 8], p=128)
        L["fcw"] = pload("ffn_conv_w", l, "k (c p) -> p c k", [128, 44, 3], p=128)
        L["fcb"] = pload("ffn_conv_b", l, "(c p) -> p c", [128, 44], p=128)
        L["gnw"] = pload("gdn_norm_w", l, "(c p) -> p c", [128, 1], p=128)
        L["hnw"] = pload("hgrn_norm_w", l, "(c p) -> p c", [128, 1], p=128)
        L["snw"] = pload("ssd_norm_w", l, "(c p) -> p c", [128, 4], p=128)
        alog = kb.sb([128, 12], F32, "alog%d" % l)
        kb.dma(alog[:, 0:4], V((P["gdn_a_log"],), P["gdn_a_log"].ap[l].partition_broadcast(128)))
        kb.dma(alog[:, 4:12], V((P["ssd_a_log"],), P["ssd_a_log"].ap[l].partition_broadcast(128)))
        dtb = kb.sb([128, 12], F32, "dtb%d" % l)
        kb.dma(dtb[:, 0:4], V((P["gdn_dt_bias"],), P["gdn_dt_bias"].ap[l].partition_broadcast(128)))
        kb.dma(dtb[:, 4:12], V((P["ssd_dt_bias"],), P["ssd_dt_bias"].ap[l].partition_broadcast(128)))
        nea = kb.sb([128, 12], F32, "nea%d" % l)
        kb.act(nea.v, alog.v, AF.Exp)
        kb.ts(nea.v, nea.v, -1.0, ALU.mult)
        L["nea"] = nea
        L["dtb"] = dtb
        dsk = pbc("ssd_d", l, 8)
        dfull = kb.sb([128, 8, 64], F32, "dfull%d" % l)
        kb.copy(dfull.v, dsk.v.un(2).bc([128, 8, 64]))
        L["dfull"] = dfull
        LP.append(L)

    lbp = [pload("hgrn_lb_param", l, "(h p) -> p h", [128, 4], p=128) for l in range(DEPTH)]
    persist = [{k_: kb.sb([128, 32], F32, "ps_%s_%d" % (k_, s_)) for k_ in ("g", "spl", "bt", "e1", "e2m", "e3")} for s_ in range(2)]
    gepool = [kb.sb([128, 32], F32, "gepool%d" % i) for i in range(2)]
    lbe = [kb.sb([128, 4], F32, "lbe%d" % l) for l in range(DEPTH)]
    for l in range(DEPTH):
        kb.act(lbe[l].v, lbp[l].v, AF.Exp)
    lsum = kb.sb([128, 4], F32, "lsum")
    kb.copy(lsum.v, lbe[0].v)
    for l in range(1, DEPTH):
        kb.tt(lsum.v, lsum.v, lbe[l].v, ALU.add)
    lrs = kb.sb([128, 4], F32, "lrs")
    kb.recip(lrs.v, lsum.v)
    cum = kb.sb([128, 4], F32, "lcum")
    kb.memset(cum.v, 0.0)
    for l in range(DEPTH):
        lb = kb.sb([128, 4], F32, "lb%d" % l)
        oml = kb.sb([128, 4], F32, "oml%d" % l)
        if l > 0:
            sm = kb.sb([128, 4], F32, "lsm%d" % l)
            kb.tt(sm.v, lbe[l].v, lrs.v, ALU.mult)
            kb.tt(cum.v, cum.v, sm.v, ALU.add)
        kb.copy(lb.v, cum.v)
        kb.ts(oml.v, lb.v, -1.0, ALU.mult, 1.0, ALU.add)
        LP[l]["lb"] = lb
        LP[l]["oml"] = oml

    for l in range(DEPTH):
        L = LP[l]
        mod = kb.sb([128, 48, NSEQ], F32, "mod%d" % l)
        for fc in range(48):
            sf = stage_f[stg["i"] % 2]; stg["i"] += 1
            sfv = sf.v.re("p (k n) -> p k n", n=128)
            kb.dma(sfv, V((P["w_ada"],), P["w_ada"].ap[l][:, fc * 128:(fc + 1) * 128].rearrange("(k p) n -> p k n", p=128)))
            pb = bank()
            for k in range(8):
                kb.mm(pb[:, 0:NSEQ], sfv[:, k, :], cact[:, k, :], start=(k == 0), stop=(k == 7))
            kb.ts(mod[:, fc, :], pb[:, 0:NSEQ], L["bada"][:, fc:fc + 1], ALU.add)
        L["mod"] = mod
        a1 = kb.sb([128, 8, NSEQ], F32, "a1_%d" % l)
        a2 = kb.sb([128, 8, NSEQ], F32, "a2_%d" % l)
        for s_ in range(NSEQ):
            kb.stt(a1[:, :, s_], mod[:, 8:16, s_], 1.0, L["n1"].v, ALU.add, ALU.mult)
            kb.stt(a2[:, :, s_], mod[:, 32:40, s_], 1.0, L["n2"].v, ALU.add, ALU.mult)
        L["a1"], L["a2"] = a1, a2

    DEPTH_C = DEPTH if stop >= 1 else 0
    cast_eng = ["act", "dve", "pool"]
    cst = {"i": 0}

    def cast_piece(src_buf, src_ap_piece, w, dst_v, src_to_dst=None, scale_v=None):
        i = stg["i"] % 2; stg["i"] += 1
        sf, sbb = stage_f[i], stage_b[i]
        kb.dma(sf[:, 0:w], V((src_buf,), src_ap_piece), eng=("sp" if i == 0 else "pool"))
        if scale_v is not None:
            kb.ts(sbb[:, 0:w], sf[:, 0:w], scale_v, ALU.mult)
        else:
            e = cast_eng[cst["i"] % 3]; cst["i"] += 1
            kb.copy(sbb[:, 0:w], sf[:, 0:w], eng=e)
        sv = sbb[:, 0:w] if src_to_dst is None else src_to_dst(sbb[:, 0:w])
        kb.dma(dst_v, sv, eng=("sp" if i == 1 else "pool"))

    def cast_weight(src_buf, src_ap, K, N, dst, scale=None):
        KC = K // 128
        for k in range(KC):
            for n0 in range(0, N, 1024):
                w = min(1024, N - n0)
                cast_piece(src_buf, src_ap[k * 128:(k + 1) * 128, n0:n0 + w], w, dst[:, k, n0:n0 + w],
                           scale_v=(scale[:, k:k + 1] if scale is not None else None))

    for l in range(DEPTH_C):
        L = LP[l]
        wi = P["w_in"].ap[l]
        for k in range(8):
            for bi, c0 in enumerate(WIN_BLK):
                cast_piece(P["w_in"], wi[k * 128:(k + 1) * 128, c0:c0 + 512], 512, win_s[l][:, bi, k * 512:(k + 1) * 512])
        cast_weight(P["w_br_a"], P["w_br_a"].ap[l], 512, D, wbr_s[l][0], scale=L["gnw"][:, 0:1].bc([128, 4]))
        cast_weight(P["w_br_b"], P["w_br_b"].ap[l], 512, D, wbr_s[l][1], scale=L["hnw"][:, 0:1].bc([128, 4]))
        cast_weight(P["w_br_c"], P["w_br_c"].ap[l], 512, D, wbr_s[l][2], scale=L["snw"].v)
        wo_ = P["w_out"].ap[l]
        for k in range(8):
            cast_piece(P["w_out"], wo_[k * 128:(k + 1) * 128, :], 1024,
                       wout_s[l].v.re("p b (k c) -> p b k c", c=512)[:, :, k, :],
                       src_to_dst=lambda v: v.re("p (b c) -> p b c", c=512))
        wu_ = P["ffn_w_up"].ap[l]
        for k in range(8):
            for half in range(2):
                for n0 in range(0, FH, 1024):
                    w = min(1024, FH - n0)
                    b0, nb = n0 // 256, w // 256
                    cast_piece(P["ffn_w_up"], wu_[k * 128:(k + 1) * 128, half * FH + n0:half * FH + n0 + w], w,
                               wup_s[l].v.re("p b (k c) -> p b k c", c=512)[:, b0:b0 + nb, k, half * 256:(half + 1) * 256],
                               src_to_dst=lambda v: v.re("p (b c) -> p b c", c=256))
        wd_ = P["ffn_w_down"].ap[l]
        for k in range(22):
            cast_piece(P["ffn_w_down"], wd_[k * 128:(k + 1) * 128, :], 1024,
                       wdn_s[l].v.re("p b (k c) -> p b k c", c=128)[:, :, k, :],
                       src_to_dst=lambda v: v.re("p (b c) -> p b c", c=128))
        wsm = kb.sb([128, 8, 16], BF16, "wsm%d" % l)
        sf = stage_f[stg["i"] % 2]; stg["i"] += 1
        sfv = sf[:, 0:128].re("p (k n) -> p k n", n=16)
        for (d0, c0, n) in ((0, O_A, 4), (4, O_DT, 8), (12, O_B, 4)):
            kb.dma(sfv[:, :, d0:d0 + n], V((P["w_in"],), wi[:, c0:c0 + n].rearrange("(k p) n -> p k n", p=128)),
                   allow_slow_non_contiguous=True)
        kb.copy(wsm.v, sfv)
        L["wsm"] = wsm

    NW = 4
    wbufs = [kb.sb([128, 4096], BF16, "wbuf%d" % i) for i in range(NW)]
    wst = {"i": 0}

    def wload(src_view, kc, ncols):
        b = wbufs[wst["i"] % NW]; wst["i"] += 1
        kb.dma(b.v[:, 0:kc * ncols], src_view)
        return b.v[:, 0:kc * ncols].re("p (k n) -> p k n", n=ncols)

    xT = kb.sb([128, 8, T], F32, "xT")
    xio = kb.sb([128, NS, D], F32, "xio")
    hT = kb.sb([128, 8, T], BF16, "hT")
    macc = kb.sb([128, 8, T], F32, "macc")
    mergedT = kb.sb([128, 8, T], BF16, "mergedT")
    obr = [kb.sb([128, 4, T], BF16, "obr%d" % i) for i in range(3)]
    hidT = kb.sb([128, 22, T], BF16, "hidT")
    ftmp = [kb.sb([128, T + 4], F32, "ftmp%d" % i) for i in range(14)]
    fst = {"i": 0}

    def ft():
        b = ftmp[fst["i"] % len(ftmp)]; fst["i"] += 1
        return b
    ft_global = ft

    abf = [kb.sb([128, 4, T], BF16, "abf%d" % i) for i in range(3)]
    tokbf = [[kb.sb([128, 512], BF16, "tokbf%d_%d" % (i, s_)) for s_ in range(NS)] for i in range(4)]
    tokf = [[kb.sb([128, 512], F32, "tokf%d_%d" % (i, s_)) for s_ in range(NS)] for i in range(2)]
    sq128 = [kb.sb([128, 128], F32, "sq%d" % i) for i in range(2)]
    sqst = {"i": 0}

    def sq():
        b = sq128[sqst["i"] % len(sq128)]; sqst["i"] += 1
        return b

    def alias(parent, ap, name):
        b = Buf(name, parent.t, ap)
        b.lw = parent.lw
        b.rd = list(parent.rd)
        return b

    sq_extra = [alias(stage_f[i // 8], stage_f[i // 8].ap[:, (i % 8) * 128:(i % 8 + 1) * 128], "sqx%d" % i) for i in range(16)]
    bf_extra = [alias(stage_b[i // 8], stage_b[i // 8].ap[:, (i % 8) * 128:(i % 8 + 1) * 128], "bfx%d" % i) for i in range(16)]
    gdn_tp = [{"sq": sq_extra[h * 4:h * 4 + 4] + [kb.sb([128, 128], F32, "gsq%d_%d" % (h, i)) for i in range(2)],
               "bf": bf_extra[h * 4:h * 4 + 4]} for h in range(4)]

    for h_ in range(4):
        kb.memset(gdn_tp[h_]["bf"][3].v, 0.0)

    def run_threads(gens):
        gens = list(gens)
        while gens:
            for g in list(gens):
                try:
                    next(g)
                except StopIteration:
                    gens.remove(g)

    sqb = [kb.sb([128, 128], BF16, "sqb%d" % i) for i in range(4)]
    sqbst = {"i": 0}

    def sqbf():
        b = sqb[sqbst["i"] % len(sqb)]; sqbst["i"] += 1
        return b

    big4 = [kb.sb([128, 4, 128], F32, "big4_%d" % i) for i in range(4)]
    big4b = [kb.sb([128, 4, 128], BF16, "big4b_%d" % i) for i in range(2)]
    small = [kb.sb([128, 32], F32, "small%d" % i) for i in range(16)]
    smst = {"i": 0}

    def sm_():
        b = small[smst["i"] % len(small)]; smst["i"] += 1
        return b

    qgz = [[kb.sb([128, 640], BF16, "qgz%d_%d" % (i, s_)) for s_ in range(NS)] for i in range(2)]
    for i in range(2):
        for s_ in range(NS):
            kb.memset(qgz[i][s_].v, 0.0)
    kendz = [[kb.sb([128, 4, 128], BF16, "kendz%d_%d" % (i, s_)) for s_ in range(NS)] for i in range(2)]
    vnew = kb.sb([128, 128], BF16, "vnew")
    kb.memset(vnew.v, 0.0)
    otok = [kb.sb([128, 256], F32, "otok%d" % i) for i in range(6)]
    otst = {"i": 0}

    def ot():
        b = otok[otst["i"] % 6]; otst["i"] += 1
        return b

    ST = []
    for l in range(DEPTH):
        s_ = {}
        s_["gS"] = kb.sb([128, 4, 128], F32, "gS%d" % l)
        s_["gSb"] = kb.sb([128, 4, 128], BF16, "gSb%d" % l)
        s_["hS"] = kb.sb([128, 4, 128], F32, "hS%d" % l)
        s_["hSb"] = kb.sb([128, 4, 128], BF16, "hSb%d" % l)
        s_["sS"] = kb.sb([128, 2, 256], F32, "sS%d" % l)
        s_["sSb"] = kb.sb([128, 2, 256], BF16, "sSb%d" % l)
        s_["gh"] = kb.sb([128, 12, 3], F32, "gh%d" % l)
        s_["sh"] = kb.sb([128, 8, 3], F32, "sh%d" % l)
        s_["fh"] = kb.sb([128, 44, 2], F32, "fh%d" % l)
        ST.append(s_)

    def rms_mod(a_v, b_v, out_bf=None, out_f32=None):
        pb = bank()
        for c in range(8):
            s2 = ft()
            kb.act(s2[:, 0:T], xT[:, c, :], AF.Square)
            s2b = abf[2]
            kb.copy(s2b[:, c % 4, :], s2[:, 0:T], eng="pool")
            kb.mm(pb[:, 0:T], ones_bf.v, s2b[:, c % 4, :], start=(c == 0), stop=(c == 7))
        rs = ft()
        kb.act(rs[:, 0:T], pb[:, 0:T], AF.Sqrt, scale=1.0 / D, bias=EPS)
        kb.recip(rs[:, 0:T], rs[:, 0:T])
        for c in range(8):
            t_ = ft()
            kb.tt(t_[:, 0:T], xT[:, c, :], rs[:, 0:T], ALU.mult)
            if out_bf is not None:
                kb.act(out_bf[:, c, :], t_[:, 0:T], AF.Identity, scale=a_v[:, c:c + 1], bias=b_v[:, c:c + 1])
            else:
                kb.act(out_f32[:, c, :], t_[:, 0:T], AF.Identity, scale=a_v[:, c:c + 1])

    def fm_proj(wv, col0, rhs_buf, KC, evac):
        pb = bank()
        for k in range(KC):
            kb.mm(pb[:, 0:T], wv[:, k, col0:col0 + 128], rhs_buf[:, k, :], start=(k == 0), stop=(k == KC - 1))
        evac(pb[:, 0:T])

    def tm_proj(wv, col0, ncols, sub, evac):
        pb = bank()
        for k in range(8):
            kb.mm(pb[:, 0:ncols], hT[:, k, sub * 128:(sub + 1) * 128], wv[:, k, col0:col0 + ncols],
                  start=(k == 0), stop=(k == 7))
        evac(pb[:, 0:ncols])

    def slot_alloc(slot, n=7):
        lst = ftmp[slot * n:(slot + 1) * n]
        stt_ = {"i": 0}

        def al():
            b = lst[stt_["i"] % n]; stt_["i"] += 1
            return b
        return al

    def run_slots(factories, nslots=2, offset=1):
        pending = list(factories)[::-1]
        slots = [None] * nslots
        steps = [0] * nslots
        while pending or any(g is not None for g in slots):
            for i in range(nslots):
                if slots[i] is None:
                    if pending and all(slots[j] is None or steps[j] >= offset for j in range(nslots) if j != i):
                        slots[i] = pending.pop()(i)
                        steps[i] = 0
                    else:
                        continue
                try:
                    next(slots[i])
                    steps[i] += 1
                except StopIteration:
                    slots[i] = None

    def conv(ps_v, hist_v, wv, ntap, bias_v=None, ft=None):
        if ft is None:
            ft = ft_global
        H = ntap - 1
        buf = ft()
        kb.copy(buf[:, H:H + T], ps_v, eng="act")
        kb.copy(buf[:, 0:H], hist_v, eng="pool")
        acc = ft()
        kb.act(acc[:, 0:T], ps_v, AF.Identity, scale=wv[:, H:H + 1], bias=(bias_v if bias_v is not None else 0.0))
        for k in range(0, H):
            kb.stt(acc[:, 0:T], buf[:, k:k + T], wv[:, k:k + 1], acc[:, 0:T], ALU.mult, ALU.add)
        kb.copy(hist_v, buf[:, T:T + H], eng="pool")
        return acc

    def norm_gate_T(o_v, width, gate_v, dst_list, junk=None, og=None):
        junk = junk or ot()
        ss = sm_()
        kb.act(junk[:, 0:width], o_v, AF.Square, accum=ss[:, 0:1])
        kb.act(ss[:, 1:2], ss[:, 0:1], AF.Sqrt, scale=1.0 / width, bias=EPS)
        kb.recip(ss[:, 2:3], ss[:, 1:2])
        og = og or ot()
        if gate_v is not None:
            kb.stt(og[:, 0:width], o_v, ss[:, 2:3], gate_v, ALU.mult, ALU.mult)
        else:
            kb.ts(og[:, 0:width], o_v, ss[:, 2:3], ALU.mult)
        for i, dst in enumerate(dst_list):
            pb = bank()
            kb.tr(pb[:, 0:128], og[:, i * 128:(i + 1) * 128], IDENT)
            kb.copy(dst, pb[:, 0:128], eng="act")

    def layer_body(l, seq, first_tile):
        L = LP[l]
        st = ST[l]
        mod = L["mod"]
        W = win_s[l]
        if first_tile:
            for k_ in ("gS", "gSb", "hS", "hSb", "sS", "sSb", "gh", "sh", "fh"):
                kb.memset(st[k_].v, 0.0)
        rms_mod(L["a1"][:, :, seq], mod[:, 0:8, seq], out_bf=hT)

        if stop < 3:
            return
        gda = [None] * NS
        dtv = [None] * NS
        beta = [None] * NS
        nbeta = [None] * NS
        for sub in range(NS):
            pb = bank()
            for k in range(8):
                kb.mm(pb[:, 0:16], hT[:, k, sub * 128:(sub + 1) * 128], L["wsm"][:, k, :], start=(k == 0), stop=(k == 7))
            t1 = sm_()
            kb.tt(t1[:, 0:12], pb[:, 0:12], L["dtb"].v, ALU.add)
            kb.act(t1[:, 0:12], t1[:, 0:12], AF.Exp)
            spl = persist[sub]["spl"]
            kb.act(spl[:, 0:12], t1[:, 0:12], AF.Ln, bias=1.0)
            g_ = persist[sub]["g"]
            kb.tt(g_[:, 0:12], spl[:, 0:12], L["nea"].v, ALU.mult)
            bt = persist[sub]["bt"]
            kb.act(bt[:, 0:4], pb[:, 12:16], AF.Sigmoid)
            kb.ts(bt[:, 4:8], bt[:, 0:4], -1.0, ALU.mult)
            gda[sub], dtv[sub], beta[sub], nbeta[sub] = g_, spl, bt[:, 0:4], bt[:, 4:8]

        expGA = [None] * NS
        eend = [None] * NS
        gend = [None] * NS
        for sub in range(NS):
            g_ = gda[sub]
            pb = bank()
            kb.mm(pb[:, 0:12], M_BTRI64, g_[:, 0:12])
            kb.mm(pb[:, 16:28], M_BUP64, g_[:, 0:12])
            kb.mm(pb[:, 32:44], M_IND64[0], g_[:, 0:12])
            kb.mm(pb[:, 48:60], M_IND64[1], g_[:, 0:12])
            e1 = persist[sub]["e1"]
            kb.act(e1[:, 0:12], pb[:, 0:12], AF.Exp)
            e2 = sm_()
            kb.act(e2[:, 0:12], pb[:, 16:28], AF.Exp)
            e2m = persist[sub]["e2m"]
            kb.ts(e2m[:, 0:12], e2[:, 0:12], INDC[:, 4:5], ALU.mult)
            kb.ts(e2m[:, 12:24], e2[:, 0:12], INDC[:, 5:6], ALU.mult)
            e3 = persist[sub]["e3"]
            kb.act(e3[:, 0:12], pb[:, 32:44], AF.Exp)
            kb.act(e3[:, 12:24], pb[:, 48:60], AF.Exp)
            expGA[sub], eend[sub], gend[sub] = e1, e2m, e3

        def decay_T(sub, h0):
            rh = big4[0]
            kb.tt(rh.v, M_TRIALL.un(1).bc([128, 4, 128]), gda[sub][:, h0:h0 + 4].un(2).bc([128, 4, 128]), ALU.mult)
            pb = bank()
            kb.mm(pb[:, 0:512], M_UPALL, rh.v.re("p h l -> p (h l)"))
            dec = big4[1]
            kb.act(dec.v.re("p h l -> p (h l)"), pb[:, 0:512], AF.Exp)
            return dec

        if stop < 4:
            return
        wq = wload(W[:, WIN_IDX[O_Q], :], 8, 512)
        wk = wload(W[:, WIN_IDX[O_K], :], 8, 512)
        wv_ = wload(W[:, WIN_IDX[O_V], :], 8, 512)
        qT, kT = abf[0], abf[1]
        vtok, kg, ke0, ke1 = tokbf[0], tokbf[1], tokbf[2], tokbf[3]

        def gdn_chunk(fc):
            def gen(slot):
                al = slot_alloc(slot, 4)
                which, h = fc // 4, fc % 4
                wv = (wq, wk, wv_)[which]
                pb = bank()
                for k in range(8):
                    kb.mm(pb[:, 0:T], wv[:, k, h * 128:(h + 1) * 128], hT[:, k, :], start=(k == 0), stop=(k == 7))
                yield
                acc = conv(pb[:, 0:T], st["gh"][:, fc, :], L["gcw"][:, fc, :], 4, ft=al)
                s_ = al()
                kb.act(s_[:, 0:T], acc[:, 0:T], AF.Silu)
                if which < 2:
                    s2 = al()
                    kb.act(s2[:, 0:T], s_[:, 0:T], AF.Square)
                    s2b = abf[2][:, slot, :]
                    kb.copy(s2b, s2[:, 0:T], eng="pool")
                    pb = bank()
                    kb.mm(pb[:, 0:T], ones_bf.v, s2b)
                    yield
                    rs = al()
                    if which == 0:
                        kb.act(rs[:, 0:T], pb[:, 0:T], AF.Sqrt, scale=128.0, bias=128.0 * EPS)
                    else:
                        kb.act(rs[:, 0:T], pb[:, 0:T], AF.Sqrt, bias=EPS)
                    kb.recip(rs[:, 0:T], rs[:, 0:T])
                    if which == 0:
                        kb.tt(qT[:, h, :], s_[:, 0:T], rs[:, 0:T], ALU.mult)
                        return
                    knf = al()
                    kb.tt(knf[:, 0:T], s_[:, 0:T], rs[:, 0:T], ALU.mult)
                    kb.copy(kT[:, h, :], knf[:, 0:T], eng="act")
                    pbs = []
                    for sub in range(NS):
                        pb2 = bank()
                        kb.tr(pb2[:, 0:128], knf[:, sub * 128:(sub + 1) * 128], IDENT)
                        pbs.append(pb2)
                    yield
                    for sub in range(NS):
                        pb2 = pbs[sub]
                        kb.act(kg[sub][:, h * 128:(h + 1) * 128], pb2[:, 0:128], AF.Identity, scale=expGA[sub][:, h:h + 1])
                        kb.act(ke0[sub][:, h * 128:(h + 1) * 128], pb2[:, 0:128], AF.Identity, scale=eend[sub][:, h:h + 1])
                        kb.act(ke1[sub][:, h * 128:(h + 1) * 128], pb2[:, 0:128], AF.Identity, scale=eend[sub][:, 12 + h:13 + h])
                else:
                    pbs = []
                    for sub in range(NS):
                        pb2 = bank()
                        kb.tr(pb2[:, 0:128], s_[:, sub * 128:(sub + 1) * 128], IDENT)
                        pbs.append(pb2)
                    yield
                    for sub in range(NS):
                        kb.copy(vtok[sub][:, h * 128:(h + 1) * 128], pbs[sub][:, 0:128], eng="act")
            return gen

        order = [0, 4, 1, 5, 2, 6, 3, 7, 8, 9, 10, 11]
        run_slots([gdn_chunk(fc) for fc in order], nslots=3, offset=1)
        if stop < 4.1:
            return
        wz = wload(W[:, WIN_IDX[O_Z], :], 8, 512)
        siluz = tokf[0]
        for sub in range(NS):
            tm_proj(wz, 0, 512, sub, lambda ps_v, sub=sub: kb.act(siluz[sub].v, ps_v, AF.Silu))

        if stop < 4.2:
            return

        def gdn_head(sub, h, decm, decs, TP):
            sl = slice(sub * 128, (sub + 1) * 128)
            hs = slice(h * 128, (h + 1) * 128)
            PA, PTA, PB, PTB, X, XT = TP["sq"]
            Xb, wT, attnT, vn = TP["bf"]
            pk = bank()
            kb.mm(pk[:, 0:128], kT[:, h, sl], kT[:, h, sl])
            yield
            kb.stt(PA.v, pk[:, 0:128], nbeta[sub][:, h:h + 1], decs[:, h, :], ALU.mult, ALU.mult)
            pt = bank()
            kb.tr(pt[:, 0:128], PA.v, IDENT)
            yield
            kb.copy(PTA.v, pt[:, 0:128], eng="act")
            kb.tt(X.v, PA.v, IDENT, ALU.add)
            kb.tt(XT.v, PTA.v, IDENT, ALU.add, eng="pool")
            Pm, PT, Pn, PTn = PA, PTA, PB, PTB
            for lev in range(5):
                last = (lev == 4)
                p2 = bank()
                kb.mm(p2[:, 0:128], PT.v, Pm.v)
                if not last:
                    kb.mm(p2[:, 128:256], Pm.v, PT.v)
                yield
                kb.copy(Pn.v, p2[:, 0:128], eng="act")
                if not last:
                    kb.copy(PTn.v, p2[:, 128:256], eng="dve")
                px = bank()
                kb.mm(px[:, 0:128], XT.v, Pn.v)
                if not last:
                    kb.mm(px[:, 128:256], Pn.v, XT.v)
                yield
                if last:
                    kb.tt(Xb.v, px[:, 0:128], X.v, ALU.add)
                else:
                    kb.tt(X.v, px[:, 0:128], X.v, ALU.add)
                    kb.tt(XT.v, px[:, 128:256], XT.v, ALU.add)
                    Pm, PT, Pn, PTn = Pn, PTn, Pm, PT
            if stop < 4.4:
                return
            bu, oa, o_ = PA, PTA, PB
            pu = bank()
            kb.mm(pu[:, 0:128], Xb.v, vtok[sub][:, hs])
            kb.mm(pu[:, 128:256], kg[sub][:, hs], Xb.v)
            kb.mm(pu[:, 256:384], kT[:, h, sl], qT[:, h, sl])
            yield
            kb.act(bu.v, pu[:, 0:128], AF.Identity, scale=beta[sub][:, h:h + 1])
            kb.copy(wT.v, pu[:, 128:256], eng="act")
            kb.tt(attnT.v, pu[:, 256:384], decm[:, h, :], ALU.mult)
            for c in range(2):
                rows = slice(c * 64, (c + 1) * 64)
                p1 = bank()
                kb.mm(p1[:, 0:128], wT.v, st["gSb"][:, h, :])
                kb.mm(p1[:, 128:256], qT[:, h, sl], st["gSb"][:, h, :])
                yield
                kb.stt(vn[rows, :], p1[rows, 0:128], nbeta[sub][rows, h:h + 1], bu[rows, :], ALU.mult, ALU.add)
                kb.act(oa[rows, :], p1[rows, 128:256], AF.Identity, scale=expGA[sub][rows, h:h + 1])
                pbq = bank()
                kb.mm(pbq[:, 0:128], attnT.v, vn.v)
                ke = (ke0, ke1)[c]
                kb.mm(pbq[:, 128:256], ke[sub][:, hs], vn.v)
                yield
                kb.tt(o_[rows, :], oa[rows, :], pbq[rows, 0:128], ALU.add)
                kb.stt(st["gS"][:, h, :], st["gS"][:, h, :], gend[sub][:, c * 12 + h:c * 12 + h + 1], pbq[:, 128:256],
                       ALU.mult, ALU.add)
                kb.copy(st["gSb"][:, h, :], st["gS"][:, h, :], eng="act")
            norm_gate_T(o_.v, 128, siluz[sub][:, hs], [obr[0][:, h, sl]])

        for sub in range(NS):
            dec = decay_T(sub, 0)
            decm = big4[2]
            kb.tt(decm.v, dec.v, M_INCL64.un(1).bc([128, 4, 128]), ALU.mult)
            decs = big4[3]
            kb.tt(decs.v, decm.v, IDENT.un(1).bc([128, 4, 128]), ALU.subtract)
            if stop < 4.3:
                continue
            run_threads([gdn_head(sub, h, decm, decs, gdn_tp[h]) for h in range(4)])

        if stop < 5:
            return
        whi = wload(W[:, WIN_IDX[O_HI], :], 8, 512)
        vh = tokbf[0]
        for sub in range(NS):
            tm_proj(whi, 0, 512, sub, lambda ps_v, sub=sub: kb.copy(vh[sub].v, ps_v, eng="act"))
        whg = wload(W[:, WIN_IDX[O_HG], :], 8, 512)
        silug = tokf[0]
        for sub in range(NS):
            tm_proj(whg, 0, 512, sub, lambda ps_v, sub=sub: kb.act(silug[sub].v, ps_v, AF.Silu))
        whq = wload(W[:, WIN_IDX[O_HQ], :], 8, 512)
        whf = wload(W[:, WIN_IDX[O_HF], :], 8, 512)
        NCH = T // 32

        def hgrn_head(h):
            def gen(slot):
                al = slot_alloc(slot)
                hs = slice(h * 128, (h + 1) * 128)
                q_, f_, t3, G, t5 = al(), al(), al(), al(), al()
                pq = bank()
                for k in range(8):
                    kb.mm(pq[:, 0:T], whq[:, k, hs], hT[:, k, :], start=(k == 0), stop=(k == 7))
                pf = bank()
                for k in range(8):
                    kb.mm(pf[:, 0:T], whf[:, k, hs], hT[:, k, :], start=(k == 0), stop=(k == 7))
                yield
                kb.act(q_[:, 0:T], pq[:, 0:T], AF.Silu)
                kb.act(f_[:, 0:T], pf[:, 0:T], AF.Sigmoid)
                kb.ts(f_[:, 0:T], f_[:, 0:T], L["oml"][:, h:h + 1], ALU.mult, L["lb"][:, h:h + 1], ALU.add)
                kb.act(t3[:, 0:T], f_[:, 0:T], AF.Ln)
                kk = f_
                kb.ts(kk[:, 0:T], f_[:, 0:T], -1.0, ALU.mult, 1.0, ALU.add)
                kb.scan(G[:, 0:T], resetm.v, t3[:, 0:T], 0.0, ALU.mult, ALU.add)
                yield
                G3 = G[:, 0:T].re("p (c r) -> p c r", r=32)
                Dm = t3
                kb.tt(Dm[:, 0:T].re("p (c r) -> p c r", r=32), G3, G3[:, :, 15:16].bc([128, NCH, 32]), ALU.subtract)
                kb.act(t5[:, 0:T], Dm[:, 0:T], AF.Exp)
                qt, kt = abf[0], abf[1]
                kb.tt(qt[:, slot, :], q_[:, 0:T], t5[:, 0:T], ALU.mult)
                yield
                kb.act(t5[:, 0:T], Dm[:, 0:T], AF.Exp, scale=-1.0)
                kb.tt(kt[:, slot, :], kk[:, 0:T], t5[:, 0:T], ALU.mult)
                yield
                kb.act(t5[:, 0:T], G[:, 0:T], AF.Exp)
                qz = qgz[slot]
                for sub in range(NS):
                    kb.tt(qz[sub].v.re("p (c r) -> p c r", r=160)[:, :, 0:32],
                          q_[:, sub * 128:(sub + 1) * 128].re("p (c r) -> p c r", r=32),
                          t5[:, sub * 128:(sub + 1) * 128].re("p (c r) -> p c r", r=32), ALU.mult)
                yield
                DL = t3
                kb.tt(DL[:, 0:T].re("p (c r) -> p c r", r=32), G3[:, :, 31:32].bc([128, NCH, 32]), G3, ALU.subtract)
                kb.act(t5[:, 0:T], DL[:, 0:T], AF.Exp)
                kend = t3
                kb.tt(kend[:, 0:T], kk[:, 0:T], t5[:, 0:T], ALU.mult)
                ge = gepool[slot]
                kb.act(ge[:, 0:NCH], G3[:, :, 31], AF.Exp)
                kz = kendz[slot]
                for sub in range(NS):
                    sl = slice(sub * 128, (sub + 1) * 128)
                    pk = bank()
                    kb.tr(pk[:, 0:128], kend[:, sl], IDENT)
                    psc = bank()
                    kb.mm(psc[:, 0:128], kt[:, slot, sl], qt[:, slot, sl])
                    yield
                    kb.tt(kz[sub].v, pk[:, 0:128].un(1).bc([128, 4, 128]), INDC[:, 0:4].un(2).bc([128, 4, 128]), ALU.mult)
                    attnT = sqb[slot]
                    kb.tt(attnT.v, psc[:, 0:128], M_INCL32, ALU.mult)
                    po = banks[6 + slot]
                    kb.mm(po[:, 0:128], attnT.v, vh[sub][:, hs], start=True, stop=False)
                    for c in range(4):
                        kb.mm(po[:, 0:128], qz[sub][:, c * 128:(c + 1) * 128], st["hSb"][:, h, :], start=False, stop=(c == 3))
                        pss = bank()
                        kb.mm(pss[:, 0:128], kz[sub][:, c, :], vh[sub][:, hs])
                        yield
                        kb.stt(st["hS"][:, h, :], st["hS"][:, h, :], ge[:, sub * 4 + c:sub * 4 + c + 1], pss[:, 0:128],
                               ALU.mult, ALU.add)
                        kb.copy(st["hSb"][:, h, :], st["hS"][:, h, :], eng="act")
                    norm_gate_T(po[:, 0:128], 128, silug[sub][:, hs], [obr[1][:, h, sl]])
            return gen

        run_slots([hgrn_head(h) for h in range(4)], nslots=2, offset=5)

        if stop < 6:
            return
        wsz = wload(W[:, WIN_IDX[O_SZ], :], 8, 512)
        siluzs = tokf[0]
        for sub in range(NS):
            tm_proj(wsz, 0, 512, sub, lambda ps_v, sub=sub: kb.act(siluzs[sub].v, ps_v, AF.Silu))
        wsx = wload(W[:, WIN_IDX[O_SX], :], 8, 512)
        wsbc = wload(W[:, WIN_IDX[O_SB], :], 8, 512)
        xtok, xdt, btok = tokf[1], tokbf[0], tokbf[1]
        BT, CT = abf[0], abf[1]

        def ssd_chunk(fc):
            def gen(slot):
                al = slot_alloc(slot, 4)
                wv = wsx if fc < 4 else wsbc
                pb = bank()
                for k in range(8):
                    kb.mm(pb[:, 0:T], wv[:, k, (fc % 4) * 128:(fc % 4 + 1) * 128], hT[:, k, :], start=(k == 0), stop=(k == 7))
                yield
                acc = conv(pb[:, 0:T], st["sh"][:, fc, :], L["scw"][:, fc, :], 4, bias_v=L["scb"][:, fc:fc + 1], ft=al)
                s_ = al()
                kb.act(s_[:, 0:T], acc[:, 0:T], AF.Silu)
                if fc >= 6:
                    kb.copy(CT[:, fc - 6, :], s_[:, 0:T], eng="pool")
                    return
                if fc >= 4:
                    kb.copy(BT[:, fc - 4, :], s_[:, 0:T], eng="pool")
                pbs = []
                for sub in range(NS):
                    pb2 = bank()
                    kb.tr(pb2[:, 0:128], s_[:, sub * 128:(sub + 1) * 128], IDENT)
                    pbs.append(pb2)
                yield
                for sub in range(NS):
                    if fc < 4:
                        kb.copy(xtok[sub][:, fc * 128:(fc + 1) * 128], pbs[sub][:, 0:128], eng="act")
                    else:
                        g = fc - 4
                        kb.copy(btok[sub][:, g * 128:(g + 1) * 128], pbs[sub][:, 0:128], eng="act")
            return gen

        run_slots([ssd_chunk(fc) for fc in range(8)], nslots=3, offset=1)
        for sub in range(NS):
            kb.tt(xdt[sub].v.re("p (h d) -> p h d", d=64), xtok[sub].v.re("p (h d) -> p h d", d=64),
                  dtv[sub][:, 4:12].un(2).bc([128, 8, 64]), ALU.mult)

        def ssd_group(sub, g):
            def gen(slot):
                sl = slice(sub * 128, (sub + 1) * 128)
                gs = slice(g * 256, (g + 1) * 256)
                h0 = 4 + 4 * g
                rh, dec = big4[2 * slot], big4[2 * slot + 1]
                t1, y, t3 = otok[3 * slot], otok[3 * slot + 1], otok[3 * slot + 2]
                kb.tt(rh.v, M_TRIALL.un(1).bc([128, 4, 128]), gda[sub][:, h0:h0 + 4].un(2).bc([128, 4, 128]), ALU.mult)
                pd = bank()
                kb.mm(pd[:, 0:512], M_UPALL, rh.v.re("p h l -> p (h l)"))
                pcb = bank()
                kb.mm(pcb[:, 0:128], BT[:, g, sl], CT[:, g, sl])
                yield
                kb.act(dec.v.re("p h l -> p (h l)"), pd[:, 0:512], AF.Exp)
                cbm = sq128[slot]
                kb.tt(cbm.v, pcb[:, 0:128], M_INCL64, ALU.mult)
                at = big4b[slot]
                kb.tt(at.v, dec.v, cbm.v.un(1).bc([128, 4, 128]), ALU.mult)
                py = banks[6 + slot]
                for hh in range(4):
                    kb.mm(py[:, hh * 64:(hh + 1) * 64], at[:, hh, :], xdt[sub][:, (4 * g + hh) * 64:(4 * g + hh + 1) * 64])
                for c in range(2):
                    rows = slice(c * 64, (c + 1) * 64)
                    poff = bank()
                    kb.mm(poff[:, 0:256], CT[:, g, sl], st["sSb"][:, g, :])
                    xw = tokbf[2 + slot][sub]
                    kb.tt(xw[:, 0:256].re("p (h d) -> p h d", d=64), xdt[sub][:, gs].re("p (h d) -> p h d", d=64),
                          eend[sub][:, c * 12 + h0:c * 12 + h0 + 4].un(2).bc([128, 4, 64]), ALU.mult)
                    pst = bank()
                    kb.mm(pst[:, 0:256], btok[sub][:, g * 128:(g + 1) * 128], xw[:, 0:256])
                    yield
                    kb.tt(t1[rows, :].re("p (h d) -> p h d", d=64), poff[rows, 0:256].re("p (h d) -> p h d", d=64),
                          expGA[sub][rows, h0:h0 + 4].un(2).bc([64, 4, 64]), ALU.mult)
                    kb.tt(st["sS"][:, g, :].re("p (h d) -> p h d", d=64), st["sS"][:, g, :].re("p (h d) -> p h d", d=64),
                          gend[sub][:, c * 12 + h0:c * 12 + h0 + 4].un(2).bc([128, 4, 64]), ALU.mult, eng="pool")
                    kb.tt(st["sS"][:, g, :], st["sS"][:, g, :], pst[:, 0:256], ALU.add)
                    kb.copy(st["sSb"][:, g, :], st["sS"][:, g, :], eng="act")
                kb.tt(y.v, py[:, 0:256], t1.v, ALU.add)
                kb.tt(t3.v, xtok[sub][:, gs], L["dfull"].v.re("p h d -> p (h d)")[:, gs], ALU.mult, eng="pool")
                kb.tt(y.v, y.v, t3.v, ALU.add)
                kb.tt(y.v, y.v, siluzs[sub][:, gs], ALU.mult)
                norm_gate_T(y.v, 256, None, [obr[2][:, 2 * g, sl], obr[2][:, 2 * g + 1, sl]], junk=t3, og=t1)
            return gen

        run_slots([ssd_group(sub, g) for sub in range(NS) for g in range(2)], nslots=2, offset=1)

        if stop < 7:
            return
        if dbg and l == 0 and first_tile and seq == 0:
            for i in range(3):
                kb.copy(macc[:, 0:4, :], obr[i].v)
                tickets.append(dbgout("obr%d" % i, macc[:, 0:4, :], [128, 4, T]))

        for br in range(3):
            wb = wload(wbr_s[l][br].v.re("p k n -> p (k n)"), 4, 1024)
            for jb in range(2):
                wg = wload(W[:, WIN_IDX[O_GATE + br * 1024 + jb * 512], :], 8, 512)
                for jj in range(4):
                    j = jb * 4 + jj
                    gt = ft()
                    fm_proj(wg, jj * 128, hT, 8, lambda ps_v, gt=gt: kb.act(gt[:, 0:T], ps_v, AF.Sigmoid))
                    if br == 0:
                        fm_proj(wb, j * 128, obr[br], 4,
                                lambda ps_v, gt=gt, j=j: kb.tt(macc[:, j, :], ps_v, gt[:, 0:T], ALU.mult))
                    else:
                        tm = ft()
                        fm_proj(wb, j * 128, obr[br], 4,
                                lambda ps_v, gt=gt, tm=tm: kb.tt(tm[:, 0:T], ps_v, gt[:, 0:T], ALU.mult))
                        kb.tt(macc[:, j, :], macc[:, j, :], tm[:, 0:T], ALU.add, eng="pool")
        kb.copy(mergedT.v, macc.v, eng="act")
        for jb in range(2):
            wo = wload(wout_s[l][:, jb, :], 8, 512)
            for jj in range(4):
                j = jb * 4 + jj
                fm_proj(wo, jj * 128, mergedT, 8,
                        lambda ps_v, j=j: kb.stt(xT[:, j, :], ps_v, mod[:, 16 + j, seq:seq + 1], xT[:, j, :], ALU.mult, ALU.add))
        if dbg and l == 0 and first_tile and seq == 0:
            tickets.append(dbgout("xmix", xT.v, [128, 8, T]))

        if stop < 8:
            return
        rms_mod(L["a2"][:, :, seq], mod[:, 24:32, seq], out_bf=hT)
        for i in range(11):
            b = wbufs[wst["i"] % NW]; wst["i"] += 1
            wv = b.v.re("p (k n) -> p k n", n=512)
            kb.dma(b.v, wup_s[l][:, i, :])
            for jj in range(2):
                fcg = i * 2 + jj
                fcv = 22 + fcg
                res = {}

                def evg(ps_v, res=res, fcg=fcg):
                    res["g"] = conv(ps_v, st["fh"][:, fcg, :], L["fcw"][:, fcg, :], 3, bias_v=L["fcb"][:, fcg:fcg + 1])

                def evv(ps_v, res=res, fcv=fcv):
                    res["v"] = conv(ps_v, st["fh"][:, fcv, :], L["fcw"][:, fcv, :], 3, bias_v=L["fcb"][:, fcv:fcv + 1])
                fm_proj(wv, jj * 128, hT, 8, evg)
                fm_proj(wv, 256 + jj * 128, hT, 8, evv)
                sg = ft()
                kb.act(sg[:, 0:T], res["g"][:, 0:T], AF.Silu)
                kb.tt(hidT[:, fcg, :], sg[:, 0:T], res["v"][:, 0:T], ALU.mult)
        for j in range(8):
            b = wbufs[wst["i"] % NW]; wst["i"] += 1
            wv = b.v[:, 0:22 * 128].re("p (k n) -> p k n", n=128)
            kb.dma(b.v[:, 0:22 * 128], wdn_s[l][:, j, :])
            fm_proj(wv, 0, hidT, 22,
                    lambda ps_v, j=j: kb.stt(xT[:, j, :], ps_v, mod[:, 40 + j, seq:seq + 1], xT[:, j, :], ALU.mult, ALU.add))
        if dbg and l == 0 and first_tile and seq == 0:
            tickets.append(dbgout("xffn", xT.v, [128, 8, T]))

    for seq in range(NSEQ):
        for ti in range(NT):
            r0 = seq * S + ti * T
            kb.dma(xio.v, V((x_d,), x_d.ap[r0:r0 + T, :].rearrange("(s p) d -> p s d", p=128)), eng="pool")
            for sub in range(NS):
                for c in range(8):
                    pb = bank()
                    kb.tr(pb[:, 0:128], xio[:, sub, c * 128:(c + 1) * 128], IDENT)
                    kb.copy(xT[:, c, sub * 128:(sub + 1) * 128], pb[:, 0:128], eng=("act" if c % 2 else "dve"))
            for l in range(DEPTH if stop >= 2 else 0):
                layer_body(l, seq, ti == 0)
            rms_mod(fnw.v, None, out_f32=macc)
            for sub in range(NS):
                for c in range(8):
                    pb = bank()
                    kb.tr(pb[:, 0:128], macc[:, c, sub * 128:(sub + 1) * 128], IDENT)
                    kb.copy(xio[:, sub, c * 128:(c + 1) * 128], pb[:, 0:128], eng=("act" if c % 2 else "dve"))
            tickets.append(kb.dma(V((out_d,), out_d.ap[r0:r0 + T, :].rearrange("(s p) d -> p s d", p=128)), xio.v, eng="pool"))

    tickets = [t for t in tickets if t is not None]
    kb.finish(tickets)
    return nc, kb, dbg_d


_CACHE = {}


def kernel(**inputs):
    NC = 8
    x = np.ascontiguousarray(inputs["x"], dtype=np.float32)
    B, S, _ = x.shape
    NSEQ = B // NC
    key = (S, NSEQ)
    if key not in _CACHE:
        _CACHE[key] = build(S, NSEQ)[0]
    nc = _CACHE[key]
    cmask = make_masks()
    in_maps = []
    for i in range(NC):
        m = {"x": x[i * NSEQ:(i + 1) * NSEQ].reshape(NSEQ * S, D),
             "c": np.ascontiguousarray(inputs["c"][i * NSEQ:(i + 1) * NSEQ], dtype=np.float32),
             "cmask": cmask}
        for n in PNAMES:
            m[n] = np.ascontiguousarray(inputs[n], dtype=np.float32)
        in_maps.append(m)
    res = run_bass_kernel_spmd(nc, in_maps, core_ids=list(range(NC)))
    out = np.concatenate([r["out"].reshape(NSEQ, S, D) for r in res.results], axis=0)
    return out.astype(np.float32)
```

```python
import numpy as np
from contextlib import ExitStack
import concourse.bass as bass
import concourse.mybir as mybir
from concourse.bass_utils import run_bass_kernel_spmd

F32 = mybir.dt.float32
BF16 = mybir.dt.bfloat16
AF = mybir.ActivationFunctionType
ALU = mybir.AluOpType

D = 1024
NIN = 8720
FH = 2816
F2 = 5632
EPS = 1e-6
O_Q, O_K, O_V = 0, 512, 1024
O_A, O_B, O_Z = 1536, 1540, 1544
O_HQ, O_HF, O_HI, O_HG = 2056, 2568, 3080, 3592
O_SZ, O_SX, O_SB, O_SC, O_DT, O_GATE = 4104, 4616, 5128, 5384, 5640, 5648


class V:
    __slots__ = ("bufs", "ap")

    def __init__(self, bufs, ap):
        self.bufs = bufs
        self.ap = ap

    def __getitem__(self, idx):
        return V(self.bufs, self.ap[idx])

    def re(self, s, **kw):
        return V(self.bufs, self.ap.rearrange(s, **kw))

    def bc(self, shape):
        return V(self.bufs, self.ap.to_broadcast(list(shape)))

    def un(self, axis):
        return V(self.bufs, self.ap.unsqueeze(axis))


class Buf:
    def __init__(self, name, t, ap):
        self.name = name
        self.t = t
        self.ap = ap
        self.lw = None
        self.rd = []
        self.psum = False

    def __getitem__(self, idx):
        return V((self,), self.ap[idx])

    @property
    def v(self):
        return V((self,), self.ap)


class KB:
    ENG = ("pe", "act", "dve", "pool", "sp")
    EP = 30000
    NEP = 10

    def __init__(self, nc, n_dma_sems=24, same_engine_sync=True):
        self.nc = nc
        self.st = ExitStack()
        self.q = {e: [] for e in self.ENG}
        self.sems = {}
        self.cnt = {}
        for e in ("pe", "act", "dve", "pool"):
            for ep in range(self.NEP):
                self.sems["%s#%d" % (e, ep)] = self.st.enter_context(nc.semaphore("c_%s_%d" % (e, ep)))
            self.cnt[e] = 0
        self.dsem = []
        for i in range(n_dma_sems):
            k = "d%d" % i
            self.sems[k] = self.st.enter_context(nc.semaphore(k))
            self.cnt[k] = 0
            self.dsem.append(k)
        self.dnext = 0
        self.dnext2 = 0
        self.waited = {e: {} for e in self.ENG}
        self.same = same_engine_sync
        self.nbuf = 0
        self.ninst = 0

    def sb(self, shape, dtype=F32, name=None):
        self.nbuf += 1
        name = name or ("b%d" % self.nbuf)
        t = self.st.enter_context(self.nc.sbuf_tensor(name, list(shape), dtype))
        return Buf(name, t, t[:])

    def ps(self, shape, dtype=F32, name=None):
        self.nbuf += 1
        name = name or ("p%d" % self.nbuf)
        t = self.st.enter_context(self.nc.psum_tensor(name, list(shape), dtype))
        b = Buf(name, t, t[:])
        b.psum = True
        return b

    def dram(self, name, shape, dtype, kind):
        t = self.nc.dram_tensor(name, list(shape), dtype, kind=kind)
        return Buf(name, t, t.ap())

    def _wait(self, eng, key, val):
        w = self.waited[eng]
        if w.get(key, 0) >= val:
            return
        w[key] = val
        sem = self.sems[key]
        self.q[eng].append(lambda e, sem=sem, val=val: e.wait_ge(sem, val))

    def _deps(self, eng, reads, writes):
        need = {}
        for v in reads:
            for b in v.bufs:
                if b.lw is not None:
                    k, val = b.lw
                    need[k] = max(need.get(k, 0), val)
        for v in writes:
            for b in v.bufs:
                if b.lw is not None:
                    k, val = b.lw
                    need[k] = max(need.get(k, 0), val)
                for (k, val) in b.rd:
                    need[k] = max(need.get(k, 0), val)
        for k, val in need.items():
            if k.split("#")[0] == eng and (eng == "pe" or not self.same):
                continue
            self._wait(eng, k, val)

    def _mark(self, ticket, reads, writes):
        for v in writes:
            for b in v.bufs:
                b.lw = ticket
                b.rd = []
        for v in reads:
            for b in v.bufs:
                if b.lw != ticket:
                    b.rd.append(ticket)
                    if len(b.rd) > 32:
                        m = {}
                        for (k, val) in b.rd:
                            m[k] = max(m.get(k, 0), val)
                        b.rd = list(m.items())

    def op(self, eng, fn, reads, writes):
        pr = [v for v in reads if any(b.psum for b in v.bufs)]
        if pr:
            writes = list(writes) + pr
        self._deps(eng, reads, writes)
        ep, within = divmod(self.cnt[eng], self.EP)
        self.cnt[eng] += 1
        self.ninst += 1
        key = "%s#%d" % (eng, ep)
        sem = self.sems[key]
        self.q[eng].append(lambda e, fn=fn, sem=sem: fn(e).then_inc(sem, 1))
        self._mark((key, within + 1), reads, writes)

    def dma(self, out, in_, eng="sp", **kw):
        if eng == "sp":
            k = self.dsem[self.dnext]
            self.dnext = (self.dnext + 1) % (len(self.dsem) - 8)
        else:
            k = self.dsem[len(self.dsem) - 8 + self.dnext2]
            self.dnext2 = (self.dnext2 + 1) % 8
        if self.cnt[k] > 0:
            self._wait(eng, k, self.cnt[k])
        self._deps(eng, [in_], [out])
        self.cnt[k] += 16
        self.ninst += 1
        sem = self.sems[k]
        self.q[eng].append(lambda e, o=out.ap, i=in_.ap, sem=sem, kw=kw: e.dma_start(out=o, in_=i, **kw).then_inc(sem, 16))
        t = (k, self.cnt[k])
        self._mark(t, [in_], [out])
        return t

    def mm(self, out, lhsT, rhs, start=True, stop=True):
        rd = [lhsT, rhs] + ([] if start else [out])
        self.op("pe", lambda e, o=out.ap, l=lhsT.ap, r=rhs.ap: e.matmul(o, l, r, start=start, stop=stop), rd, [out])

    def tr(self, out, in_, ident):
        self.op("pe", lambda e, o=out.ap, i=in_.ap, d=ident.ap: e.transpose(o, i, d), [in_, ident], [out])

    def act(self, out, in_, func, scale=None, bias=None, accum=None):
        rd = [in_]
        kw = {}
        if scale is not None:
            if isinstance(scale, V):
                rd.append(scale); kw["scale"] = scale.ap
            else:
                kw["scale"] = scale
        if bias is not None:
            if isinstance(bias, V):
                rd.append(bias); kw["bias"] = bias.ap
            else:
                kw["bias"] = bias
        wr = [out]
        if accum is not None:
            wr.append(accum); kw["accum_out"] = accum.ap
        self.op("act", lambda e, o=out.ap, i=in_.ap, kw=kw: e.activation(o, i, func, **kw), rd, wr)

    def tt(self, out, a, b, op, eng="dve"):
        self.op(eng, lambda e, o=out.ap, a_=a.ap, b_=b.ap: e.tensor_tensor(o, a_, b_, op), [a, b], [out])

    def ts(self, out, a, s1, op0, s2=None, op1=None, eng="dve"):
        rd = [a]
        s1a = s1.ap if isinstance(s1, V) else s1
        s2a = s2.ap if isinstance(s2, V) else s2
        if isinstance(s1, V): rd.append(s1)
        if isinstance(s2, V): rd.append(s2)
        if op1 is None:
            self.op(eng, lambda e, o=out.ap, a_=a.ap: e.tensor_scalar(o, a_, s1a, None, op0), rd, [out])
        else:
            self.op(eng, lambda e, o=out.ap, a_=a.ap: e.tensor_scalar(o, a_, s1a, s2a, op0, op1), rd, [out])

    def stt(self, out, a, s, b, op0, op1):
        rd = [a, b]
        sa = s.ap if isinstance(s, V) else s
        if isinstance(s, V): rd.append(s)
        self.op("dve", lambda e, o=out.ap, a_=a.ap, b_=b.ap: e.scalar_tensor_tensor(o, a_, sa, b_, op0, op1), rd, [out])

    def scan(self, out, d0, d1, init, op0, op1):
        self.op("dve", lambda e, o=out.ap, a_=d0.ap, b_=d1.ap: e.tensor_tensor_scan(o, a_, b_, init, op0, op1), [d0, d1], [out])

    def copy(self, out, in_, eng="dve"):
        if eng == "act":
            self.act(out, in_, AF.Copy)
        else:
            self.op(eng, lambda e, o=out.ap, i=in_.ap: e.tensor_copy(o, i), [in_], [out])

    def memset(self, out, val, eng="pool"):
        self.op(eng, lambda e, o=out.ap: e.memset(o, val), [], [out])

    def recip(self, out, in_):
        self.op("dve", lambda e, o=out.ap, i=in_.ap: e.reciprocal(o, i), [in_], [out])

    def finish(self, out_tickets):
        for (k, val) in out_tickets:
            self._wait("sp", k, val)
        nc = self.nc
        with nc.Block() as block:
            @block.tensor
            def _(e):
                for f in self.q["pe"]: f(e)

            @block.scalar
            def _(e):
                for f in self.q["act"]: f(e)

            @block.vector
            def _(e):
                for f in self.q["dve"]: f(e)

            @block.gpsimd
            def _(e):
                for f in self.q["pool"]: f(e)

            @block.sync
            def _(e):
                for f in self.q["sp"]: f(e)
        self.st.close()


def make_masks():
    t = np.arange(128)
    r = t[:, None]
    c = t[None, :]
    same64 = (r // 64) == (c // 64)
    same32 = (r // 32) == (c // 32)
    m = np.zeros((128, 11, 128), np.float32)
    m[:, 0] = (r == c)
    m[:, 1] = 1.0
    m[:, 2] = (c >= r) & same64
    m[:, 3] = (r > c)
    m[:, 4] = (r <= c)
    m[:, 5] = (r <= c) & same64
    m[:, 6] = (r > c) & same64
    m[:, 7] = (r < 64) & (c >= 0)
    m[:, 8] = (r >= 64) & (c >= 0)
    m[:, 9] = (c >= r) & same32
    ind = np.zeros((128, 128), np.float32)
    for j in range(4):
        ind[:, j] = (t // 32 == j)
    ind[:, 4] = (t < 64)
    ind[:, 5] = (t >= 64)
    m[:, 10] = ind
    return m


PNAMES = ["w_ada", "b_ada", "norm1_w", "w_in", "gdn_conv_w", "gdn_a_log", "gdn_dt_bias", "gdn_norm_w",
          "hgrn_lb_param", "hgrn_norm_w", "ssd_conv_w", "ssd_conv_b", "ssd_a_log", "ssd_dt_bias", "ssd_d",
          "ssd_norm_w", "w_br_a", "w_br_b", "w_br_c", "w_out", "norm2_w", "ffn_w_up", "ffn_conv_w",
          "ffn_conv_b", "ffn_w_down", "final_norm_w"]
PSHAPES = {
    "w_ada": (D, 6 * D), "b_ada": (6 * D,), "norm1_w": (D,), "w_in": (D, NIN), "gdn_conv_w": (4, 1536),
    "gdn_a_log": (4,), "gdn_dt_bias": (4,), "gdn_norm_w": (128,), "hgrn_lb_param": (512,),
    "hgrn_norm_w": (128,), "ssd_conv_w": (4, 1024), "ssd_conv_b": (1024,), "ssd_a_log": (8,),
    "ssd_dt_bias": (8,), "ssd_d": (8,), "ssd_norm_w": (512,), "w_br_a": (512, D), "w_br_b": (512, D),
    "w_br_c": (512, D), "w_out": (D, D), "norm2_w": (D,), "ffn_w_up": (D, F2), "ffn_conv_w": (3, F2),
    "ffn_conv_b": (F2,), "ffn_w_down": (FH, D),
}


def build(S, NSEQ, DEPTH=2, T=256, dbg=False, stop=99):
    NS = T // 128
    NT = S // T
    assert S % T == 0
    nc = bass.Bass("TRN2", target_bir_lowering=False)
    kb = KB(nc)
    x_d = kb.dram("x", [NSEQ * S, D], F32, "ExternalInput")
    c_d = kb.dram("c", [NSEQ, D], F32, "ExternalInput")
    cm_d = kb.dram("cmask", [128, 11, 128], F32, "ExternalInput")
    P = {}
    for n in PNAMES:
        if n == "final_norm_w":
            P[n] = kb.dram(n, [D], F32, "ExternalInput")
        else:
            P[n] = kb.dram(n, [DEPTH] + list(PSHAPES[n]), F32, "ExternalInput")
    out_d = kb.dram("out", [NSEQ * S, D], F32, "ExternalOutput")
    dbg_d = {}

    def dbgout(name, view, shape):
        if not dbg:
            return None
        dbg_d[name] = kb.dram("dbg_" + name, list(shape), F32, "ExternalOutput")
        return kb.dma(dbg_d[name].v, view)

    win_s = [kb.dram("win_s%d" % l, [128, 8, NIN], BF16, "Internal") for l in range(DEPTH)]
    wbr_s = [[kb.dram("wbr_s%d_%d" % (l, b), [128, 4, D], BF16, "Internal") for b in range(3)] for l in range(DEPTH)]
    wout_s = [kb.dram("wout_s%d" % l, [128, 8, D], BF16, "Internal") for l in range(DEPTH)]
    wup_s = [kb.dram("wup_s%d" % l, [128, 8, F2], BF16, "Internal") for l in range(DEPTH)]
    wdn_s = [kb.dram("wdn_s%d" % l, [128, 22, D], BF16, "Internal") for l in range(DEPTH)]

    tickets = []
    cm = kb.sb([128, 11, 128], F32, "cm")
    kb.dma(cm.v, cm_d.v)
    IDENT = cm[:, 0, :]
    ONES = cm[:, 1, :]
    M_INCL64 = cm[:, 2, :]
    M_UPALL = cm[:, 3, :]
    M_TRIALL = cm[:, 4, :]
    M_BTRI64 = cm[:, 5, :]
    M_BUP64 = cm[:, 6, :]
    M_IND64 = [cm[:, 7, :], cm[:, 8, :]]
    M_INCL32 = cm[:, 9, :]
    INDC = cm[:, 10, :]
    ones_bf = kb.sb([128, 128], BF16, "ones_bf")
    kb.memset(ones_bf.v, 1.0)
    resetm = kb.sb([128, T], F32, "resetm")
    kb.memset(resetm.v, 1.0)
    kb.memset(resetm.v.re("p (c r) -> p c r", r=32)[:, :, 0:1], 0.0)

    banks = [kb.ps([128, 512], F32, "bank%d" % i) for i in range(8)]
    bstate = {"i": 0}

    def bank():
        b = banks[bstate["i"]]
        bstate["i"] = (bstate["i"] + 1) % 6
        return b

    def pload(name, l, pattern, shape, **kw):
        b = kb.sb(shape, F32, "%s_%d" % (name, l))
        src = P[name].ap[l] if name != "final_norm_w" else P[name].ap
        C = shape[1]
        for c in range(C):
            if len(shape) == 3:
                kb.dma(b[:, c, :], V((P[name],), src[:, c * 128:(c + 1) * 128].rearrange("k p -> p k")),
                       allow_slow_non_contiguous=True)
            else:
                kb.dma(b[:, c:c + 1], V((P[name],), src[c * 128:(c + 1) * 128].rearrange("(p o) -> p o", o=1)),
                       allow_slow_non_contiguous=True)
        return b

    def pbc(name, l, n):
        b = kb.sb([128, n], F32, "%s_bc%d" % (name, l))
        kb.dma(b.v, V((P[name],), P[name].ap[l].partition_broadcast(128)))
        return b

    cT = kb.sb([128, 8, NSEQ], F32, "cT")
    for s_ in range(NSEQ):
        kb.dma(cT[:, :, s_], V((c_d,), c_d.ap[s_].rearrange("(k p) -> p k", p=128)), allow_slow_non_contiguous=True)
    cact = kb.sb([128, 8, NSEQ], F32, "cact")
    kb.act(cact.v, cT.v, AF.Silu)
    fnw = pload("final_norm_w", 0, "(c p) -> p c", [128, 8], p=128)

    LP = []
    stage_f = [kb.sb([128, 1024], F32, "stf%d" % i) for i in range(2)]
    stage_b = [kb.sb([128, 1024], BF16, "stb%d" % i) for i in range(2)]
    stg = {"i": 0}
    for l in range(DEPTH):
        L = {}
        L["n1"] = pload("norm1_w", l, "(c p) -> p c", [128, 8], p=128)
        L["n2"] = pload("norm2_w", l, "(c p) -> p c", [128, 8], p=128)
        L["bada"] = pload("b_ada", l, "(c p) -> p c", [128, 48], p=128)
        L["gcw"] = pload("gdn_conv_w", l, "k (c p) -> p c k", [128, 12, 4], p=128)
        L["scw"] = pload("ssd_conv_w", l, "k (c p) -> p c k", [128, 8, 4], p=128)
        L["scb"] = pload("ssd_conv_b", l, "(c p) -> p c", [128, 8], p=128)
        L["fcw"] = pload("ffn_conv_w", l, "k (c p) -> p c k", [128, 44, 3], p=128)
        L["fcb"] = pload("ffn_conv_b", l, "(c p) -> p c", [128, 44], p=128)
        L["gnw"] = pload("gdn_norm_w", l, "(c p) -> p c", [128, 1], p=128)
        L["hnw"] = pload("hgrn_norm_w", l, "(c p) -> p c", [128, 1], p=128)
        L["snw"] = pload("ssd_norm_w", l, "(c p) -> p c", [128, 4], p=128)
        alog = kb.sb([128, 12], F32, "alog%d" % l)
        kb.dma(alog[:, 0:4], V((P["gdn_a_log"],), P["gdn_a_log"].ap[l].partition_broadcast(128)))
        kb.dma(alog[:, 4:12], V((P["ssd_a_log"],), P["ssd_a_log"].ap[l].partition_broadcast(128)))
        dtb = kb.sb([128, 12], F32, "dtb%d" % l)
        kb.dma(dtb[:, 0:4], V((P["gdn_dt_bias"],), P["gdn_dt_bias"].ap[l].partition_broadcast(128)))
        kb.dma(dtb[:, 4:12], V((P["ssd_dt_bias"],), P["ssd_dt_bias"].ap[l].partition_broadcast(128)))
        nea = kb.sb([128, 12], F32, "nea%d" % l)
        kb.act(nea.v, alog.v, AF.Exp)
        kb.ts(nea.v, nea.v, -1.0, ALU.mult)
        L["nea"] = nea
        L["dtb"] = dtb
        dsk = pbc("ssd_d", l, 8)
        dfull = kb.sb([128, 8, 64], F32, "dfull%d" % l)
        kb.copy(dfull.v, dsk.v.un(2).bc([128, 8, 64]))
        L["dfull"] = dfull
        LP.append(L)

    lbp = [pload("hgrn_lb_param", l, "(h p) -> p h", [128, 4], p=128) for l in range(DEPTH)]
    persist = [{k_: kb.sb([128, 32], F32, "ps_%s_%d" % (k_, s_)) for k_ in ("g", "spl", "bt", "e1", "e2m", "e3")} for s_ in range(2)]
    gepool = [kb.sb([128, 32], F32, "gepool%d" % i) for i in range(2)]
    lbe = [kb.sb([128, 4], F32, "lbe%d" % l) for l in range(DEPTH)]
    for l in range(DEPTH):
        kb.act(lbe[l].v, lbp[l].v, AF.Exp)
    lsum = kb.sb([128, 4], F32, "lsum")
    kb.copy(lsum.v, lbe[0].v)
    for l in range(1, DEPTH):
        kb.tt(lsum.v, lsum.v, lbe[l].v, ALU.add)
    lrs = kb.sb([128, 4], F32, "lrs")
    kb.recip(lrs.v, lsum.v)
    cum = kb.sb([128, 4], F32, "lcum")
    kb.memset(cum.v, 0.0)
    for l in range(DEPTH):
        lb = kb.sb([128, 4], F32, "lb%d" % l)
        oml = kb.sb([128, 4], F32, "oml%d" % l)
        if l > 0:
            sm = kb.sb([128, 4], F32, "lsm%d" % l)
            kb.tt(sm.v, lbe[l].v, lrs.v, ALU.mult)
            kb.tt(cum.v, cum.v, sm.v, ALU.add)
        kb.copy(lb.v, cum.v)
        kb.ts(oml.v, lb.v, -1.0, ALU.mult, 1.0, ALU.add)
        LP[l]["lb"] = lb
        LP[l]["oml"] = oml

    for l in range(DEPTH):
        L = LP[l]
        mod = kb.sb([128, 48, NSEQ], F32, "mod%d" % l)
        for fc in range(48):
            sf = stage_f[stg["i"] % 2]; stg["i"] += 1
            sfv = sf.v.re("p (k n) -> p k n", n=128)
            kb.dma(sfv, V((P["w_ada"],), P["w_ada"].ap[l][:, fc * 128:(fc + 1) * 128].rearrange("(k p) n -> p k n", p=128)))
            pb = bank()
            for k in range(8):
                kb.mm(pb[:, 0:NSEQ], sfv[:, k, :], cact[:, k, :], start=(k == 0), stop=(k == 7))
            kb.ts(mod[:, fc, :], pb[:, 0:NSEQ], L["bada"][:, fc:fc + 1], ALU.add)
        L["mod"] = mod
        a1 = kb.sb([128, 8, NSEQ], F32, "a1_%d" % l)
        a2 = kb.sb([128, 8, NSEQ], F32, "a2_%d" % l)
        for s_ in range(NSEQ):
            kb.stt(a1[:, :, s_], mod[:, 8:16, s_], 1.0, L["n1"].v, ALU.add, ALU.mult)
            kb.stt(a2[:, :, s_], mod[:, 32:40, s_], 1.0, L["n2"].v, ALU.add, ALU.mult)
        L["a1"], L["a2"] = a1, a2

    DEPTH_C = DEPTH if stop >= 1 else 0
    cast_eng = ["act", "dve", "pool"]
    cst = {"i": 0}

    def cast_weight(src_buf, src_ap, K, N, dst, scale=None):
        KC = K // 128
        for k in range(KC):
            for n0 in range(0, N, 1024):
                w = min(1024, N - n0)
                i = stg["i"] % 2; stg["i"] += 1
                sf, sbb = stage_f[i], stage_b[i]
                kb.dma(sf[:, 0:w], V((src_buf,), src_ap[k * 128:(k + 1) * 128, n0:n0 + w]),
                       eng=("sp" if i == 0 else "pool"))
                if scale is not None:
                    kb.ts(sbb[:, 0:w], sf[:, 0:w], scale[:, k:k + 1], ALU.mult)
                else:
                    e = cast_eng[cst["i"] % 3]; cst["i"] += 1
                    kb.copy(sbb[:, 0:w], sf[:, 0:w], eng=e)
                kb.dma(dst[:, k, n0:n0 + w], sbb[:, 0:w], eng=("sp" if i == 1 else "pool"))

    for l in range(DEPTH_C):
        L = LP[l]
        cast_weight(P["w_in"], P["w_in"].ap[l], D, NIN, win_s[l])
        cast_weight(P["w_br_a"], P["w_br_a"].ap[l], 512, D, wbr_s[l][0], scale=L["gnw"][:, 0:1].bc([128, 4]))
        cast_weight(P["w_br_b"], P["w_br_b"].ap[l], 512, D, wbr_s[l][1], scale=L["hnw"][:, 0:1].bc([128, 4]))
        cast_weight(P["w_br_c"], P["w_br_c"].ap[l], 512, D, wbr_s[l][2], scale=L["snw"].v)
        cast_weight(P["w_out"], P["w_out"].ap[l], D, D, wout_s[l])
        cast_weight(P["ffn_w_up"], P["ffn_w_up"].ap[l], D, F2, wup_s[l])
        cast_weight(P["ffn_w_down"], P["ffn_w_down"].ap[l], FH, D, wdn_s[l])
        wsm = kb.sb([128, 8, 16], BF16, "wsm%d" % l)
        kb.dma(wsm[:, :, 0:4], win_s[l][:, :, O_A:O_A + 4], allow_slow_non_contiguous=True)
        kb.dma(wsm[:, :, 4:12], win_s[l][:, :, O_DT:O_DT + 8], allow_slow_non_contiguous=True)
        kb.dma(wsm[:, :, 12:16], win_s[l][:, :, O_B:O_B + 4], allow_slow_non_contiguous=True)
        L["wsm"] = wsm

    NW = 4
    wbufs = [kb.sb([128, 4096], BF16, "wbuf%d" % i) for i in range(NW)]
    wst = {"i": 0}

    def wload(src_view, kc, ncols):
        b = wbufs[wst["i"] % NW]; wst["i"] += 1
        kb.dma(b.v[:, 0:kc * ncols].re("p (k n) -> p k n", n=ncols), src_view)
        return b.v[:, 0:kc * ncols].re("p (k n) -> p k n", n=ncols)

    xT = kb.sb([128, 8, T], F32, "xT")
    xio = kb.sb([128, NS, D], F32, "xio")
    hT = kb.sb([128, 8, T], BF16, "hT")
    macc = kb.sb([128, 8, T], F32, "macc")
    mergedT = kb.sb([128, 8, T], BF16, "mergedT")
    obr = [kb.sb([128, 4, T], BF16, "obr%d" % i) for i in range(3)]
    hidT = kb.sb([128, 22, T], BF16, "hidT")
    ftmp = [kb.sb([128, T + 4], F32, "ftmp%d" % i) for i in range(14)]
    fst = {"i": 0}

    def ft():
        b = ftmp[fst["i"] % len(ftmp)]; fst["i"] += 1
        return b
    ft_global = ft

    abf = [kb.sb([128, 4, T], BF16, "abf%d" % i) for i in range(3)]
    tokbf = [[kb.sb([128, 512], BF16, "tokbf%d_%d" % (i, s_)) for s_ in range(NS)] for i in range(4)]
    tokf = [[kb.sb([128, 512], F32, "tokf%d_%d" % (i, s_)) for s_ in range(NS)] for i in range(2)]
    sq128 = [kb.sb([128, 128], F32, "sq%d" % i) for i in range(2)]
    sqst = {"i": 0}

    def sq():
        b = sq128[sqst["i"] % len(sq128)]; sqst["i"] += 1
        return b

    def alias(parent, ap, name):
        b = Buf(name, parent.t, ap)
        b.lw = parent.lw
        b.rd = list(parent.rd)
        return b

    sq_extra = [alias(stage_f[i // 8], stage_f[i // 8].ap[:, (i % 8) * 128:(i % 8 + 1) * 128], "sqx%d" % i) for i in range(16)]
    bf_extra = [alias(stage_b[i // 8], stage_b[i // 8].ap[:, (i % 8) * 128:(i % 8 + 1) * 128], "bfx%d" % i) for i in range(16)]
    gdn_tp = [{"sq": sq_extra[h * 4:h * 4 + 4] + [kb.sb([128, 128], F32, "gsq%d_%d" % (h, i)) for i in range(2)],
               "bf": bf_extra[h * 4:h * 4 + 4]} for h in range(4)]

    for h_ in range(4):
        kb.memset(gdn_tp[h_]["bf"][3].v, 0.0)

    def run_threads(gens):
        gens = list(gens)
        while gens:
            for g in list(gens):
                try:
                    next(g)
                except StopIteration:
                    gens.remove(g)

    sqb = [kb.sb([128, 128], BF16, "sqb%d" % i) for i in range(4)]
    sqbst = {"i": 0}

    def sqbf():
        b = sqb[sqbst["i"] % len(sqb)]; sqbst["i"] += 1
        return b

    big4 = [kb.sb([128, 4, 128], F32, "big4_%d" % i) for i in range(4)]
    big4b = [kb.sb([128, 4, 128], BF16, "big4b_%d" % i) for i in range(2)]
    small = [kb.sb([128, 32], F32, "small%d" % i) for i in range(16)]
    smst = {"i": 0}

    def sm_():
        b = small[smst["i"] % len(small)]; smst["i"] += 1
        return b

    qgz = [[kb.sb([128, 640], BF16, "qgz%d_%d" % (i, s_)) for s_ in range(NS)] for i in range(2)]
    for i in range(2):
        for s_ in range(NS):
            kb.memset(qgz[i][s_].v, 0.0)
    kendz = [[kb.sb([128, 4, 128], BF16, "kendz%d_%d" % (i, s_)) for s_ in range(NS)] for i in range(2)]
    vnew = kb.sb([128, 128], BF16, "vnew")
    kb.memset(vnew.v, 0.0)
    otok = [kb.sb([128, 256], F32, "otok%d" % i) for i in range(6)]
    otst = {"i": 0}

    def ot():
        b = otok[otst["i"] % 6]; otst["i"] += 1
        return b

    ST = []
    for l in range(DEPTH):
        s_ = {}
        s_["gS"] = kb.sb([128, 4, 128], F32, "gS%d" % l)
        s_["gSb"] = kb.sb([128, 4, 128], BF16, "gSb%d" % l)
        s_["hS"] = kb.sb([128, 4, 128], F32, "hS%d" % l)
        s_["hSb"] = kb.sb([128, 4, 128], BF16, "hSb%d" % l)
        s_["sS"] = kb.sb([128, 2, 256], F32, "sS%d" % l)
        s_["sSb"] = kb.sb([128, 2, 256], BF16, "sSb%d" % l)
        s_["gh"] = kb.sb([128, 12, 3], F32, "gh%d" % l)
        s_["sh"] = kb.sb([128, 8, 3], F32, "sh%d" % l)
        s_["fh"] = kb.sb([128, 44, 2], F32, "fh%d" % l)
        ST.append(s_)

    def rms_mod(a_v, b_v, out_bf=None, out_f32=None):
        pb = bank()
        for c in range(8):
            s2 = ft()
            kb.act(s2[:, 0:T], xT[:, c, :], AF.Square)
            s2b = abf[2]
            kb.copy(s2b[:, c % 4, :], s2[:, 0:T], eng="pool")
            kb.mm(pb[:, 0:T], ones_bf.v, s2b[:, c % 4, :], start=(c == 0), stop=(c == 7))
        rs = ft()
        kb.act(rs[:, 0:T], pb[:, 0:T], AF.Sqrt, scale=1.0 / D, bias=EPS)
        kb.recip(rs[:, 0:T], rs[:, 0:T])
        for c in range(8):
            t_ = ft()
            kb.tt(t_[:, 0:T], xT[:, c, :], rs[:, 0:T], ALU.mult)
            if out_bf is not None:
                kb.act(out_bf[:, c, :], t_[:, 0:T], AF.Identity, scale=a_v[:, c:c + 1], bias=b_v[:, c:c + 1])
            else:
                kb.act(out_f32[:, c, :], t_[:, 0:T], AF.Identity, scale=a_v[:, c:c + 1])

    def fm_proj(wv, col0, rhs_buf, KC, evac):
        pb = bank()
        for k in range(KC):
            kb.mm(pb[:, 0:T], wv[:, k, col0:col0 + 128], rhs_buf[:, k, :], start=(k == 0), stop=(k == KC - 1))
        evac(pb[:, 0:T])

    def tm_proj(wv, col0, ncols, sub, evac):
        pb = bank()
        for k in range(8):
            kb.mm(pb[:, 0:ncols], hT[:, k, sub * 128:(sub + 1) * 128], wv[:, k, col0:col0 + ncols],
                  start=(k == 0), stop=(k == 7))
        evac(pb[:, 0:ncols])

    def slot_alloc(slot, n=7):
        lst = ftmp[slot * n:(slot + 1) * n]
        stt_ = {"i": 0}

        def al():
            b = lst[stt_["i"] % n]; stt_["i"] += 1
            return b
        return al

    def run_slots(factories, nslots=2, offset=1):
        pending = list(factories)[::-1]
        slots = [None] * nslots
        steps = [0] * nslots
        while pending or any(g is not None for g in slots):
            for i in range(nslots):
                if slots[i] is None:
                    if pending and all(slots[j] is None or steps[j] >= offset for j in range(nslots) if j != i):
                        slots[i] = pending.pop()(i)
                        steps[i] = 0
                    else:
                        continue
                try:
                    next(slots[i])
                    steps[i] += 1
                except StopIteration:
                    slots[i] = None

    def conv(ps_v, hist_v, wv, ntap, bias_v=None, ft=None):
        if ft is None:
            ft = ft_global
        H = ntap - 1
        buf = ft()
        kb.copy(buf[:, H:H + T], ps_v, eng="act")
        kb.copy(buf[:, 0:H], hist_v, eng="pool")
        acc = ft()
        kb.act(acc[:, 0:T], ps_v, AF.Identity, scale=wv[:, H:H + 1], bias=(bias_v if bias_v is not None else 0.0))
        for k in range(0, H):
            kb.stt(acc[:, 0:T], buf[:, k:k + T], wv[:, k:k + 1], acc[:, 0:T], ALU.mult, ALU.add)
        kb.copy(hist_v, buf[:, T:T + H], eng="pool")
        return acc

    def norm_gate_T(o_v, width, gate_v, dst_list, junk=None, og=None):
        junk = junk or ot()
        ss = sm_()
        kb.act(junk[:, 0:width], o_v, AF.Square, accum=ss[:, 0:1])
        kb.act(ss[:, 1:2], ss[:, 0:1], AF.Sqrt, scale=1.0 / width, bias=EPS)
        kb.recip(ss[:, 2:3], ss[:, 1:2])
        og = og or ot()
        if gate_v is not None:
            kb.stt(og[:, 0:width], o_v, ss[:, 2:3], gate_v, ALU.mult, ALU.mult)
        else:
            kb.ts(og[:, 0:width], o_v, ss[:, 2:3], ALU.mult)
        for i, dst in enumerate(dst_list):
            pb = bank()
            kb.tr(pb[:, 0:128], og[:, i * 128:(i + 1) * 128], IDENT)
            kb.copy(dst, pb[:, 0:128], eng="act")

    def layer_body(l, seq, first_tile):
        L = LP[l]
        st = ST[l]
        mod = L["mod"]
        W = win_s[l]
        if first_tile:
            for k_ in ("gS", "gSb", "hS", "hSb", "sS", "sSb", "gh", "sh", "fh"):
                kb.memset(st[k_].v, 0.0)
        rms_mod(L["a1"][:, :, seq], mod[:, 0:8, seq], out_bf=hT)

        if stop < 3:
            return
        gda = [None] * NS
        dtv = [None] * NS
        beta = [None] * NS
        nbeta = [None] * NS
        for sub in range(NS):
            pb = bank()
            for k in range(8):
                kb.mm(pb[:, 0:16], hT[:, k, sub * 128:(sub + 1) * 128], L["wsm"][:, k, :], start=(k == 0), stop=(k == 7))
            t1 = sm_()
            kb.tt(t1[:, 0:12], pb[:, 0:12], L["dtb"].v, ALU.add)
            kb.act(t1[:, 0:12], t1[:, 0:12], AF.Exp)
            spl = persist[sub]["spl"]
            kb.act(spl[:, 0:12], t1[:, 0:12], AF.Ln, bias=1.0)
            g_ = persist[sub]["g"]
            kb.tt(g_[:, 0:12], spl[:, 0:12], L["nea"].v, ALU.mult)
            bt = persist[sub]["bt"]
            kb.act(bt[:, 0:4], pb[:, 12:16], AF.Sigmoid)
            kb.ts(bt[:, 4:8], bt[:, 0:4], -1.0, ALU.mult)
            gda[sub], dtv[sub], beta[sub], nbeta[sub] = g_, spl, bt[:, 0:4], bt[:, 4:8]

        expGA = [None] * NS
        eend = [None] * NS
        gend = [None] * NS
        for sub in range(NS):
            g_ = gda[sub]
            pb = bank()
            kb.mm(pb[:, 0:12], M_BTRI64, g_[:, 0:12])
            kb.mm(pb[:, 16:28], M_BUP64, g_[:, 0:12])
            kb.mm(pb[:, 32:44], M_IND64[0], g_[:, 0:12])
            kb.mm(pb[:, 48:60], M_IND64[1], g_[:, 0:12])
            e1 = persist[sub]["e1"]
            kb.act(e1[:, 0:12], pb[:, 0:12], AF.Exp)
            e2 = sm_()
            kb.act(e2[:, 0:12], pb[:, 16:28], AF.Exp)
            e2m = persist[sub]["e2m"]
            kb.ts(e2m[:, 0:12], e2[:, 0:12], INDC[:, 4:5], ALU.mult)
            kb.ts(e2m[:, 12:24], e2[:, 0:12], INDC[:, 5:6], ALU.mult)
            e3 = persist[sub]["e3"]
            kb.act(e3[:, 0:12], pb[:, 32:44], AF.Exp)
            kb.act(e3[:, 12:24], pb[:, 48:60], AF.Exp)
            expGA[sub], eend[sub], gend[sub] = e1, e2m, e3

        def decay_T(sub, h0):
            rh = big4[0]
            kb.tt(rh.v, M_TRIALL.un(1).bc([128, 4, 128]), gda[sub][:, h0:h0 + 4].un(2).bc([128, 4, 128]), ALU.mult)
            pb = bank()
            kb.mm(pb[:, 0:512], M_UPALL, rh.v.re("p h l -> p (h l)"))
            dec = big4[1]
            kb.act(dec.v.re("p h l -> p (h l)"), pb[:, 0:512], AF.Exp)
            return dec

        if stop < 4:
            return
        wq = wload(W[:, :, O_Q:O_Q + 512], 8, 512)
        wk = wload(W[:, :, O_K:O_K + 512], 8, 512)
        wv_ = wload(W[:, :, O_V:O_V + 512], 8, 512)
        qT, kT = abf[0], abf[1]
        vtok, kg, ke0, ke1 = tokbf[0], tokbf[1], tokbf[2], tokbf[3]

        def gdn_chunk(fc):
            def gen(slot):
                al = slot_alloc(slot, 4)
                which, h = fc // 4, fc % 4
                wv = (wq, wk, wv_)[which]
                pb = bank()
                for k in range(8):
                    kb.mm(pb[:, 0:T], wv[:, k, h * 128:(h + 1) * 128], hT[:, k, :], start=(k == 0), stop=(k == 7))
                yield
                acc = conv(pb[:, 0:T], st["gh"][:, fc, :], L["gcw"][:, fc, :], 4, ft=al)
                s_ = al()
                kb.act(s_[:, 0:T], acc[:, 0:T], AF.Silu)
                if which < 2:
                    s2 = al()
                    kb.act(s2[:, 0:T], s_[:, 0:T], AF.Square)
                    s2b = abf[2][:, slot, :]
                    kb.copy(s2b, s2[:, 0:T], eng="pool")
                    pb = bank()
                    kb.mm(pb[:, 0:T], ones_bf.v, s2b)
                    yield
                    rs = al()
                    if which == 0:
                        kb.act(rs[:, 0:T], pb[:, 0:T], AF.Sqrt, scale=128.0, bias=128.0 * EPS)
                    else:
                        kb.act(rs[:, 0:T], pb[:, 0:T], AF.Sqrt, bias=EPS)
                    kb.recip(rs[:, 0:T], rs[:, 0:T])
                    if which == 0:
                        kb.tt(qT[:, h, :], s_[:, 0:T], rs[:, 0:T], ALU.mult)
                        return
                    knf = al()
                    kb.tt(knf[:, 0:T], s_[:, 0:T], rs[:, 0:T], ALU.mult)
                    kb.copy(kT[:, h, :], knf[:, 0:T], eng="act")
                    pbs = []
                    for sub in range(NS):
                        pb2 = bank()
                        kb.tr(pb2[:, 0:128], knf[:, sub * 128:(sub + 1) * 128], IDENT)
                        pbs.append(pb2)
                    yield
                    for sub in range(NS):
                        pb2 = pbs[sub]
                        kb.act(kg[sub][:, h * 128:(h + 1) * 128], pb2[:, 0:128], AF.Identity, scale=expGA[sub][:, h:h + 1])
                        kb.act(ke0[sub][:, h * 128:(h + 1) * 128], pb2[:, 0:128], AF.Identity, scale=eend[sub][:, h:h + 1])
                        kb.act(ke1[sub][:, h * 128:(h + 1) * 128], pb2[:, 0:128], AF.Identity, scale=eend[sub][:, 12 + h:13 + h])
                else:
                    pbs = []
                    for sub in range(NS):
                        pb2 = bank()
                        kb.tr(pb2[:, 0:128], s_[:, sub * 128:(sub + 1) * 128], IDENT)
                        pbs.append(pb2)
                    yield
                    for sub in range(NS):
                        kb.copy(vtok[sub][:, h * 128:(h + 1) * 128], pbs[sub][:, 0:128], eng="act")
            return gen

        order = [0, 4, 1, 5, 2, 6, 3, 7, 8, 9, 10, 11]
        run_slots([gdn_chunk(fc) for fc in order], nslots=3, offset=1)
        if stop < 4.1:
            return
        wz = wload(W[:, :, O_Z:O_Z + 512], 8, 512)
        siluz = tokf[0]
        for sub in range(NS):
            tm_proj(wz, 0, 512, sub, lambda ps_v, sub=sub: kb.act(siluz[sub].v, ps_v, AF.Silu))

        if stop < 4.2:
            return

        def gdn_head(sub, h, decm, decs, TP):
            sl = slice(sub * 128, (sub + 1) * 128)
            hs = slice(h * 128, (h + 1) * 128)
            PA, PTA, PB, PTB, X, XT = TP["sq"]
            Xb, wT, attnT, vn = TP["bf"]
            pk = bank()
            kb.mm(pk[:, 0:128], kT[:, h, sl], kT[:, h, sl])
            yield
            kb.stt(PA.v, pk[:, 0:128], nbeta[sub][:, h:h + 1], decs[:, h, :], ALU.mult, ALU.mult)
            pt = bank()
            kb.tr(pt[:, 0:128], PA.v, IDENT)
            yield
            kb.copy(PTA.v, pt[:, 0:128], eng="act")
            kb.tt(X.v, PA.v, IDENT, ALU.add)
            kb.tt(XT.v, PTA.v, IDENT, ALU.add, eng="pool")
            Pm, PT, Pn, PTn = PA, PTA, PB, PTB
            for lev in range(5):
                last = (lev == 4)
                p2 = bank()
                kb.mm(p2[:, 0:128], PT.v, Pm.v)
                if not last:
                    kb.mm(p2[:, 128:256], Pm.v, PT.v)
                yield
                kb.copy(Pn.v, p2[:, 0:128], eng="act")
                if not last:
                    kb.copy(PTn.v, p2[:, 128:256], eng="dve")
                px = bank()
                kb.mm(px[:, 0:128], XT.v, Pn.v)
                if not last:
                    kb.mm(px[:, 128:256], Pn.v, XT.v)
                yield
                if last:
                    kb.tt(Xb.v, px[:, 0:128], X.v, ALU.add)
                else:
                    kb.tt(X.v, px[:, 0:128], X.v, ALU.add)
                    kb.tt(XT.v, px[:, 128:256], XT.v, ALU.add)
                    Pm, PT, Pn, PTn = Pn, PTn, Pm, PT
            if stop < 4.4:
                return
            bu, oa, o_ = PA, PTA, PB
            pu = bank()
            kb.mm(pu[:, 0:128], Xb.v, vtok[sub][:, hs])
            kb.mm(pu[:, 128:256], kg[sub][:, hs], Xb.v)
            kb.mm(pu[:, 256:384], kT[:, h, sl], qT[:, h, sl])
            yield
            kb.act(bu.v, pu[:, 0:128], AF.Identity, scale=beta[sub][:, h:h + 1])
            kb.copy(wT.v, pu[:, 128:256], eng="act")
            kb.tt(attnT.v, pu[:, 256:384], decm[:, h, :], ALU.mult)
            for c in range(2):
                rows = slice(c * 64, (c + 1) * 64)
                p1 = bank()
                kb.mm(p1[:, 0:128], wT.v, st["gSb"][:, h, :])
                kb.mm(p1[:, 128:256], qT[:, h, sl], st["gSb"][:, h, :])
                yield
                kb.stt(vn[rows, :], p1[rows, 0:128], nbeta[sub][rows, h:h + 1], bu[rows, :], ALU.mult, ALU.add)
                kb.act(oa[rows, :], p1[rows, 128:256], AF.Identity, scale=expGA[sub][rows, h:h + 1])
                pbq = bank()
                kb.mm(pbq[:, 0:128], attnT.v, vn.v)
                ke = (ke0, ke1)[c]
                kb.mm(pbq[:, 128:256], ke[sub][:, hs], vn.v)
                yield
                kb.tt(o_[rows, :], oa[rows, :], pbq[rows, 0:128], ALU.add)
                kb.stt(st["gS"][:, h, :], st["gS"][:, h, :], gend[sub][:, c * 12 + h:c * 12 + h + 1], pbq[:, 128:256],
                       ALU.mult, ALU.add)
                kb.copy(st["gSb"][:, h, :], st["gS"][:, h, :], eng="act")
            norm_gate_T(o_.v, 128, siluz[sub][:, hs], [obr[0][:, h, sl]])

        for sub in range(NS):
            dec = decay_T(sub, 0)
            decm = big4[2]
            kb.tt(decm.v, dec.v, M_INCL64.un(1).bc([128, 4, 128]), ALU.mult)
            decs = big4[3]
            kb.tt(decs.v, decm.v, IDENT.un(1).bc([128, 4, 128]), ALU.subtract)
            if stop < 4.3:
                continue
            run_threads([gdn_head(sub, h, decm, decs, gdn_tp[h]) for h in range(4)])

        if stop < 5:
            return
        whi = wload(W[:, :, O_HI:O_HI + 512], 8, 512)
        vh = tokbf[0]
        for sub in range(NS):
            tm_proj(whi, 0, 512, sub, lambda ps_v, sub=sub: kb.copy(vh[sub].v, ps_v, eng="act"))
        whg = wload(W[:, :, O_HG:O_HG + 512], 8, 512)
        silug = tokf[0]
        for sub in range(NS):
            tm_proj(whg, 0, 512, sub, lambda ps_v, sub=sub: kb.act(silug[sub].v, ps_v, AF.Silu))
        whq = wload(W[:, :, O_HQ:O_HQ + 512], 8, 512)
        whf = wload(W[:, :, O_HF:O_HF + 512], 8, 512)
        NCH = T // 32

        def hgrn_head(h):
            def gen(slot):
                al = slot_alloc(slot)
                hs = slice(h * 128, (h + 1) * 128)
                q_, f_, t3, G, t5 = al(), al(), al(), al(), al()
                pq = bank()
                for k in range(8):
                    kb.mm(pq[:, 0:T], whq[:, k, hs], hT[:, k, :], start=(k == 0), stop=(k == 7))
                pf = bank()
                for k in range(8):
                    kb.mm(pf[:, 0:T], whf[:, k, hs], hT[:, k, :], start=(k == 0), stop=(k == 7))
                yield
                kb.act(q_[:, 0:T], pq[:, 0:T], AF.Silu)
                kb.act(f_[:, 0:T], pf[:, 0:T], AF.Sigmoid)
                kb.ts(f_[:, 0:T], f_[:, 0:T], L["oml"][:, h:h + 1], ALU.mult, L["lb"][:, h:h + 1], ALU.add)
                kb.act(t3[:, 0:T], f_[:, 0:T], AF.Ln)
                kk = f_
                kb.ts(kk[:, 0:T], f_[:, 0:T], -1.0, ALU.mult, 1.0, ALU.add)
                kb.scan(G[:, 0:T], resetm.v, t3[:, 0:T], 0.0, ALU.mult, ALU.add)
                yield
                G3 = G[:, 0:T].re("p (c r) -> p c r", r=32)
                Dm = t3
                kb.tt(Dm[:, 0:T].re("p (c r) -> p c r", r=32), G3, G3[:, :, 15:16].bc([128, NCH, 32]), ALU.subtract)
                kb.act(t5[:, 0:T], Dm[:, 0:T], AF.Exp)
                qt, kt = abf[0], abf[1]
                kb.tt(qt[:, slot, :], q_[:, 0:T], t5[:, 0:T], ALU.mult)
                yield
                kb.act(t5[:, 0:T], Dm[:, 0:T], AF.Exp, scale=-1.0)
                kb.tt(kt[:, slot, :], kk[:, 0:T], t5[:, 0:T], ALU.mult)
                yield
                kb.act(t5[:, 0:T], G[:, 0:T], AF.Exp)
                qz = qgz[slot]
                for sub in range(NS):
                    kb.tt(qz[sub].v.re("p (c r) -> p c r", r=160)[:, :, 0:32],
                          q_[:, sub * 128:(sub + 1) * 128].re("p (c r) -> p c r", r=32),
                          t5[:, sub * 128:(sub + 1) * 128].re("p (c r) -> p c r", r=32), ALU.mult)
                yield
                DL = t3
                kb.tt(DL[:, 0:T].re("p (c r) -> p c r", r=32), G3[:, :, 31:32].bc([128, NCH, 32]), G3, ALU.subtract)
                kb.act(t5[:, 0:T], DL[:, 0:T], AF.Exp)
                kend = t3
                kb.tt(kend[:, 0:T], kk[:, 0:T], t5[:, 0:T], ALU.mult)
                ge = gepool[slot]
                kb.act(ge[:, 0:NCH], G3[:, :, 31], AF.Exp)
                kz = kendz[slot]
                for sub in range(NS):
                    sl = slice(sub * 128, (sub + 1) * 128)
                    pk = bank()
                    kb.tr(pk[:, 0:128], kend[:, sl], IDENT)
                    psc = bank()
                    kb.mm(psc[:, 0:128], kt[:, slot, sl], qt[:, slot, sl])
                    yield
                    kb.tt(kz[sub].v, pk[:, 0:128].un(1).bc([128, 4, 128]), INDC[:, 0:4].un(2).bc([128, 4, 128]), ALU.mult)
                    attnT = sqb[slot]
                    kb.tt(attnT.v, psc[:, 0:128], M_INCL32, ALU.mult)
                    po = banks[6 + slot]
                    kb.mm(po[:, 0:128], attnT.v, vh[sub][:, hs], start=True, stop=False)
                    for c in range(4):
                        kb.mm(po[:, 0:128], qz[sub][:, c * 128:(c + 1) * 128], st["hSb"][:, h, :], start=False, stop=(c == 3))
                        pss = bank()
                        kb.mm(pss[:, 0:128], kz[sub][:, c, :], vh[sub][:, hs])
                        yield
                        kb.stt(st["hS"][:, h, :], st["hS"][:, h, :], ge[:, sub * 4 + c:sub * 4 + c + 1], pss[:, 0:128],
                               ALU.mult, ALU.add)
                        kb.copy(st["hSb"][:, h, :], st["hS"][:, h, :], eng="act")
                    norm_gate_T(po[:, 0:128], 128, silug[sub][:, hs], [obr[1][:, h, sl]])
            return gen

        run_slots([hgrn_head(h) for h in range(4)], nslots=2, offset=5)

        if stop < 6:
            return
        wsz = wload(W[:, :, O_SZ:O_SZ + 512], 8, 512)
        siluzs = tokf[0]
        for sub in range(NS):
            tm_proj(wsz, 0, 512, sub, lambda ps_v, sub=sub: kb.act(siluzs[sub].v, ps_v, AF.Silu))
        wsx = wload(W[:, :, O_SX:O_SX + 512], 8, 512)
        wsbc = wload(W[:, :, O_SB:O_SB + 512], 8, 512)
        xtok, xdt, btok = tokf[1], tokbf[0], tokbf[1]
        BT, CT = abf[0], abf[1]

        def ssd_chunk(fc):
            def gen(slot):
                al = slot_alloc(slot, 4)
                wv = wsx if fc < 4 else wsbc
                pb = bank()
                for k in range(8):
                    kb.mm(pb[:, 0:T], wv[:, k, (fc % 4) * 128:(fc % 4 + 1) * 128], hT[:, k, :], start=(k == 0), stop=(k == 7))
                yield
                acc = conv(pb[:, 0:T], st["sh"][:, fc, :], L["scw"][:, fc, :], 4, bias_v=L["scb"][:, fc:fc + 1], ft=al)
                s_ = al()
                kb.act(s_[:, 0:T], acc[:, 0:T], AF.Silu)
                if fc >= 6:
                    kb.copy(CT[:, fc - 6, :], s_[:, 0:T], eng="pool")
                    return
                if fc >= 4:
                    kb.copy(BT[:, fc - 4, :], s_[:, 0:T], eng="pool")
                pbs = []
                for sub in range(NS):
                    pb2 = bank()
                    kb.tr(pb2[:, 0:128], s_[:, sub * 128:(sub + 1) * 128], IDENT)
                    pbs.append(pb2)
                yield
                for sub in range(NS):
                    if fc < 4:
                        kb.copy(xtok[sub][:, fc * 128:(fc + 1) * 128], pbs[sub][:, 0:128], eng="act")
                    else:
                        g = fc - 4
                        kb.copy(btok[sub][:, g * 128:(g + 1) * 128], pbs[sub][:, 0:128], eng="act")
            return gen

        run_slots([ssd_chunk(fc) for fc in range(8)], nslots=3, offset=1)
        for sub in range(NS):
            kb.tt(xdt[sub].v.re("p (h d) -> p h d", d=64), xtok[sub].v.re("p (h d) -> p h d", d=64),
                  dtv[sub][:, 4:12].un(2).bc([128, 8, 64]), ALU.mult)

        def ssd_group(sub, g):
            def gen(slot):
                sl = slice(sub * 128, (sub + 1) * 128)
                gs = slice(g * 256, (g + 1) * 256)
                h0 = 4 + 4 * g
                rh, dec = big4[2 * slot], big4[2 * slot + 1]
                t1, y, t3 = otok[3 * slot], otok[3 * slot + 1], otok[3 * slot + 2]
                kb.tt(rh.v, M_TRIALL.un(1).bc([128, 4, 128]), gda[sub][:, h0:h0 + 4].un(2).bc([128, 4, 128]), ALU.mult)
                pd = bank()
                kb.mm(pd[:, 0:512], M_UPALL, rh.v.re("p h l -> p (h l)"))
                pcb = bank()
                kb.mm(pcb[:, 0:128], BT[:, g, sl], CT[:, g, sl])
                yield
                kb.act(dec.v.re("p h l -> p (h l)"), pd[:, 0:512], AF.Exp)
                cbm = sq128[slot]
                kb.tt(cbm.v, pcb[:, 0:128], M_INCL64, ALU.mult)
                at = big4b[slot]
                kb.tt(at.v, dec.v, cbm.v.un(1).bc([128, 4, 128]), ALU.mult)
                py = banks[6 + slot]
                for hh in range(4):
                    kb.mm(py[:, hh * 64:(hh + 1) * 64], at[:, hh, :], xdt[sub][:, (4 * g + hh) * 64:(4 * g + hh + 1) * 64])
                for c in range(2):
                    rows = slice(c * 64, (c + 1) * 64)
                    poff = bank()
                    kb.mm(poff[:, 0:256], CT[:, g, sl], st["sSb"][:, g, :])
                    xw = tokbf[2 + slot][sub]
                    kb.tt(xw[:, 0:256].re("p (h d) -> p h d", d=64), xdt[sub][:, gs].re("p (h d) -> p h d", d=64),
                          eend[sub][:, c * 12 + h0:c * 12 + h0 + 4].un(2).bc([128, 4, 64]), ALU.mult)
                    pst = bank()
                    kb.mm(pst[:, 0:256], btok[sub][:, g * 128:(g + 1) * 128], xw[:, 0:256])
                    yield
                    kb.tt(t1[rows, :].re("p (h d) -> p h d", d=64), poff[rows, 0:256].re("p (h d) -> p h d", d=64),
                          expGA[sub][rows, h0:h0 + 4].un(2).bc([64, 4, 64]), ALU.mult)
                    kb.tt(st["sS"][:, g, :].re("p (h d) -> p h d", d=64), st["sS"][:, g, :].re("p (h d) -> p h d", d=64),
                          gend[sub][:, c * 12 + h0:c * 12 + h0 + 4].un(2).bc([128, 4, 64]), ALU.mult, eng="pool")
                    kb.tt(st["sS"][:, g, :], st["sS"][:, g, :], pst[:, 0:256], ALU.add)
                    kb.copy(st["sSb"][:, g, :], st["sS"][:, g, :], eng="act")
                kb.tt(y.v, py[:, 0:256], t1.v, ALU.add)
                kb.tt(t3.v, xtok[sub][:, gs], L["dfull"].v.re("p h d -> p (h d)")[:, gs], ALU.mult, eng="pool")
                kb.tt(y.v, y.v, t3.v, ALU.add)
                kb.tt(y.v, y.v, siluzs[sub][:, gs], ALU.mult)
                norm_gate_T(y.v, 256, None, [obr[2][:, 2 * g, sl], obr[2][:, 2 * g + 1, sl]], junk=t3, og=t1)
            return gen

        run_slots([ssd_group(sub, g) for sub in range(NS) for g in range(2)], nslots=2, offset=1)

        if stop < 7:
            return
        if dbg and l == 0 and first_tile and seq == 0:
            for i in range(3):
                kb.copy(macc[:, 0:4, :], obr[i].v)
                tickets.append(dbgout("obr%d" % i, macc[:, 0:4, :], [128, 4, T]))

        for br in range(3):
            wb = wload(wbr_s[l][br].v, 4, 1024)
            for jb in range(2):
                wg = wload(W[:, :, O_GATE + br * 1024 + jb * 512:O_GATE + br * 1024 + (jb + 1) * 512], 8, 512)
                for jj in range(4):
                    j = jb * 4 + jj
                    gt = ft()
                    fm_proj(wg, jj * 128, hT, 8, lambda ps_v, gt=gt: kb.act(gt[:, 0:T], ps_v, AF.Sigmoid))
                    if br == 0:
                        fm_proj(wb, j * 128, obr[br], 4,
                                lambda ps_v, gt=gt, j=j: kb.tt(macc[:, j, :], ps_v, gt[:, 0:T], ALU.mult))
                    else:
                        tm = ft()
                        fm_proj(wb, j * 128, obr[br], 4,
                                lambda ps_v, gt=gt, tm=tm: kb.tt(tm[:, 0:T], ps_v, gt[:, 0:T], ALU.mult))
                        kb.tt(macc[:, j, :], macc[:, j, :], tm[:, 0:T], ALU.add, eng="pool")
        kb.copy(mergedT.v, macc.v, eng="act")
        for jb in range(2):
            wo = wload(wout_s[l][:, :, jb * 512:(jb + 1) * 512], 8, 512)
            for jj in range(4):
                j = jb * 4 + jj
                fm_proj(wo, jj * 128, mergedT, 8,
                        lambda ps_v, j=j: kb.stt(xT[:, j, :], ps_v, mod[:, 16 + j, seq:seq + 1], xT[:, j, :], ALU.mult, ALU.add))
        if dbg and l == 0 and first_tile and seq == 0:
            tickets.append(dbgout("xmix", xT.v, [128, 8, T]))

        if stop < 8:
            return
        rms_mod(L["a2"][:, :, seq], mod[:, 24:32, seq], out_bf=hT)
        for i in range(11):
            b = wbufs[wst["i"] % NW]; wst["i"] += 1
            wv = b.v.re("p (k n) -> p k n", n=512)
            kb.dma(wv[:, :, 0:256], wup_s[l][:, :, i * 256:(i + 1) * 256])
            kb.dma(wv[:, :, 256:512], wup_s[l][:, :, FH + i * 256:FH + (i + 1) * 256])
            for jj in range(2):
                fcg = i * 2 + jj
                fcv = 22 + fcg
                res = {}

                def evg(ps_v, res=res, fcg=fcg):
                    res["g"] = conv(ps_v, st["fh"][:, fcg, :], L["fcw"][:, fcg, :], 3, bias_v=L["fcb"][:, fcg:fcg + 1])

                def evv(ps_v, res=res, fcv=fcv):
                    res["v"] = conv(ps_v, st["fh"][:, fcv, :], L["fcw"][:, fcv, :], 3, bias_v=L["fcb"][:, fcv:fcv + 1])
                fm_proj(wv, jj * 128, hT, 8, evg)
                fm_proj(wv, 256 + jj * 128, hT, 8, evv)
                sg = ft()
                kb.act(sg[:, 0:T], res["g"][:, 0:T], AF.Silu)
                kb.tt(hidT[:, fcg, :], sg[:, 0:T], res["v"][:, 0:T], ALU.mult)
        for j in range(8):
            b = wbufs[wst["i"] % NW]; wst["i"] += 1
            wv = b.v[:, 0:22 * 128].re("p (k n) -> p k n", n=128)
            kb.dma(wv, wdn_s[l][:, :, j * 128:(j + 1) * 128])
            fm_proj(wv, 0, hidT, 22,
                    lambda ps_v, j=j: kb.stt(xT[:, j, :], ps_v, mod[:, 40 + j, seq:seq + 1], xT[:, j, :], ALU.mult, ALU.add))
        if dbg and l == 0 and first_tile and seq == 0:
            tickets.append(dbgout("xffn", xT.v, [128, 8, T]))

    for seq in range(NSEQ):
        for ti in range(NT):
            r0 = seq * S + ti * T
            kb.dma(xio.v, V((x_d,), x_d.ap[r0:r0 + T, :].rearrange("(s p) d -> p s d", p=128)), eng="pool")
            for sub in range(NS):
                for c in range(8):
                    pb = bank()
                    kb.tr(pb[:, 0:128], xio[:, sub, c * 128:(c + 1) * 128], IDENT)
                    kb.copy(xT[:, c, sub * 128:(sub + 1) * 128], pb[:, 0:128], eng=("act" if c % 2 else "dve"))
            for l in range(DEPTH if stop >= 2 else 0):
                layer_body(l, seq, ti == 0)
            rms_mod(fnw.v, None, out_f32=macc)
            for sub in range(NS):
                for c in range(8):
                    pb = bank()
                    kb.tr(pb[:, 0:128], macc[:, c, sub * 128:(sub + 1) * 128], IDENT)
                    kb.copy(xio[:, sub, c * 128:(c + 1) * 128], pb[:, 0:128], eng=("act" if c % 2 else "dve"))
            tickets.append(kb.dma(V((out_d,), out_d.ap[r0:r0 + T, :].rearrange("(s p) d -> p s d", p=128)), xio.v, eng="pool"))

    tickets = [t for t in tickets if t is not None]
    kb.finish(tickets)
    return nc, kb, dbg_d


_CACHE = {}


def kernel(**inputs):
    NC = 8
    x = np.ascontiguousarray(inputs["x"], dtype=np.float32)
    B, S, _ = x.shape
    NSEQ = B // NC
    key = (S, NSEQ)
    if key not in _CACHE:
        _CACHE[key] = build(S, NSEQ)[0]
    nc = _CACHE[key]
    cmask = make_masks()
    in_maps = []
    for i in range(NC):
        m = {"x": x[i * NSEQ:(i + 1) * NSEQ].reshape(NSEQ * S, D),
             "c": np.ascontiguousarray(inputs["c"][i * NSEQ:(i + 1) * NSEQ], dtype=np.float32),
             "cmask": cmask}
        for n in PNAMES:
            m[n] = np.ascontiguousarray(inputs[n], dtype=np.float32)
        in_maps.append(m)
    res = run_bass_kernel_spmd(nc, in_maps, core_ids=list(range(NC)))
    out = np.concatenate([r["out"].reshape(NSEQ, S, D) for r in res.results], axis=0)
    return out.astype(np.float32)
```

```python
import numpy as np
from contextlib import ExitStack
import concourse.bass as bass
import concourse.mybir as mybir
from concourse.bass_utils import run_bass_kernel_spmd

F32 = mybir.dt.float32
BF16 = mybir.dt.bfloat16
AF = mybir.ActivationFunctionType
ALU = mybir.AluOpType

D = 1024
NIN = 8720
FH = 2816
F2 = 5632
EPS = 1e-6
O_Q, O_K, O_V = 0, 512, 1024
O_A, O_B, O_Z = 1536, 1540, 1544
O_HQ, O_HF, O_HI, O_HG = 2056, 2568, 3080, 3592
O_SZ, O_SX, O_SB, O_SC, O_DT, O_GATE = 4104, 4616, 5128, 5384, 5640, 5648


class V:
    __slots__ = ("bufs", "ap")

    def __init__(self, bufs, ap):
        self.bufs = bufs
        self.ap = ap

    def __getitem__(self, idx):
        return V(self.bufs, self.ap[idx])

    def re(self, s, **kw):
        return V(self.bufs, self.ap.rearrange(s, **kw))

    def bc(self, shape):
        return V(self.bufs, self.ap.to_broadcast(list(shape)))

    def un(self, axis):
        return V(self.bufs, self.ap.unsqueeze(axis))


class Buf:
    def __init__(self, name, t, ap):
        self.name = name
        self.t = t
        self.ap = ap
        self.lw = None
        self.rd = []
        self.psum = False

    def __getitem__(self, idx):
        return V((self,), self.ap[idx])

    @property
    def v(self):
        return V((self,), self.ap)


class KB:
    ENG = ("pe", "act", "dve", "pool", "sp")
    EP = 30000
    NEP = 10

    def __init__(self, nc, n_dma_sems=24, same_engine_sync=True):
        self.nc = nc
        self.st = ExitStack()
        self.q = {e: [] for e in self.ENG}
        self.sems = {}
        self.cnt = {}
        for e in ("pe", "act", "dve", "pool"):
            for ep in range(self.NEP):
                self.sems["%s#%d" % (e, ep)] = self.st.enter_context(nc.semaphore("c_%s_%d" % (e, ep)))
            self.cnt[e] = 0
        self.dsem = []
        for i in range(n_dma_sems):
            k = "d%d" % i
            self.sems[k] = self.st.enter_context(nc.semaphore(k))
            self.cnt[k] = 0
            self.dsem.append(k)
        self.dnext = 0
        self.dnext2 = 0
        self.waited = {e: {} for e in self.ENG}
        self.same = same_engine_sync
        self.nbuf = 0
        self.ninst = 0

    def sb(self, shape, dtype=F32, name=None):
        self.nbuf += 1
        name = name or ("b%d" % self.nbuf)
        t = self.st.enter_context(self.nc.sbuf_tensor(name, list(shape), dtype))
        return Buf(name, t, t[:])

    def ps(self, shape, dtype=F32, name=None):
        self.nbuf += 1
        name = name or ("p%d" % self.nbuf)
        t = self.st.enter_context(self.nc.psum_tensor(name, list(shape), dtype))
        b = Buf(name, t, t[:])
        b.psum = True
        return b

    def dram(self, name, shape, dtype, kind):
        t = self.nc.dram_tensor(name, list(shape), dtype, kind=kind)
        return Buf(name, t, t.ap())

    def _wait(self, eng, key, val):
        w = self.waited[eng]
        if w.get(key, 0) >= val:
            return
        w[key] = val
        sem = self.sems[key]
        self.q[eng].append(lambda e, sem=sem, val=val: e.wait_ge(sem, val))

    def _deps(self, eng, reads, writes):
        need = {}
        for v in reads:
            for b in v.bufs:
                if b.lw is not None:
                    k, val = b.lw
                    need[k] = max(need.get(k, 0), val)
        for v in writes:
            for b in v.bufs:
                if b.lw is not None:
                    k, val = b.lw
                    need[k] = max(need.get(k, 0), val)
                for (k, val) in b.rd:
                    need[k] = max(need.get(k, 0), val)
        for k, val in need.items():
            if k.split("#")[0] == eng and (eng == "pe" or not self.same):
                continue
            self._wait(eng, k, val)

    def _mark(self, ticket, reads, writes):
        for v in writes:
            for b in v.bufs:
                b.lw = ticket
                b.rd = []
        for v in reads:
            for b in v.bufs:
                if b.lw != ticket:
                    b.rd.append(ticket)
                    if len(b.rd) > 32:
                        m = {}
                        for (k, val) in b.rd:
                            m[k] = max(m.get(k, 0), val)
                        b.rd = list(m.items())

    def op(self, eng, fn, reads, writes):
        pr = [v for v in reads if any(b.psum for b in v.bufs)]
        if pr:
            writes = list(writes) + pr
        self._deps(eng, reads, writes)
        ep, within = divmod(self.cnt[eng], self.EP)
        self.cnt[eng] += 1
        self.ninst += 1
        key = "%s#%d" % (eng, ep)
        sem = self.sems[key]
        self.q[eng].append(lambda e, fn=fn, sem=sem: fn(e).then_inc(sem, 1))
        self._mark((key, within + 1), reads, writes)

    def dma(self, out, in_, eng="sp", **kw):
        if eng == "sp":
            k = self.dsem[self.dnext]
            self.dnext = (self.dnext + 1) % (len(self.dsem) - 8)
        else:
            k = self.dsem[len(self.dsem) - 8 + self.dnext2]
            self.dnext2 = (self.dnext2 + 1) % 8
        if self.cnt[k] > 0:
            self._wait(eng, k, self.cnt[k])
        self._deps(eng, [in_], [out])
        self.cnt[k] += 16
        self.ninst += 1
        sem = self.sems[k]
        self.q[eng].append(lambda e, o=out.ap, i=in_.ap, sem=sem, kw=kw: e.dma_start(out=o, in_=i, **kw).then_inc(sem, 16))
        t = (k, self.cnt[k])
        self._mark(t, [in_], [out])
        return t

    def mm(self, out, lhsT, rhs, start=True, stop=True):
        rd = [lhsT, rhs] + ([] if start else [out])
        self.op("pe", lambda e, o=out.ap, l=lhsT.ap, r=rhs.ap: e.matmul(o, l, r, start=start, stop=stop), rd, [out])

    def tr(self, out, in_, ident):
        self.op("pe", lambda e, o=out.ap, i=in_.ap, d=ident.ap: e.transpose(o, i, d), [in_, ident], [out])

    def act(self, out, in_, func, scale=None, bias=None, accum=None):
        rd = [in_]
        kw = {}
        if scale is not None:
            if isinstance(scale, V):
                rd.append(scale); kw["scale"] = scale.ap
            else:
                kw["scale"] = scale
        if bias is not None:
            if isinstance(bias, V):
                rd.append(bias); kw["bias"] = bias.ap
            else:
                kw["bias"] = bias
        wr = [out]
        if accum is not None:
            wr.append(accum); kw["accum_out"] = accum.ap
        self.op("act", lambda e, o=out.ap, i=in_.ap, kw=kw: e.activation(o, i, func, **kw), rd, wr)

    def tt(self, out, a, b, op, eng="dve"):
        self.op(eng, lambda e, o=out.ap, a_=a.ap, b_=b.ap: e.tensor_tensor(o, a_, b_, op), [a, b], [out])

    def ts(self, out, a, s1, op0, s2=None, op1=None, eng="dve"):
        rd = [a]
        s1a = s1.ap if isinstance(s1, V) else s1
        s2a = s2.ap if isinstance(s2, V) else s2
        if isinstance(s1, V): rd.append(s1)
        if isinstance(s2, V): rd.append(s2)
        if op1 is None:
            self.op(eng, lambda e, o=out.ap, a_=a.ap: e.tensor_scalar(o, a_, s1a, None, op0), rd, [out])
        else:
            self.op(eng, lambda e, o=out.ap, a_=a.ap: e.tensor_scalar(o, a_, s1a, s2a, op0, op1), rd, [out])

    def stt(self, out, a, s, b, op0, op1):
        rd = [a, b]
        sa = s.ap if isinstance(s, V) else s
        if isinstance(s, V): rd.append(s)
        self.op("dve", lambda e, o=out.ap, a_=a.ap, b_=b.ap: e.scalar_tensor_tensor(o, a_, sa, b_, op0, op1), rd, [out])

    def scan(self, out, d0, d1, init, op0, op1):
        self.op("dve", lambda e, o=out.ap, a_=d0.ap, b_=d1.ap: e.tensor_tensor_scan(o, a_, b_, init, op0, op1), [d0, d1], [out])

    def copy(self, out, in_, eng="dve"):
        if eng == "act":
            self.act(out, in_, AF.Copy)
        else:
            self.op(eng, lambda e, o=out.ap, i=in_.ap: e.tensor_copy(o, i), [in_], [out])

    def memset(self, out, val, eng="pool"):
        self.op(eng, lambda e, o=out.ap: e.memset(o, val), [], [out])

    def recip(self, out, in_):
        self.op("dve", lambda e, o=out.ap, i=in_.ap: e.reciprocal(o, i), [in_], [out])

    def finish(self, out_tickets):
        for (k, val) in out_tickets:
            self._wait("sp", k, val)
        nc = self.nc
        with nc.Block() as block:
            @block.tensor
            def _(e):
                for f in self.q["pe"]: f(e)

            @block.scalar
            def _(e):
                for f in self.q["act"]: f(e)

            @block.vector
            def _(e):
                for f in self.q["dve"]: f(e)

            @block.gpsimd
            def _(e):
                for f in self.q["pool"]: f(e)

            @block.sync
            def _(e):
                for f in self.q["sp"]: f(e)
        self.st.close()


def make_masks():
    t = np.arange(128)
    r = t[:, None]
    c = t[None, :]
    same64 = (r // 64) == (c // 64)
    same32 = (r // 32) == (c // 32)
    m = np.zeros((128, 11, 128), np.float32)
    m[:, 0] = (r == c)
    m[:, 1] = 1.0
    m[:, 2] = (c >= r) & same64
    m[:, 3] = (r > c)
    m[:, 4] = (r <= c)
    m[:, 5] = (r <= c) & same64
    m[:, 6] = (r > c) & same64
    m[:, 7] = (r < 64) & (c >= 0)
    m[:, 8] = (r >= 64) & (c >= 0)
    m[:, 9] = (c >= r) & same32
    ind = np.zeros((128, 128), np.float32)
    for j in range(4):
        ind[:, j] = (t // 32 == j)
    ind[:, 4] = (t < 64)
    ind[:, 5] = (t >= 64)
    m[:, 10] = ind
    return m


PNAMES = ["w_ada", "b_ada", "norm1_w", "w_in", "gdn_conv_w", "gdn_a_log", "gdn_dt_bias", "gdn_norm_w",
          "hgrn_lb_param", "hgrn_norm_w", "ssd_conv_w", "ssd_conv_b", "ssd_a_log", "ssd_dt_bias", "ssd_d",
          "ssd_norm_w", "w_br_a", "w_br_b", "w_br_c", "w_out", "norm2_w", "ffn_w_up", "ffn_conv_w",
          "ffn_conv_b", "ffn_w_down", "final_norm_w"]
PSHAPES = {
    "w_ada": (D, 6 * D), "b_ada": (6 * D,), "norm1_w": (D,), "w_in": (D, NIN), "gdn_conv_w": (4, 1536),
    "gdn_a_log": (4,), "gdn_dt_bias": (4,), "gdn_norm_w": (128,), "hgrn_lb_param": (512,),
    "hgrn_norm_w": (128,), "ssd_conv_w": (4, 1024), "ssd_conv_b": (1024,), "ssd_a_log": (8,),
    "ssd_dt_bias": (8,), "ssd_d": (8,), "ssd_norm_w": (512,), "w_br_a": (512, D), "w_br_b": (512, D),
    "w_br_c": (512, D), "w_out": (D, D), "norm2_w": (D,), "ffn_w_up": (D, F2), "ffn_conv_w": (3, F2),
    "ffn_conv_b": (F2,), "ffn_w_down": (FH, D),
}


def build(S, NSEQ, DEPTH=2, T=256, dbg=False, stop=99):
    NS = T // 128
    NT = S // T
    assert S % T == 0
    nc = bass.Bass("TRN2", target_bir_lowering=False)
    kb = KB(nc)
    x_d = kb.dram("x", [NSEQ * S, D], F32, "ExternalInput")
    c_d = kb.dram("c", [NSEQ, D], F32, "ExternalInput")
    cm_d = kb.dram("cmask", [128, 11, 128], F32, "ExternalInput")
    P = {}
    for n in PNAMES:
        if n == "final_norm_w":
            P[n] = kb.dram(n, [D], F32, "ExternalInput")
        else:
            P[n] = kb.dram(n, [DEPTH] + list(PSHAPES[n]), F32, "ExternalInput")
    out_d = kb.dram("out", [NSEQ * S, D], F32, "ExternalOutput")
    dbg_d = {}

    def dbgout(name, view, shape):
        if not dbg:
            return None
        dbg_d[name] = kb.dram("dbg_" + name, list(shape), F32, "ExternalOutput")
        return kb.dma(dbg_d[name].v, view)

    win_s = [kb.dram("win_s%d" % l, [128, 8, NIN], BF16, "Internal") for l in range(DEPTH)]
    wbr_s = [[kb.dram("wbr_s%d_%d" % (l, b), [128, 4, D], BF16, "Internal") for b in range(3)] for l in range(DEPTH)]
    wout_s = [kb.dram("wout_s%d" % l, [128, 8, D], BF16, "Internal") for l in range(DEPTH)]
    wup_s = [kb.dram("wup_s%d" % l, [128, 8, F2], BF16, "Internal") for l in range(DEPTH)]
    wdn_s = [kb.dram("wdn_s%d" % l, [128, 22, D], BF16, "Internal") for l in range(DEPTH)]

    tickets = []
    cm = kb.sb([128, 11, 128], F32, "cm")
    kb.dma(cm.v, cm_d.v)
    IDENT = cm[:, 0, :]
    ONES = cm[:, 1, :]
    M_INCL64 = cm[:, 2, :]
    M_UPALL = cm[:, 3, :]
    M_TRIALL = cm[:, 4, :]
    M_BTRI64 = cm[:, 5, :]
    M_BUP64 = cm[:, 6, :]
    M_IND64 = [cm[:, 7, :], cm[:, 8, :]]
    M_INCL32 = cm[:, 9, :]
    INDC = cm[:, 10, :]
    ones_bf = kb.sb([128, 128], BF16, "ones_bf")
    kb.memset(ones_bf.v, 1.0)
    resetm = kb.sb([128, T], F32, "resetm")
    kb.memset(resetm.v, 1.0)
    kb.memset(resetm.v.re("p (c r) -> p c r", r=32)[:, :, 0:1], 0.0)

    banks = [kb.ps([128, 512], F32, "bank%d" % i) for i in range(8)]
    bstate = {"i": 0}

    def bank():
        b = banks[bstate["i"]]
        bstate["i"] = (bstate["i"] + 1) % 6
        return b

    def pload(name, l, pattern, shape, **kw):
        b = kb.sb(shape, F32, "%s_%d" % (name, l))
        src = P[name].ap[l] if name != "final_norm_w" else P[name].ap
        C = shape[1]
        for c in range(C):
            if len(shape) == 3:
                kb.dma(b[:, c, :], V((P[name],), src[:, c * 128:(c + 1) * 128].rearrange("k p -> p k")),
                       allow_slow_non_contiguous=True)
            else:
                kb.dma(b[:, c:c + 1], V((P[name],), src[c * 128:(c + 1) * 128].rearrange("(p o) -> p o", o=1)),
                       allow_slow_non_contiguous=True)
        return b

    def pbc(name, l, n):
        b = kb.sb([128, n], F32, "%s_bc%d" % (name, l))
        kb.dma(b.v, V((P[name],), P[name].ap[l].partition_broadcast(128)))
        return b

    cT = kb.sb([128, 8, NSEQ], F32, "cT")
    for s_ in range(NSEQ):
        kb.dma(cT[:, :, s_], V((c_d,), c_d.ap[s_].rearrange("(k p) -> p k", p=128)), allow_slow_non_contiguous=True)
    cact = kb.sb([128, 8, NSEQ], F32, "cact")
    kb.act(cact.v, cT.v, AF.Silu)
    fnw = pload("final_norm_w", 0, "(c p) -> p c", [128, 8], p=128)

    LP = []
    stage_f = [kb.sb([128, 1024], F32, "stf%d" % i) for i in range(2)]
    stage_b = [kb.sb([128, 1024], BF16, "stb%d" % i) for i in range(2)]
    stg = {"i": 0}
    for l in range(DEPTH):
        L = {}
        L["n1"] = pload("norm1_w", l, "(c p) -> p c", [128, 8], p=128)
        L["n2"] = pload("norm2_w", l, "(c p) -> p c", [128, 8], p=128)
        L["bada"] = pload("b_ada", l, "(c p) -> p c", [128, 48], p=128)
        L["gcw"] = pload("gdn_conv_w", l, "k (c p) -> p c k", [128, 12, 4], p=128)
        L["scw"] = pload("ssd_conv_w", l, "k (c p) -> p c k", [128, 8, 4], p=128)
        L["scb"] = pload("ssd_conv_b", l, "(c p) -> p c", [128, 8], p=128)
        L["fcw"] = pload("ffn_conv_w", l, "k (c p) -> p c k", [128, 44, 3], p=128)
        L["fcb"] = pload("ffn_conv_b", l, "(c p) -> p c", [128, 44], p=128)
        L["gnw"] = pload("gdn_norm_w", l, "(c p) -> p c", [128, 1], p=128)
        L["hnw"] = pload("hgrn_norm_w", l, "(c p) -> p c", [128, 1], p=128)
        L["snw"] = pload("ssd_norm_w", l, "(c p) -> p c", [128, 4], p=128)
        alog = kb.sb([128, 12], F32, "alog%d" % l)
        kb.dma(alog[:, 0:4], V((P["gdn_a_log"],), P["gdn_a_log"].ap[l].partition_broadcast(128)))
        kb.dma(alog[:, 4:12], V((P["ssd_a_log"],), P["ssd_a_log"].ap[l].partition_broadcast(128)))
        dtb = kb.sb([128, 12], F32, "dtb%d" % l)
        kb.dma(dtb[:, 0:4], V((P["gdn_dt_bias"],), P["gdn_dt_bias"].ap[l].partition_broadcast(128)))
        kb.dma(dtb[:, 4:12], V((P["ssd_dt_bias"],), P["ssd_dt_bias"].ap[l].partition_broadcast(128)))
        nea = kb.sb([128, 12], F32, "nea%d" % l)
        kb.act(nea.v, alog.v, AF.Exp)
        kb.ts(nea.v, nea.v, -1.0, ALU.mult)
        L["nea"] = nea
        L["dtb"] = dtb
        dsk = pbc("ssd_d", l, 8)
        dfull = kb.sb([128, 8, 64], F32, "dfull%d" % l)
        kb.copy(dfull.v, dsk.v.un(2).bc([128, 8, 64]))
        L["dfull"] = dfull
        LP.append(L)

    lbp = [pload("hgrn_lb_param", l, "(h p) -> p h", [128, 4], p=128) for l in range(DEPTH)]
    persist = [{k_: kb.sb([128, 32], F32, "ps_%s_%d" % (k_, s_)) for k_ in ("g", "spl", "bt", "e1", "e2m", "e3")} for s_ in range(2)]
    gepool = [kb.sb([128, 32], F32, "gepool%d" % i) for i in range(2)]
    lbe = [kb.sb([128, 4], F32, "lbe%d" % l) for l in range(DEPTH)]
    for l in range(DEPTH):
        kb.act(lbe[l].v, lbp[l].v, AF.Exp)
    lsum = kb.sb([128, 4], F32, "lsum")
    kb.copy(lsum.v, lbe[0].v)
    for l in range(1, DEPTH):
        kb.tt(lsum.v, lsum.v, lbe[l].v, ALU.add)
    lrs = kb.sb([128, 4], F32, "lrs")
    kb.recip(lrs.v, lsum.v)
    cum = kb.sb([128, 4], F32, "lcum")
    kb.memset(cum.v, 0.0)
    for l in range(DEPTH):
        lb = kb.sb([128, 4], F32, "lb%d" % l)
        oml = kb.sb([128, 4], F32, "oml%d" % l)
        if l > 0:
            sm = kb.sb([128, 4], F32, "lsm%d" % l)
            kb.tt(sm.v, lbe[l].v, lrs.v, ALU.mult)
            kb.tt(cum.v, cum.v, sm.v, ALU.add)
        kb.copy(lb.v, cum.v)
        kb.ts(oml.v, lb.v, -1.0, ALU.mult, 1.0, ALU.add)
        LP[l]["lb"] = lb
        LP[l]["oml"] = oml

    for l in range(DEPTH):
        L = LP[l]
        mod = kb.sb([128, 48, NSEQ], F32, "mod%d" % l)
        for fc in range(48):
            sf = stage_f[stg["i"] % 2]; stg["i"] += 1
            sfv = sf.v.re("p (k n) -> p k n", n=128)
            kb.dma(sfv, V((P["w_ada"],), P["w_ada"].ap[l][:, fc * 128:(fc + 1) * 128].rearrange("(k p) n -> p k n", p=128)))
            pb = bank()
            for k in range(8):
                kb.mm(pb[:, 0:NSEQ], sfv[:, k, :], cact[:, k, :], start=(k == 0), stop=(k == 7))
            kb.ts(mod[:, fc, :], pb[:, 0:NSEQ], L["bada"][:, fc:fc + 1], ALU.add)
        L["mod"] = mod
        a1 = kb.sb([128, 8, NSEQ], F32, "a1_%d" % l)
        a2 = kb.sb([128, 8, NSEQ], F32, "a2_%d" % l)
        for s_ in range(NSEQ):
            kb.stt(a1[:, :, s_], mod[:, 8:16, s_], 1.0, L["n1"].v, ALU.add, ALU.mult)
            kb.stt(a2[:, :, s_], mod[:, 32:40, s_], 1.0, L["n2"].v, ALU.add, ALU.mult)
        L["a1"], L["a2"] = a1, a2

    DEPTH_C = DEPTH if stop >= 1 else 0
    cast_eng = ["act", "dve", "pool"]
    cst = {"i": 0}

    def cast_weight(src_buf, src_ap, K, N, dst, scale=None):
        KC = K // 128
        for k in range(KC):
            for n0 in range(0, N, 1024):
                w = min(1024, N - n0)
                i = stg["i"] % 2; stg["i"] += 1
                sf, sbb = stage_f[i], stage_b[i]
                kb.dma(sf[:, 0:w], V((src_buf,), src_ap[k * 128:(k + 1) * 128, n0:n0 + w]),
                       eng=("sp" if i == 0 else "pool"))
                if scale is not None:
                    kb.ts(sbb[:, 0:w], sf[:, 0:w], scale[:, k:k + 1], ALU.mult)
                else:
                    e = cast_eng[cst["i"] % 3]; cst["i"] += 1
                    kb.copy(sbb[:, 0:w], sf[:, 0:w], eng=e)
                kb.dma(dst[:, k, n0:n0 + w], sbb[:, 0:w], eng=("sp" if i == 1 else "pool"))

    for l in range(DEPTH_C):
        L = LP[l]
        cast_weight(P["w_in"], P["w_in"].ap[l], D, NIN, win_s[l])
        cast_weight(P["w_br_a"], P["w_br_a"].ap[l], 512, D, wbr_s[l][0], scale=L["gnw"][:, 0:1].bc([128, 4]))
        cast_weight(P["w_br_b"], P["w_br_b"].ap[l], 512, D, wbr_s[l][1], scale=L["hnw"][:, 0:1].bc([128, 4]))
        cast_weight(P["w_br_c"], P["w_br_c"].ap[l], 512, D, wbr_s[l][2], scale=L["snw"].v)
        cast_weight(P["w_out"], P["w_out"].ap[l], D, D, wout_s[l])
        cast_weight(P["ffn_w_up"], P["ffn_w_up"].ap[l], D, F2, wup_s[l])
        cast_weight(P["ffn_w_down"], P["ffn_w_down"].ap[l], FH, D, wdn_s[l])
        wsm = kb.sb([128, 8, 16], BF16, "wsm%d" % l)
        kb.dma(wsm[:, :, 0:4], win_s[l][:, :, O_A:O_A + 4], allow_slow_non_contiguous=True)
        kb.dma(wsm[:, :, 4:12], win_s[l][:, :, O_DT:O_DT + 8], allow_slow_non_contiguous=True)
        kb.dma(wsm[:, :, 12:16], win_s[l][:, :, O_B:O_B + 4], allow_slow_non_contiguous=True)
        L["wsm"] = wsm

    NW = 4
    wbufs = [kb.sb([128, 4096], BF16, "wbuf%d" % i) for i in range(NW)]
    wst = {"i": 0}

    def wload(src_view, kc, ncols):
        b = wbufs[wst["i"] % NW]; wst["i"] += 1
        kb.dma(b.v[:, 0:kc * ncols].re("p (k n) -> p k n", n=ncols), src_view)
        return b.v[:, 0:kc * ncols].re("p (k n) -> p k n", n=ncols)

    xT = kb.sb([128, 8, T], F32, "xT")
    xio = kb.sb([128, NS, D], F32, "xio")
    hT = kb.sb([128, 8, T], BF16, "hT")
    macc = kb.sb([128, 8, T], F32, "macc")
    mergedT = kb.sb([128, 8, T], BF16, "mergedT")
    obr = [kb.sb([128, 4, T], BF16, "obr%d" % i) for i in range(3)]
    hidT = kb.sb([128, 22, T], BF16, "hidT")
    ftmp = [kb.sb([128, T + 4], F32, "ftmp%d" % i) for i in range(14)]
    fst = {"i": 0}

    def ft():
        b = ftmp[fst["i"] % len(ftmp)]; fst["i"] += 1
        return b
    ft_global = ft

    abf = [kb.sb([128, 4, T], BF16, "abf%d" % i) for i in range(3)]
    tokbf = [[kb.sb([128, 512], BF16, "tokbf%d_%d" % (i, s_)) for s_ in range(NS)] for i in range(4)]
    tokf = [[kb.sb([128, 512], F32, "tokf%d_%d" % (i, s_)) for s_ in range(NS)] for i in range(2)]
    sq128 = [kb.sb([128, 128], F32, "sq%d" % i) for i in range(2)]
    sqst = {"i": 0}

    def sq():
        b = sq128[sqst["i"] % len(sq128)]; sqst["i"] += 1
        return b

    def alias(parent, ap, name):
        b = Buf(name, parent.t, ap)
        b.lw = parent.lw
        b.rd = list(parent.rd)
        return b

    sq_extra = [alias(stage_f[i // 8], stage_f[i // 8].ap[:, (i % 8) * 128:(i % 8 + 1) * 128], "sqx%d" % i) for i in range(16)]
    bf_extra = [alias(stage_b[i // 8], stage_b[i // 8].ap[:, (i % 8) * 128:(i % 8 + 1) * 128], "bfx%d" % i) for i in range(16)]
    gdn_tp = [{"sq": sq_extra[h * 4:h * 4 + 4] + [kb.sb([128, 128], F32, "gsq%d_%d" % (h, i)) for i in range(2)],
               "bf": bf_extra[h * 4:h * 4 + 4]} for h in range(4)]

    for h_ in range(4):
        kb.memset(gdn_tp[h_]["bf"][3].v, 0.0)

    def run_threads(gens):
        gens = list(gens)
        while gens:
            for g in list(gens):
                try:
                    next(g)
                except StopIteration:
                    gens.remove(g)

    sqb = [kb.sb([128, 128], BF16, "sqb%d" % i) for i in range(4)]
    sqbst = {"i": 0}

    def sqbf():
        b = sqb[sqbst["i"] % len(sqb)]; sqbst["i"] += 1
        return b

    big4 = [kb.sb([128, 4, 128], F32, "big4_%d" % i) for i in range(4)]
    big4b = [kb.sb([128, 4, 128], BF16, "big4b_%d" % i) for i in range(2)]
    small = [kb.sb([128, 32], F32, "small%d" % i) for i in range(16)]
    smst = {"i": 0}

    def sm_():
        b = small[smst["i"] % len(small)]; smst["i"] += 1
        return b

    qgz = [[kb.sb([128, 640], BF16, "qgz%d_%d" % (i, s_)) for s_ in range(NS)] for i in range(2)]
    for i in range(2):
        for s_ in range(NS):
            kb.memset(qgz[i][s_].v, 0.0)
    kendz = [[kb.sb([128, 4, 128], BF16, "kendz%d_%d" % (i, s_)) for s_ in range(NS)] for i in range(2)]
    vnew = kb.sb([128, 128], BF16, "vnew")
    kb.memset(vnew.v, 0.0)
    otok = [kb.sb([128, 256], F32, "otok%d" % i) for i in range(6)]
    otst = {"i": 0}

    def ot():
        b = otok[otst["i"] % 6]; otst["i"] += 1
        return b

    ST = []
    for l in range(DEPTH):
        s_ = {}
        s_["gS"] = kb.sb([128, 4, 128], F32, "gS%d" % l)
        s_["gSb"] = kb.sb([128, 4, 128], BF16, "gSb%d" % l)
        s_["hS"] = kb.sb([128, 4, 128], F32, "hS%d" % l)
        s_["hSb"] = kb.sb([128, 4, 128], BF16, "hSb%d" % l)
        s_["sS"] = kb.sb([128, 2, 256], F32, "sS%d" % l)
        s_["sSb"] = kb.sb([128, 2, 256], BF16, "sSb%d" % l)
        s_["gh"] = kb.sb([128, 12, 3], F32, "gh%d" % l)
        s_["sh"] = kb.sb([128, 8, 3], F32, "sh%d" % l)
        s_["fh"] = kb.sb([128, 44, 2], F32, "fh%d" % l)
        ST.append(s_)

    def rms_mod(a_v, b_v, out_bf=None, out_f32=None):
        pb = bank()
        for c in range(8):
            s2b = abf[2]
            kb.act(s2b[:, c % 4, :], xT[:, c, :], AF.Square)
            kb.mm(pb[:, 0:T], ones_bf.v, s2b[:, c % 4, :], start=(c == 0), stop=(c == 7))
        rs = ft()
        kb.act(rs[:, 0:T], pb[:, 0:T], AF.Sqrt, scale=1.0 / D, bias=EPS)
        kb.recip(rs[:, 0:T], rs[:, 0:T])
        for c in range(8):
            t_ = ft()
            kb.tt(t_[:, 0:T], xT[:, c, :], rs[:, 0:T], ALU.mult)
            if out_bf is not None:
                kb.act(out_bf[:, c, :], t_[:, 0:T], AF.Identity, scale=a_v[:, c:c + 1], bias=b_v[:, c:c + 1])
            else:
                kb.act(out_f32[:, c, :], t_[:, 0:T], AF.Identity, scale=a_v[:, c:c + 1])

    def fm_proj(wv, col0, rhs_buf, KC, evac):
        pb = bank()
        for k in range(KC):
            kb.mm(pb[:, 0:T], wv[:, k, col0:col0 + 128], rhs_buf[:, k, :], start=(k == 0), stop=(k == KC - 1))
        evac(pb[:, 0:T])

    def tm_proj(wv, col0, ncols, sub, evac):
        pb = bank()
        for k in range(8):
            kb.mm(pb[:, 0:ncols], hT[:, k, sub * 128:(sub + 1) * 128], wv[:, k, col0:col0 + ncols],
                  start=(k == 0), stop=(k == 7))
        evac(pb[:, 0:ncols])

    def slot_alloc(slot, n=7):
        lst = ftmp[slot * n:(slot + 1) * n]
        stt_ = {"i": 0}

        def al():
            b = lst[stt_["i"] % n]; stt_["i"] += 1
            return b
        return al

    def run_slots(factories, nslots=2, offset=1):
        pending = list(factories)[::-1]
        slots = [None] * nslots
        steps = [0] * nslots
        while pending or any(g is not None for g in slots):
            for i in range(nslots):
                if slots[i] is None:
                    if pending and all(slots[j] is None or steps[j] >= offset for j in range(nslots) if j != i):
                        slots[i] = pending.pop()(i)
                        steps[i] = 0
                    else:
                        continue
                try:
                    next(slots[i])
                    steps[i] += 1
                except StopIteration:
                    slots[i] = None

    def conv(ps_v, hist_v, wv, ntap, bias_v=None, ft=None):
        if ft is None:
            ft = ft_global
        H = ntap - 1
        buf = ft()
        kb.copy(buf[:, H:H + T], ps_v, eng="act")
        kb.copy(buf[:, 0:H], hist_v, eng="pool")
        acc = ft()
        kb.act(acc[:, 0:T], ps_v, AF.Identity, scale=wv[:, H:H + 1], bias=(bias_v if bias_v is not None else 0.0))
        for k in range(0, H):
            kb.stt(acc[:, 0:T], buf[:, k:k + T], wv[:, k:k + 1], acc[:, 0:T], ALU.mult, ALU.add)
        kb.copy(hist_v, buf[:, T:T + H], eng="pool")
        return acc

    def norm_gate_T(o_v, width, gate_v, dst_list, junk=None, og=None):
        junk = junk or ot()
        ss = sm_()
        kb.act(junk[:, 0:width], o_v, AF.Square, accum=ss[:, 0:1])
        kb.act(ss[:, 1:2], ss[:, 0:1], AF.Sqrt, scale=1.0 / width, bias=EPS)
        kb.recip(ss[:, 2:3], ss[:, 1:2])
        og = og or ot()
        if gate_v is not None:
            kb.stt(og[:, 0:width], o_v, ss[:, 2:3], gate_v, ALU.mult, ALU.mult)
        else:
            kb.ts(og[:, 0:width], o_v, ss[:, 2:3], ALU.mult)
        for i, dst in enumerate(dst_list):
            pb = bank()
            kb.tr(pb[:, 0:128], og[:, i * 128:(i + 1) * 128], IDENT)
            kb.copy(dst, pb[:, 0:128], eng="act")

    def layer_body(l, seq, first_tile):
        L = LP[l]
        st = ST[l]
        mod = L["mod"]
        W = win_s[l]
        if first_tile:
            for k_ in ("gS", "gSb", "hS", "hSb", "sS", "sSb", "gh", "sh", "fh"):
                kb.memset(st[k_].v, 0.0)
        rms_mod(L["a1"][:, :, seq], mod[:, 0:8, seq], out_bf=hT)

        if stop < 3:
            return
        gda = [None] * NS
        dtv = [None] * NS
        beta = [None] * NS
        nbeta = [None] * NS
        for sub in range(NS):
            pb = bank()
            for k in range(8):
                kb.mm(pb[:, 0:16], hT[:, k, sub * 128:(sub + 1) * 128], L["wsm"][:, k, :], start=(k == 0), stop=(k == 7))
            t1 = sm_()
            kb.tt(t1[:, 0:12], pb[:, 0:12], L["dtb"].v, ALU.add)
            kb.act(t1[:, 0:12], t1[:, 0:12], AF.Exp)
            spl = persist[sub]["spl"]
            kb.act(spl[:, 0:12], t1[:, 0:12], AF.Ln, bias=1.0)
            g_ = persist[sub]["g"]
            kb.tt(g_[:, 0:12], spl[:, 0:12], L["nea"].v, ALU.mult)
            bt = persist[sub]["bt"]
            kb.act(bt[:, 0:4], pb[:, 12:16], AF.Sigmoid)
            kb.ts(bt[:, 4:8], bt[:, 0:4], -1.0, ALU.mult)
            gda[sub], dtv[sub], beta[sub], nbeta[sub] = g_, spl, bt[:, 0:4], bt[:, 4:8]

        expGA = [None] * NS
        eend = [None] * NS
        gend = [None] * NS
        for sub in range(NS):
            g_ = gda[sub]
            pb = bank()
            kb.mm(pb[:, 0:12], M_BTRI64, g_[:, 0:12])
            kb.mm(pb[:, 16:28], M_BUP64, g_[:, 0:12])
            kb.mm(pb[:, 32:44], M_IND64[0], g_[:, 0:12])
            kb.mm(pb[:, 48:60], M_IND64[1], g_[:, 0:12])
            e1 = persist[sub]["e1"]
            kb.act(e1[:, 0:12], pb[:, 0:12], AF.Exp)
            e2 = sm_()
            kb.act(e2[:, 0:12], pb[:, 16:28], AF.Exp)
            e2m = persist[sub]["e2m"]
            kb.ts(e2m[:, 0:12], e2[:, 0:12], INDC[:, 4:5], ALU.mult)
            kb.ts(e2m[:, 12:24], e2[:, 0:12], INDC[:, 5:6], ALU.mult)
            e3 = persist[sub]["e3"]
            kb.act(e3[:, 0:12], pb[:, 32:44], AF.Exp)
            kb.act(e3[:, 12:24], pb[:, 48:60], AF.Exp)
            expGA[sub], eend[sub], gend[sub] = e1, e2m, e3

        def decay_T(sub, h0):
            rh = big4[0]
            kb.tt(rh.v, M_TRIALL.un(1).bc([128, 4, 128]), gda[sub][:, h0:h0 + 4].un(2).bc([128, 4, 128]), ALU.mult)
            pb = bank()
            kb.mm(pb[:, 0:512], M_UPALL, rh.v.re("p h l -> p (h l)"))
            dec = big4[1]
            kb.act(dec.v.re("p h l -> p (h l)"), pb[:, 0:512], AF.Exp)
            return dec

        if stop < 4:
            return
        wq = wload(W[:, :, O_Q:O_Q + 512], 8, 512)
        wk = wload(W[:, :, O_K:O_K + 512], 8, 512)
        wv_ = wload(W[:, :, O_V:O_V + 512], 8, 512)
        qT, kT = abf[0], abf[1]
        vtok, kg, ke0, ke1 = tokbf[0], tokbf[1], tokbf[2], tokbf[3]

        def gdn_chunk(fc):
            def gen(slot):
                al = slot_alloc(slot, 4)
                which, h = fc // 4, fc % 4
                wv = (wq, wk, wv_)[which]
                pb = bank()
                for k in range(8):
                    kb.mm(pb[:, 0:T], wv[:, k, h * 128:(h + 1) * 128], hT[:, k, :], start=(k == 0), stop=(k == 7))
                yield
                acc = conv(pb[:, 0:T], st["gh"][:, fc, :], L["gcw"][:, fc, :], 4, ft=al)
                s_ = al()
                kb.act(s_[:, 0:T], acc[:, 0:T], AF.Silu)
                if which < 2:
                    s2b = abf[2][:, slot, :]
                    kb.act(s2b, s_[:, 0:T], AF.Square)
                    pb = bank()
                    kb.mm(pb[:, 0:T], ones_bf.v, s2b)
                    yield
                    rs = al()
                    if which == 0:
                        kb.act(rs[:, 0:T], pb[:, 0:T], AF.Sqrt, scale=128.0, bias=128.0 * EPS)
                    else:
                        kb.act(rs[:, 0:T], pb[:, 0:T], AF.Sqrt, bias=EPS)
                    kb.recip(rs[:, 0:T], rs[:, 0:T])
                    if which == 0:
                        kb.tt(qT[:, h, :], s_[:, 0:T], rs[:, 0:T], ALU.mult)
                        return
                    knf = al()
                    kb.tt(knf[:, 0:T], s_[:, 0:T], rs[:, 0:T], ALU.mult)
                    kb.copy(kT[:, h, :], knf[:, 0:T], eng="act")
                    pbs = []
                    for sub in range(NS):
                        pb2 = bank()
                        kb.tr(pb2[:, 0:128], knf[:, sub * 128:(sub + 1) * 128], IDENT)
                        pbs.append(pb2)
                    yield
                    for sub in range(NS):
                        pb2 = pbs[sub]
                        kb.act(kg[sub][:, h * 128:(h + 1) * 128], pb2[:, 0:128], AF.Identity, scale=expGA[sub][:, h:h + 1])
                        kb.act(ke0[sub][:, h * 128:(h + 1) * 128], pb2[:, 0:128], AF.Identity, scale=eend[sub][:, h:h + 1])
                        kb.act(ke1[sub][:, h * 128:(h + 1) * 128], pb2[:, 0:128], AF.Identity, scale=eend[sub][:, 12 + h:13 + h])
                else:
                    pbs = []
                    for sub in range(NS):
                        pb2 = bank()
                        kb.tr(pb2[:, 0:128], s_[:, sub * 128:(sub + 1) * 128], IDENT)
                        pbs.append(pb2)
                    yield
                    for sub in range(NS):
                        kb.copy(vtok[sub][:, h * 128:(h + 1) * 128], pbs[sub][:, 0:128], eng="act")
            return gen

        order = [0, 4, 1, 5, 2, 6, 3, 7, 8, 9, 10, 11]
        run_slots([gdn_chunk(fc) for fc in order], nslots=3, offset=1)
        if stop < 4.1:
            return
        wz = wload(W[:, :, O_Z:O_Z + 512], 8, 512)
        siluz = tokf[0]
        for sub in range(NS):
            tm_proj(wz, 0, 512, sub, lambda ps_v, sub=sub: kb.act(siluz[sub].v, ps_v, AF.Silu))

        if stop < 4.2:
            return

        def gdn_head(sub, h, decm, decs, TP):
            sl = slice(sub * 128, (sub + 1) * 128)
            hs = slice(h * 128, (h + 1) * 128)
            PA, PTA, PB, PTB, X, XT = TP["sq"]
            Xb, wT, attnT, vn = TP["bf"]
            pk = bank()
            kb.mm(pk[:, 0:128], kT[:, h, sl], kT[:, h, sl])
            yield
            kb.stt(PA.v, pk[:, 0:128], nbeta[sub][:, h:h + 1], decs[:, h, :], ALU.mult, ALU.mult)
            pt = bank()
            kb.tr(pt[:, 0:128], PA.v, IDENT)
            yield
            kb.copy(PTA.v, pt[:, 0:128], eng="act")
            kb.tt(X.v, PA.v, IDENT, ALU.add)
            kb.tt(XT.v, PTA.v, IDENT, ALU.add, eng="pool")
            Pm, PT, Pn, PTn = PA, PTA, PB, PTB
            for lev in range(5):
                last = (lev == 4)
                p2 = bank()
                kb.mm(p2[:, 0:128], PT.v, Pm.v)
                if not last:
                    kb.mm(p2[:, 128:256], Pm.v, PT.v)
                yield
                kb.copy(Pn.v, p2[:, 0:128], eng="act")
                if not last:
                    kb.copy(PTn.v, p2[:, 128:256], eng="dve")
                px = bank()
                kb.mm(px[:, 0:128], XT.v, Pn.v)
                if not last:
                    kb.mm(px[:, 128:256], Pn.v, XT.v)
                yield
                if last:
                    kb.tt(Xb.v, px[:, 0:128], X.v, ALU.add)
                else:
                    kb.tt(X.v, px[:, 0:128], X.v, ALU.add)
                    kb.tt(XT.v, px[:, 128:256], XT.v, ALU.add)
                    Pm, PT, Pn, PTn = Pn, PTn, Pm, PT
            if stop < 4.4:
                return
            bu, oa, o_ = PA, PTA, PB
            pu = bank()
            kb.mm(pu[:, 0:128], Xb.v, vtok[sub][:, hs])
            kb.mm(pu[:, 128:256], kg[sub][:, hs], Xb.v)
            kb.mm(pu[:, 256:384], kT[:, h, sl], qT[:, h, sl])
            yield
            kb.act(bu.v, pu[:, 0:128], AF.Identity, scale=beta[sub][:, h:h + 1])
            kb.copy(wT.v, pu[:, 128:256], eng="act")
            kb.tt(attnT.v, pu[:, 256:384], decm[:, h, :], ALU.mult)
            for c in range(2):
                rows = slice(c * 64, (c + 1) * 64)
                p1 = bank()
                kb.mm(p1[:, 0:128], wT.v, st["gSb"][:, h, :])
                kb.mm(p1[:, 128:256], qT[:, h, sl], st["gSb"][:, h, :])
                yield
                kb.stt(vn[rows, :], p1[rows, 0:128], nbeta[sub][rows, h:h + 1], bu[rows, :], ALU.mult, ALU.add)
                kb.act(oa[rows, :], p1[rows, 128:256], AF.Identity, scale=expGA[sub][rows, h:h + 1])
                pbq = bank()
                kb.mm(pbq[:, 0:128], attnT.v, vn.v)
                ke = (ke0, ke1)[c]
                kb.mm(pbq[:, 128:256], ke[sub][:, hs], vn.v)
                yield
                kb.tt(o_[rows, :], oa[rows, :], pbq[rows, 0:128], ALU.add)
                kb.stt(st["gS"][:, h, :], st["gS"][:, h, :], gend[sub][:, c * 12 + h:c * 12 + h + 1], pbq[:, 128:256],
                       ALU.mult, ALU.add)
                kb.copy(st["gSb"][:, h, :], st["gS"][:, h, :], eng="act")
            norm_gate_T(o_.v, 128, siluz[sub][:, hs], [obr[0][:, h, sl]])

        for sub in range(NS):
            dec = decay_T(sub, 0)
            decm = big4[2]
            kb.tt(decm.v, dec.v, M_INCL64.un(1).bc([128, 4, 128]), ALU.mult)
            decs = big4[3]
            kb.tt(decs.v, decm.v, IDENT.un(1).bc([128, 4, 128]), ALU.subtract)
            if stop < 4.3:
                continue
            run_threads([gdn_head(sub, h, decm, decs, gdn_tp[h]) for h in range(4)])

        if stop < 5:
            return
        whi = wload(W[:, :, O_HI:O_HI + 512], 8, 512)
        vh = tokbf[0]
        for sub in range(NS):
            tm_proj(whi, 0, 512, sub, lambda ps_v, sub=sub: kb.copy(vh[sub].v, ps_v, eng="act"))
        whg = wload(W[:, :, O_HG:O_HG + 512], 8, 512)
        silug = tokf[0]
        for sub in range(NS):
            tm_proj(whg, 0, 512, sub, lambda ps_v, sub=sub: kb.act(silug[sub].v, ps_v, AF.Silu))
        whq = wload(W[:, :, O_HQ:O_HQ + 512], 8, 512)
        whf = wload(W[:, :, O_HF:O_HF + 512], 8, 512)
        NCH = T // 32

        def hgrn_head(h):
            def gen(slot):
                al = slot_alloc(slot)
                hs = slice(h * 128, (h + 1) * 128)
                q_, f_, t3, G, t5 = al(), al(), al(), al(), al()
                pq = bank()
                for k in range(8):
                    kb.mm(pq[:, 0:T], whq[:, k, hs], hT[:, k, :], start=(k == 0), stop=(k == 7))
                pf = bank()
                for k in range(8):
                    kb.mm(pf[:, 0:T], whf[:, k, hs], hT[:, k, :], start=(k == 0), stop=(k == 7))
                yield
                kb.act(q_[:, 0:T], pq[:, 0:T], AF.Silu)
                kb.act(f_[:, 0:T], pf[:, 0:T], AF.Sigmoid)
                kb.ts(f_[:, 0:T], f_[:, 0:T], L["oml"][:, h:h + 1], ALU.mult, L["lb"][:, h:h + 1], ALU.add)
                kb.act(t3[:, 0:T], f_[:, 0:T], AF.Ln)
                kk = f_
                kb.ts(kk[:, 0:T], f_[:, 0:T], -1.0, ALU.mult, 1.0, ALU.add)
                kb.scan(G[:, 0:T], resetm.v, t3[:, 0:T], 0.0, ALU.mult, ALU.add)
                yield
                G3 = G[:, 0:T].re("p (c r) -> p c r", r=32)
                Dm = t3
                kb.tt(Dm[:, 0:T].re("p (c r) -> p c r", r=32), G3, G3[:, :, 15:16].bc([128, NCH, 32]), ALU.subtract)
                kb.act(t5[:, 0:T], Dm[:, 0:T], AF.Exp)
                qt, kt = abf[0], abf[1]
                kb.tt(qt[:, slot, :], q_[:, 0:T], t5[:, 0:T], ALU.mult)
                yield
                kb.act(t5[:, 0:T], Dm[:, 0:T], AF.Exp, scale=-1.0)
                kb.tt(kt[:, slot, :], kk[:, 0:T], t5[:, 0:T], ALU.mult)
                yield
                kb.act(t5[:, 0:T], G[:, 0:T], AF.Exp)
                qz = qgz[slot]
                for sub in range(NS):
                    kb.tt(qz[sub].v.re("p (c r) -> p c r", r=160)[:, :, 0:32],
                          q_[:, sub * 128:(sub + 1) * 128].re("p (c r) -> p c r", r=32),
                          t5[:, sub * 128:(sub + 1) * 128].re("p (c r) -> p c r", r=32), ALU.mult)
                yield
                DL = t3
                kb.tt(DL[:, 0:T].re("p (c r) -> p c r", r=32), G3[:, :, 31:32].bc([128, NCH, 32]), G3, ALU.subtract)
                kb.act(t5[:, 0:T], DL[:, 0:T], AF.Exp)
                kend = t3
                kb.tt(kend[:, 0:T], kk[:, 0:T], t5[:, 0:T], ALU.mult)
                ge = gepool[slot]
                kb.act(ge[:, 0:NCH], G3[:, :, 31], AF.Exp)
                kz = kendz[slot]
                for sub in range(NS):
                    sl = slice(sub * 128, (sub + 1) * 128)
                    pk = bank()
                    kb.tr(pk[:, 0:128], kend[:, sl], IDENT)
                    psc = bank()
                    kb.mm(psc[:, 0:128], kt[:, slot, sl], qt[:, slot, sl])
                    yield
                    kb.tt(kz[sub].v, pk[:, 0:128].un(1).bc([128, 4, 128]), INDC[:, 0:4].un(2).bc([128, 4, 128]), ALU.mult)
                    attnT = sqb[slot]
                    kb.tt(attnT.v, psc[:, 0:128], M_INCL32, ALU.mult)
                    po = banks[6 + slot]
                    kb.mm(po[:, 0:128], attnT.v, vh[sub][:, hs], start=True, stop=False)
                    for c in range(4):
                        kb.mm(po[:, 0:128], qz[sub][:, c * 128:(c + 1) * 128], st["hSb"][:, h, :], start=False, stop=(c == 3))
                        pss = bank()
                        kb.mm(pss[:, 0:128], kz[sub][:, c, :], vh[sub][:, hs])
                        yield
                        kb.stt(st["hS"][:, h, :], st["hS"][:, h, :], ge[:, sub * 4 + c:sub * 4 + c + 1], pss[:, 0:128],
                               ALU.mult, ALU.add)
                        kb.copy(st["hSb"][:, h, :], st["hS"][:, h, :], eng="act")
                    norm_gate_T(po[:, 0:128], 128, silug[sub][:, hs], [obr[1][:, h, sl]])
            return gen

        run_slots([hgrn_head(h) for h in range(4)], nslots=2, offset=5)

        if stop < 6:
            return
        wsz = wload(W[:, :, O_SZ:O_SZ + 512], 8, 512)
        siluzs = tokf[0]
        for sub in range(NS):
            tm_proj(wsz, 0, 512, sub, lambda ps_v, sub=sub: kb.act(siluzs[sub].v, ps_v, AF.Silu))
        wsx = wload(W[:, :, O_SX:O_SX + 512], 8, 512)
        wsbc = wload(W[:, :, O_SB:O_SB + 512], 8, 512)
        xtok, xdt, btok = tokf[1], tokbf[0], tokbf[1]
        BT, CT = abf[0], abf[1]

        def ssd_chunk(fc):
            def gen(slot):
                al = slot_alloc(slot, 4)
                wv = wsx if fc < 4 else wsbc
                pb = bank()
                for k in range(8):
                    kb.mm(pb[:, 0:T], wv[:, k, (fc % 4) * 128:(fc % 4 + 1) * 128], hT[:, k, :], start=(k == 0), stop=(k == 7))
                yield
                acc = conv(pb[:, 0:T], st["sh"][:, fc, :], L["scw"][:, fc, :], 4, bias_v=L["scb"][:, fc:fc + 1], ft=al)
                s_ = al()
                kb.act(s_[:, 0:T], acc[:, 0:T], AF.Silu)
                if fc >= 6:
                    kb.copy(CT[:, fc - 6, :], s_[:, 0:T], eng="pool")
                    return
                if fc >= 4:
                    kb.copy(BT[:, fc - 4, :], s_[:, 0:T], eng="pool")
                pbs = []
                for sub in range(NS):
                    pb2 = bank()
                    kb.tr(pb2[:, 0:128], s_[:, sub * 128:(sub + 1) * 128], IDENT)
                    pbs.append(pb2)
                yield
                for sub in range(NS):
                    if fc < 4:
                        kb.copy(xtok[sub][:, fc * 128:(fc + 1) * 128], pbs[sub][:, 0:128], eng="act")
                    else:
                        g = fc - 4
                        kb.copy(btok[sub][:, g * 128:(g + 1) * 128], pbs[sub][:, 0:128], eng="act")
            return gen

        run_slots([ssd_chunk(fc) for fc in range(8)], nslots=3, offset=1)
        for sub in range(NS):
            kb.tt(xdt[sub].v.re("p (h d) -> p h d", d=64), xtok[sub].v.re("p (h d) -> p h d", d=64),
                  dtv[sub][:, 4:12].un(2).bc([128, 8, 64]), ALU.mult)

        def ssd_group(sub, g):
            def gen(slot):
                sl = slice(sub * 128, (sub + 1) * 128)
                gs = slice(g * 256, (g + 1) * 256)
                h0 = 4 + 4 * g
                rh, dec = big4[2 * slot], big4[2 * slot + 1]
                t1, y, t3 = otok[3 * slot], otok[3 * slot + 1], otok[3 * slot + 2]
                kb.tt(rh.v, M_TRIALL.un(1).bc([128, 4, 128]), gda[sub][:, h0:h0 + 4].un(2).bc([128, 4, 128]), ALU.mult)
                pd = bank()
                kb.mm(pd[:, 0:512], M_UPALL, rh.v.re("p h l -> p (h l)"))
                pcb = bank()
                kb.mm(pcb[:, 0:128], BT[:, g, sl], CT[:, g, sl])
                yield
                kb.act(dec.v.re("p h l -> p (h l)"), pd[:, 0:512], AF.Exp)
                cbm = sq128[slot]
                kb.tt(cbm.v, pcb[:, 0:128], M_INCL64, ALU.mult)
                at = big4b[slot]
                kb.tt(at.v, dec.v, cbm.v.un(1).bc([128, 4, 128]), ALU.mult)
                py = banks[6 + slot]
                for hh in range(4):
                    kb.mm(py[:, hh * 64:(hh + 1) * 64], at[:, hh, :], xdt[sub][:, (4 * g + hh) * 64:(4 * g + hh + 1) * 64])
                for c in range(2):
                    rows = slice(c * 64, (c + 1) * 64)
                    poff = bank()
                    kb.mm(poff[:, 0:256], CT[:, g, sl], st["sSb"][:, g, :])
                    xw = tokbf[2 + slot][sub]
                    kb.tt(xw[:, 0:256].re("p (h d) -> p h d", d=64), xdt[sub][:, gs].re("p (h d) -> p h d", d=64),
                          eend[sub][:, c * 12 + h0:c * 12 + h0 + 4].un(2).bc([128, 4, 64]), ALU.mult)
                    pst = bank()
                    kb.mm(pst[:, 0:256], btok[sub][:, g * 128:(g + 1) * 128], xw[:, 0:256])
                    yield
                    kb.tt(t1[rows, :].re("p (h d) -> p h d", d=64), poff[rows, 0:256].re("p (h d) -> p h d", d=64),
                          expGA[sub][rows, h0:h0 + 4].un(2).bc([64, 4, 64]), ALU.mult)
                    kb.tt(st["sS"][:, g, :].re("p (h d) -> p h d", d=64), st["sS"][:, g, :].re("p (h d) -> p h d", d=64),
                          gend[sub][:, c * 12 + h0:c * 12 + h0 + 4].un(2).bc([128, 4, 64]), ALU.mult, eng="pool")
                    kb.tt(st["sS"][:, g, :], st["sS"][:, g, :], pst[:, 0:256], ALU.add)
                    kb.copy(st["sSb"][:, g, :], st["sS"][:, g, :], eng="act")
                kb.tt(y.v, py[:, 0:256], t1.v, ALU.add)
                kb.tt(t3.v, xtok[sub][:, gs], L["dfull"].v.re("p h d -> p (h d)")[:, gs], ALU.mult, eng="pool")
                kb.tt(y.v, y.v, t3.v, ALU.add)
                kb.tt(y.v, y.v, siluzs[sub][:, gs], ALU.mult)
                norm_gate_T(y.v, 256, None, [obr[2][:, 2 * g, sl], obr[2][:, 2 * g + 1, sl]], junk=t3, og=t1)
            return gen

        run_slots([ssd_group(sub, g) for sub in range(NS) for g in range(2)], nslots=2, offset=1)

        if stop < 7:
            return
        if dbg and l == 0 and first_tile and seq == 0:
            for i in range(3):
                kb.copy(macc[:, 0:4, :], obr[i].v)
                tickets.append(dbgout("obr%d" % i, macc[:, 0:4, :], [128, 4, T]))

        for br in range(3):
            wb = wload(wbr_s[l][br].v, 4, 1024)
            for jb in range(2):
                wg = wload(W[:, :, O_GATE + br * 1024 + jb * 512:O_GATE + br * 1024 + (jb + 1) * 512], 8, 512)
                for jj in range(4):
                    j = jb * 4 + jj
                    gt = ft()
                    fm_proj(wg, jj * 128, hT, 8, lambda ps_v, gt=gt: kb.act(gt[:, 0:T], ps_v, AF.Sigmoid))
                    if br == 0:
                        fm_proj(wb, j * 128, obr[br], 4,
                                lambda ps_v, gt=gt, j=j: kb.tt(macc[:, j, :], ps_v, gt[:, 0:T], ALU.mult))
                    else:
                        tm = ft()
                        fm_proj(wb, j * 128, obr[br], 4,
                                lambda ps_v, gt=gt, tm=tm: kb.tt(tm[:, 0:T], ps_v, gt[:, 0:T], ALU.mult))
                        kb.tt(macc[:, j, :], macc[:, j, :], tm[:, 0:T], ALU.add, eng="pool")
        kb.copy(mergedT.v, macc.v, eng="act")
        for jb in range(2):
            wo = wload(wout_s[l][:, :, jb * 512:(jb + 1) * 512], 8, 512)
            for jj in range(4):
                j = jb * 4 + jj
                fm_proj(wo, jj * 128, mergedT, 8,
                        lambda ps_v, j=j: kb.stt(xT[:, j, :], ps_v, mod[:, 16 + j, seq:seq + 1], xT[:, j, :], ALU.mult, ALU.add))
        if dbg and l == 0 and first_tile and seq == 0:
            tickets.append(dbgout("xmix", xT.v, [128, 8, T]))

        if stop < 8:
            return
        rms_mod(L["a2"][:, :, seq], mod[:, 24:32, seq], out_bf=hT)
        for i in range(11):
            b = wbufs[wst["i"] % NW]; wst["i"] += 1
            wv = b.v.re("p (k n) -> p k n", n=512)
            kb.dma(wv[:, :, 0:256], wup_s[l][:, :, i * 256:(i + 1) * 256])
            kb.dma(wv[:, :, 256:512], wup_s[l][:, :, FH + i * 256:FH + (i + 1) * 256])
            for jj in range(2):
                fcg = i * 2 + jj
                fcv = 22 + fcg
                res = {}

                def evg(ps_v, res=res, fcg=fcg):
                    res["g"] = conv(ps_v, st["fh"][:, fcg, :], L["fcw"][:, fcg, :], 3, bias_v=L["fcb"][:, fcg:fcg + 1])

                def evv(ps_v, res=res, fcv=fcv):
                    res["v"] = conv(ps_v, st["fh"][:, fcv, :], L["fcw"][:, fcv, :], 3, bias_v=L["fcb"][:, fcv:fcv + 1])
                fm_proj(wv, jj * 128, hT, 8, evg)
                fm_proj(wv, 256 + jj * 128, hT, 8, evv)
                sg = ft()
                kb.act(sg[:, 0:T], res["g"][:, 0:T], AF.Silu)
                kb.tt(hidT[:, fcg, :], sg[:, 0:T], res["v"][:, 0:T], ALU.mult)
        for j in range(8):
            b = wbufs[wst["i"] % NW]; wst["i"] += 1
            wv = b.v[:, 0:22 * 128].re("p (k n) -> p k n", n=128)
            kb.dma(wv, wdn_s[l][:, :, j * 128:(j + 1) * 128])
            fm_proj(wv, 0, hidT, 22,
                    lambda ps_v, j=j: kb.stt(xT[:, j, :], ps_v, mod[:, 40 + j, seq:seq + 1], xT[:, j, :], ALU.mult, ALU.add))
        if dbg and l == 0 and first_tile and seq == 0:
            tickets.append(dbgout("xffn", xT.v, [128, 8, T]))

    for seq in range(NSEQ):
        for ti in range(NT):
            r0 = seq * S + ti * T
            kb.dma(xio.v, V((x_d,), x_d.ap[r0:r0 + T, :].rearrange("(s p) d -> p s d", p=128)), eng="pool")
            for sub in range(NS):
                for c in range(8):
                    pb = bank()
                    kb.tr(pb[:, 0:128], xio[:, sub, c * 128:(c + 1) * 128], IDENT)
                    kb.copy(xT[:, c, sub * 128:(sub + 1) * 128], pb[:, 0:128], eng=("act" if c % 2 else "dve"))
            for l in range(DEPTH if stop >= 2 else 0):
                layer_body(l, seq, ti == 0)
            rms_mod(fnw.v, None, out_f32=macc)
            for sub in range(NS):
                for c in range(8):
                    pb = bank()
                    kb.tr(pb[:, 0:128], macc[:, c, sub * 128:(sub + 1) * 128], IDENT)
                    kb.copy(xio[:, sub, c * 128:(c + 1) * 128], pb[:, 0:128], eng=("act" if c % 2 else "dve"))
            tickets.append(kb.dma(V((out_d,), out_d.ap[r0:r0 + T, :].rearrange("(s p) d -> p s d", p=128)), xio.v, eng="pool"))

    tickets = [t for t in tickets if t is not None]
    kb.finish(tickets)
    return nc, kb, dbg_d


_CACHE = {}


def kernel(**inputs):
    NC = 8
    x = np.ascontiguousarray(inputs["x"], dtype=np.float32)
    B, S, _ = x.shape
    NSEQ = B // NC
    key = (S, NSEQ)
    if key not in _CACHE:
        _CACHE[key] = build(S, NSEQ)[0]
    nc = _CACHE[key]
    cmask = make_masks()
    in_maps = []
    for i in range(NC):
        m = {"x": x[i * NSEQ:(i + 1) * NSEQ].reshape(NSEQ * S, D),
             "c": np.ascontiguousarray(inputs["c"][i * NSEQ:(i + 1) * NSEQ], dtype=np.float32),
             "cmask": cmask}
        for n in PNAMES:
            m[n] = np.ascontiguousarray(inputs[n], dtype=np.float32)
        in_maps.append(m)
    res = run_bass_kernel_spmd(nc, in_maps, core_ids=list(range(NC)))
    out = np.concatenate([r["out"].reshape(NSEQ, S, D) for r in res.results], axis=0)
    return out.astype(np.float32)
```
